# Optimizing a Trainium2 kernel written in Bass

```python
import math
import numpy as np
import jax
import jax.numpy as jnp
from jax import lax

D_MODEL = 2048
BATCH = 2
SEQ = 8192
DEPTH = 4

PLE_DIM = 256
D_FF = 5632
HALF_STEP = 0.5
NORM_EPS = 1e-6
CHUNK = 128
N_BRANCH = 3
A_HEADS = 4
A_QK = 128
A_V = 256
B_HEADS = 16
B_HEADDIM = 64
B_INNER = B_HEADS * B_HEADDIM
B_STATE = 128
B_GROUPS = 2
CONV_K = 5
DT_MIN = 1e-3
DT_MAX = 1e-1
C_HEADS = 8
C_Q_RANK = 512
C_KV_RANK = 512
C_NOPE = 128
C_ROPE = 64
C_V = 128
ROPE_THETA = 10000.0
MAX_POS_OFFSET = 1024

BRANCH_W = 1024
A_QK_W = A_HEADS * A_QK
A_V_W = A_HEADS * A_V
B_XBC_W = B_INNER + 2 * B_GROUPS * B_STATE
IN_SPLITS = (A_QK_W, A_QK_W, A_V_W, A_V_W, 2 * A_HEADS, 2 * A_HEADS,
             B_INNER, B_XBC_W, 2 * B_HEADS,
             C_Q_RANK, C_KV_RANK, C_ROPE,
             N_BRANCH * D_MODEL)
D_IN = sum(IN_SPLITS)

kernel_name = "hybrid_mlstm_ssd_mla_macaron_encoder"


def rmsnorm(x, g):
    xf = x.astype(jnp.float32)
    y = xf * lax.rsqrt(jnp.mean(xf * xf, axis=-1, keepdims=True) + NORM_EPS)
    return (y * g.astype(jnp.float32)).astype(x.dtype)


def swiglu(x, w13, w2):
    a, b = jnp.split(x @ w13, 2, axis=-1)
    return (jax.nn.silu(a) * b) @ w2


def split_cols(t, sizes):
    idx = [int(v) for v in np.cumsum(sizes)[:-1]]
    return jnp.split(t, idx, axis=-1)


def mlstm_chunkwise(q, k, v, li, lf):
    bsz, nh, seq, dk = q.shape
    dv = v.shape[-1]
    nc = seq // CHUNK
    q = q.reshape(bsz, nh, nc, CHUNK, dk)
    k = k.reshape(bsz, nh, nc, CHUNK, dk)
    v = v.reshape(bsz, nh, nc, CHUNK, dv)
    li = li.reshape(bsz, nh, nc, CHUNK)
    lf = lf.reshape(bsz, nh, nc, CHUNK)
    b = jnp.cumsum(lf, axis=-1)
    g = b[..., -1]
    w_state = g[..., None] - b + li
    m_loc = jnp.max(w_state, axis=-1)
    e_state = jnp.exp(w_state - m_loc[..., None])
    c_loc = jnp.einsum('bhcsk,bhcsv->bhckv', k * e_state[..., None], v)
    n_loc = jnp.einsum('bhcs,bhcsk->bhck', e_state, k)

    def step(carry, inp):
        c_st, n_st, m_st = carry
        c_l, n_l, m_l, g_c = inp
        m_new = jnp.maximum(g_c + m_st, m_l)
        a_prev = jnp.exp(g_c + m_st - m_new)
        a_loc = jnp.exp(m_l - m_new)
        c_new = a_prev[..., None, None] * c_st + a_loc[..., None, None] * c_l
        n_new = a_prev[..., None] * n_st + a_loc[..., None] * n_l
        return (c_new, n_new, m_new), (c_st, n_st, m_st)

    init = (jnp.zeros((bsz, nh, dk, dv), jnp.float32),
            jnp.zeros((bsz, nh, dk), jnp.float32),
            jnp.zeros((bsz, nh), jnp.float32))
    xs = (jnp.moveaxis(c_loc, 2, 0), jnp.moveaxis(n_loc, 2, 0),
          jnp.moveaxis(m_loc, 2, 0), jnp.moveaxis(g, 2, 0))
    _, (c0, n0, m0) = lax.scan(step, init, xs)
    c0 = jnp.moveaxis(c0, 0, 2)
    n0 = jnp.moveaxis(n0, 0, 2)
    m0 = jnp.moveaxis(m0, 0, 2)

    causal = jnp.tril(jnp.ones((CHUNK, CHUNK), dtype=bool))
    log_d = b[..., :, None] - b[..., None, :] + li[..., None, :]
    log_d = jnp.where(causal, log_d, -jnp.inf)
    log_inter = b + m0[..., None]
    m_t = jnp.maximum(log_inter, jnp.max(log_d, axis=-1))
    s_qk = jnp.einsum('bhctk,bhcsk->bhcts', q, k) * jnp.exp(log_d - m_t[..., None])
    e_inter = jnp.exp(log_inter - m_t)
    num = (jnp.einsum('bhcts,bhcsv->bhctv', s_qk, v)
           + e_inter[..., None] * jnp.einsum('bhctk,bhckv->bhctv', q, c0))
    den = jnp.sum(s_qk, axis=-1) + e_inter * jnp.einsum('bhctk,bhck->bhct', q, n0)
    h = num / jnp.maximum(jnp.abs(den), jnp.exp(-m_t))[..., None]
    return h.reshape(bsz, nh, seq, dv)


def mlstm_branch(q, k, v, o, ig, fg, b_ig, b_fg, norm_g):
    bsz, seq, _ = q.shape
    f32 = jnp.float32

    def heads(t, d):
        return t.astype(f32).reshape(bsz, seq, A_HEADS, d).transpose(0, 2, 1, 3)

    qh = heads(q, A_QK)
    kh = heads(k, A_QK) * (A_QK ** -0.5)
    vh = heads(v, A_V)
    li = (ig.astype(f32).reshape(bsz, seq, 2, A_HEADS) + b_ig.astype(f32)).transpose(0, 2, 3, 1)
    lf = jax.nn.log_sigmoid(fg.astype(f32).reshape(bsz, seq, 2, A_HEADS) + b_fg.astype(f32)).transpose(0, 2, 3, 1)
    flip = lambda t: jnp.flip(t, axis=2)
    h_fwd = mlstm_chunkwise(qh, kh, vh, li[:, 0], lf[:, 0])
    h_bwd = flip(mlstm_chunkwise(flip(qh), flip(kh), flip(vh), flip(li[:, 1]), flip(lf[:, 1])))
    h = (h_fwd + h_bwd).transpose(0, 2, 1, 3)
    h = h * lax.rsqrt(jnp.mean(h * h, axis=-1, keepdims=True) + NORM_EPS)
    h = h.reshape(bsz, seq, A_V_W) * norm_g.astype(f32)
    return (jax.nn.sigmoid(o.astype(f32)) * h).astype(q.dtype)


def centred_dwconv(x, w, b):
    pad = (CONV_K - 1) // 2
    y = lax.conv_general_dilated(x, w[:, None, :], window_strides=(1,), padding=[(pad, pad)],
                                 dimension_numbers=('NWC', 'WIO', 'NWC'),
                                 feature_group_count=x.shape[-1])
    return y + b


def ssd_chunked(x, dt, a, bm, cm):
    bsz, seq, nh, hp = x.shape
    ng, ns = bm.shape[2], bm.shape[3]
    ne = nh // ng
    nc = seq // CHUNK
    x = x.reshape(bsz, nc, CHUNK, ng, ne, hp)
    dt = dt.reshape(bsz, nc, CHUNK, ng, ne)
    bm = bm.reshape(bsz, nc, CHUNK, ng, ns)
    cm = cm.reshape(bsz, nc, CHUNK, ng, ns)
    acs = jnp.cumsum(dt * a.reshape(ng, ne), axis=2)
    xdt = x * dt[..., None]
    causal = jnp.tril(jnp.ones((CHUNK, CHUNK), dtype=bool))[:, :, None, None]
    seg = acs[:, :, :, None] - acs[:, :, None, :]
    decay = jnp.exp(jnp.where(causal, seg, -jnp.inf))
    cb = jnp.einsum('bctgn,bcsgn->bctsg', cm, bm)
    y_diag = jnp.einsum('bctsge,bcsgep->bctgep', cb[..., None] * decay, xdt)
    decay_to_end = jnp.exp(acs[:, :, -1:] - acs)
    states = jnp.einsum('bcsgn,bcsgep->bcgenp', bm, xdt * decay_to_end[..., None])
    chunk_decay = jnp.exp(acs[:, :, -1])

    def step(s, inp):
        st, dc = inp
        return dc[..., None, None] * s + st, s

    init = jnp.zeros((bsz, ng, ne, ns, hp), jnp.float32)
    _, s0 = lax.scan(step, init, (jnp.moveaxis(states, 1, 0), jnp.moveaxis(chunk_decay, 1, 0)))
    s0 = jnp.moveaxis(s0, 0, 1)
    y_off = jnp.einsum('bctgn,bcgenp->bctgep', cm, s0) * jnp.exp(acs)[..., None]
    return (y_diag + y_off).reshape(bsz, seq, nh, hp)


def mamba2_branch(z, xbc, dt_raw, conv_w, conv_b, a_log, dt_bias, d_skip, norm_g):
    bsz, seq, _ = z.shape
    f32 = jnp.float32
    xbc = jax.nn.silu(centred_dwconv(xbc, conv_w, conv_b)).astype(f32)
    xs, bm, cm = split_cols(xbc, (B_INNER, B_GROUPS * B_STATE, B_GROUPS * B_STATE))
    xh = xs.reshape(bsz, seq, B_HEADS, B_HEADDIM)
    bm = bm.reshape(bsz, seq, B_GROUPS, B_STATE)
    cm = cm.reshape(bsz, seq, B_GROUPS, B_STATE)
    dt = jax.nn.softplus(dt_raw.astype(f32).reshape(bsz, seq, 2, B_HEADS) + dt_bias.astype(f32))
    a = -jnp.exp(a_log.astype(f32))
    flip = lambda t: jnp.flip(t, axis=1)
    y_f = ssd_chunked(xh, dt[:, :, 0], a[0], bm, cm)
    y_b = flip(ssd_chunked(flip(xh), flip(dt[:, :, 1]), a[1], flip(bm), flip(cm)))
    y = y_f + y_b + d_skip.astype(f32)[:, None] * xh
    y = y.reshape(bsz, seq, B_INNER) * jax.nn.silu(z.astype(f32))
    return rmsnorm(y, norm_g).astype(z.dtype)


def apply_rope(t, positions):
    half = C_ROPE // 2
    inv_freq = ROPE_THETA ** (-jnp.arange(half, dtype=jnp.float32) / half)
    ang = positions.astype(jnp.float32)[:, :, None, None] * inv_freq
    cos, sin = jnp.cos(ang), jnp.sin(ang)
    tf = t.astype(jnp.float32)
    t1, t2 = tf[..., :half], tf[..., half:]
    return jnp.concatenate([t1 * cos - t2 * sin, t2 * cos + t1 * sin], axis=-1).astype(t.dtype)


def mla_branch(c_q, c_kv, k_rope, positions, q_norm, kv_norm, w_uq, w_ukv):
    bsz, seq, _ = c_q.shape
    dqk = C_NOPE + C_ROPE
    q = (rmsnorm(c_q, q_norm) @ w_uq).reshape(bsz, seq, C_HEADS, dqk)
    kv = (rmsnorm(c_kv, kv_norm) @ w_ukv).reshape(bsz, seq, C_HEADS, C_NOPE + C_V)
    q_nope, q_rot = q[..., :C_NOPE], q[..., C_NOPE:]
    k_nope, v = kv[..., :C_NOPE], kv[..., C_NOPE:]
    q_rot = apply_rope(q_rot, positions)
    k_rot = jnp.broadcast_to(apply_rope(k_rope[:, :, None, :], positions), (bsz, seq, C_HEADS, C_ROPE))
    q = jnp.concatenate([q_nope, q_rot], axis=-1) * (dqk ** -0.5)
    k = jnp.concatenate([k_nope, k_rot], axis=-1)
    nb = seq // CHUNK
    qb = q.reshape(bsz, nb, CHUNK, C_HEADS, dqk).transpose(1, 0, 2, 3, 4)

    def attend(q_blk):
        s = jnp.einsum('blhd,bshd->bhls', q_blk, k).astype(jnp.float32)
        pr = jax.nn.softmax(s, axis=-1).astype(v.dtype)
        return jnp.einsum('bhls,bshd->blhd', pr, v)

    o = lax.map(attend, qb)
    return o.transpose(1, 0, 2, 3, 4).reshape(bsz, seq, C_HEADS * C_V)


def setup_inputs(seed: int = 0) -> dict:
    key = jax.random.key(seed)
    keys = iter(jax.random.split(key, 48))
    f32 = jnp.float32
    L, D = DEPTH, D_MODEL

    def nrm(shape, scale):
        return jax.random.normal(next(keys), shape, f32) * scale

    def gain(shape):
        return 1.0 + 0.1 * jax.random.normal(next(keys), shape, f32)

    x = nrm((BATCH, SEQ, D), 1.0)
    p = nrm((DEPTH, BATCH, SEQ, PLE_DIM), 1.0)
    positions = (jnp.arange(SEQ, dtype=jnp.int32)[None, :]
                 + jax.random.randint(next(keys), (BATCH, 1), 0, MAX_POS_OFFSET, dtype=jnp.int32))
    ffn1_norm = gain((L, D))
    ffn1_w13 = nrm((L, D, 2 * D_FF), D ** -0.5)
    ffn1_w2 = nrm((L, D_FF, D), D_FF ** -0.5)
    mix_norm = gain((L, D))
    w_in = nrm((L, D, D_IN), D ** -0.5)
    mlstm_b_igate = nrm((L, 2, A_HEADS), 0.1)
    mlstm_b_fgate = jax.random.uniform(next(keys), (L, 2, A_HEADS), f32, 3.0, 6.0)
    mlstm_norm = gain((L, A_V_W))
    conv_w = nrm((L, CONV_K, B_XBC_W), CONV_K ** -0.5)
    conv_b = nrm((L, B_XBC_W), 0.01)
    ssm_a_log = jnp.log(jax.random.uniform(next(keys), (L, 2, B_HEADS), f32, 1.0, 16.0))
    dt0 = jnp.exp(jax.random.uniform(next(keys), (L, 2, B_HEADS), f32, math.log(DT_MIN), math.log(DT_MAX)))
    ssm_dt_bias = dt0 + jnp.log(-jnp.expm1(-dt0))
    ssm_d = gain((L, B_HEADS))
    ssm_norm = gain((L, B_INNER))
    mla_q_norm = gain((L, C_Q_RANK))
    mla_kv_norm = gain((L, C_KV_RANK))
    mla_w_uq = nrm((L, C_Q_RANK, C_HEADS * (C_NOPE + C_ROPE)), C_Q_RANK ** -0.5)
    mla_w_ukv = nrm((L, C_KV_RANK, C_HEADS * (C_NOPE + C_V)), C_KV_RANK ** -0.5)
    w_branch = nrm((L, N_BRANCH, BRANCH_W, D), BRANCH_W ** -0.5)
    w_out = nrm((L, D, D), D ** -0.5)
    ffn2_norm = gain((L, D))
    ffn2_w13 = nrm((L, D, 2 * D_FF), D ** -0.5)
    ffn2_w2 = nrm((L, D_FF, D), D_FF ** -0.5)
    ple_norm = gain((L, D))
    w_ple_gate = nrm((L, D, D), D ** -0.5)
    w_ple_proj = nrm((L, PLE_DIM, D), PLE_DIM ** -0.5)
    final_norm = gain((D,))
    return {"x": x, "p": p, "positions": positions,
            "ffn1_norm": ffn1_norm, "ffn1_w13": ffn1_w13, "ffn1_w2": ffn1_w2,
            "mix_norm": mix_norm, "w_in": w_in,
            "mlstm_b_igate": mlstm_b_igate, "mlstm_b_fgate": mlstm_b_fgate, "mlstm_norm": mlstm_norm,
            "conv_w": conv_w, "conv_b": conv_b, "ssm_a_log": ssm_a_log, "ssm_dt_bias": ssm_dt_bias,
            "ssm_d": ssm_d, "ssm_norm": ssm_norm,
            "mla_q_norm": mla_q_norm, "mla_kv_norm": mla_kv_norm, "mla_w_uq": mla_w_uq, "mla_w_ukv": mla_w_ukv,
            "w_branch": w_branch, "w_out": w_out,
            "ffn2_norm": ffn2_norm, "ffn2_w13": ffn2_w13, "ffn2_w2": ffn2_w2,
            "ple_norm": ple_norm, "w_ple_gate": w_ple_gate, "w_ple_proj": w_ple_proj,
            "final_norm": final_norm}


def reference(x, p, positions, ffn1_norm, ffn1_w13, ffn1_w2, mix_norm, w_in,
              mlstm_b_igate, mlstm_b_fgate, mlstm_norm, conv_w, conv_b,
              ssm_a_log, ssm_dt_bias, ssm_d, ssm_norm,
              mla_q_norm, mla_kv_norm, mla_w_uq, mla_w_ukv,
              w_branch, w_out, ffn2_norm, ffn2_w13, ffn2_w2,
              ple_norm, w_ple_gate, w_ple_proj, final_norm):
    bsz, seq, _ = x.shape
    h = x
    for i in range(DEPTH):
        h = h + HALF_STEP * swiglu(rmsnorm(h, ffn1_norm[i]), ffn1_w13[i], ffn1_w2[i])
        u = rmsnorm(h, mix_norm[i])
        (a_q, a_k, a_v, a_o, a_ig, a_fg, b_z, b_xbc, b_dt,
         c_q, c_kv, c_kr, gate_pre) = split_cols(u @ w_in[i], IN_SPLITS)
        y_a = mlstm_branch(a_q, a_k, a_v, a_o, a_ig, a_fg,
                           mlstm_b_igate[i], mlstm_b_fgate[i], mlstm_norm[i]) @ w_branch[i, 0]
        y_b = mamba2_branch(b_z, b_xbc, b_dt, conv_w[i], conv_b[i], ssm_a_log[i],
                            ssm_dt_bias[i], ssm_d[i], ssm_norm[i]) @ w_branch[i, 1]
        y_c = mla_branch(c_q, c_kv, c_kr, positions, mla_q_norm[i], mla_kv_norm[i],
                         mla_w_uq[i], mla_w_ukv[i]) @ w_branch[i, 2]
        gates = jax.nn.sigmoid(gate_pre.reshape(bsz, seq, N_BRANCH, D_MODEL))
        merged = gates[:, :, 0] * y_a + gates[:, :, 1] * y_b + gates[:, :, 2] * y_c
        h = h + merged @ w_out[i]
        h = h + HALF_STEP * swiglu(rmsnorm(h, ffn2_norm[i]), ffn2_w13[i], ffn2_w2[i])
        ple_gate = jax.nn.sigmoid(rmsnorm(h, ple_norm[i]) @ w_ple_gate[i])
        h = h + ple_gate * (p[i] @ w_ple_proj[i])
    return rmsnorm(h, final_norm)
```

```python
import contextlib
import math
import numpy as np
import ml_dtypes
import concourse.bass as bass
import concourse.mybir as mybir
from concourse.bass_utils import run_bass_kernel_spmd

F32 = mybir.dt.float32
BF16 = mybir.dt.bfloat16
I32 = mybir.dt.int32
AF = mybir.ActivationFunctionType
ALU = mybir.AluOpType
AX = mybir.AxisListType

D = 2048
DFF = 5632
KC = D // 128
JC = DFF // 128
SEQ = 8192
BATCH = 2
DEPTH = 4
EPS = 1e-6
CH = 128
PLE_DIM = 256
OFF_Q, OFF_K, OFF_V, OFF_O, OFF_IG, OFF_FG = 0, 512, 1024, 2048, 3072, 3080
OFF_Z, OFF_XBC, OFF_DT = 3088, 4112, 5648
OFF_CQ, OFF_CKV, OFF_KR, OFF_GATE = 5680, 6192, 6704, 6768
EX_U, EX_Q, EX_KV, EX_KR, EX_COS, EX_SIN, EX_ROWS = 0, 2048, 2560, 3072, 3136, 3200, 3264
TWO_PI = 2.0 * math.pi
ENGS = ("pe", "act", "dve", "pool", "sp")


class Buf:
    __slots__ = ("name", "w", "r", "x")

    def __init__(self, name=None, x=False):
        self.name = name
        self.w = None
        self.r = {}
        self.x = x


class DSem:
    __slots__ = ("key", "count")

    def __init__(self, key):
        self.key = key
        self.count = 0


class Shared:
    def __init__(self, nc):
        self.nc = nc
        self.stack = contextlib.ExitStack()
        self.cnt = {e: 0 for e in ENGS}
        self.waited = {e: {} for e in ENGS}
        self.sems = {}
        for e in ENGS:
            self.sems[e] = self.stack.enter_context(nc.semaphore("s_" + e))
        self.dpool = []
        self.ntens = 0

    def close(self):
        self.stack.close()


class KB:
    def __init__(self, nc, shared=None):
        self.nc = nc
        self.stack = contextlib.ExitStack()
        self.streams = {e: [] for e in ENGS}
        self.own = shared is None
        if shared is None:
            shared = Shared(nc)
        self.shared = shared
        self.cnt = shared.cnt
        self.waited = shared.waited
        self.sems = shared.sems
        self.dsems = []
        self.sb_bytes = 0
        self.n_ops = 0

    def sbuf(self, name, shape, dtype):
        self.shared.ntens += 1
        t = self.stack.enter_context(self.nc.sbuf_tensor("%s_%d" % (name, self.shared.ntens), list(shape), dtype))
        sz = 4 if dtype in (F32, I32) else 2
        self.sb_bytes += int(np.prod(shape[1:])) * sz
        return t

    def psum(self, name, shape, dtype=F32):
        self.shared.ntens += 1
        return self.stack.enter_context(self.nc.psum_tensor("%s_%d" % (name, self.shared.ntens), list(shape), dtype))

    def buf(self, name=None):
        return Buf(name)

    def dsem(self):
        i = len(self.dsems)
        pool = self.shared.dpool
        if i >= len(pool):
            key = "d%d" % i
            self.sems[key] = self.shared.stack.enter_context(self.nc.semaphore("sd_%d" % i))
            pool.append(DSem(key))
        d = pool[i]
        self.dsems.append(d)
        return d

    def _deps(self, reads, writes, accum=None):
        deps = {}
        for b in reads:
            if b.w is not None:
                kk, v = b.w
                if deps.get(kk, 0) < v:
                    deps[kk] = v
        for b in writes:
            if b.w is not None and b.w[0] != accum:
                kk, v = b.w
                if deps.get(kk, 0) < v:
                    deps[kk] = v
            for kk, v in b.r.items():
                if deps.get(kk, 0) < v:
                    deps[kk] = v
        return deps

    def _emit_waits(self, eng, deps):
        wd = self.waited[eng]
        for kk, v in deps.items():
            if kk == eng and eng in ("pe", "sp"):
                continue
            if wd.get(kk, 0) >= v:
                continue
            wd[kk] = v
            self.streams[eng].append(("w", kk, v))

    def _mark(self, tok, reads, writes):
        kk, v = tok
        for b in reads:
            if b.r.get(kk, 0) < v:
                b.r[kk] = v
        for b in writes:
            b.w = tok
            b.r = {}

    def op(self, eng, fn, reads=(), writes=()):
        xr = [b for b in reads if b.x]
        if xr:
            writes = list(writes) + xr
        deps = self._deps(reads, writes)
        self._emit_waits(eng, deps)
        self.cnt[eng] += 1
        tok = (eng, self.cnt[eng])
        self.streams[eng].append(("o", fn, eng, 1))
        self._mark(tok, reads, writes)
        self.n_ops += 1
        return tok

    def dma(self, q, ds, out, in_, reads=(), writes=(), accum=False):
        deps = self._deps(reads, writes, accum=(ds.key if accum else None))
        self._emit_waits(q, deps)
        ds.count += 16
        tok = (ds.key, ds.count)

        def fn(e, out=out, in_=in_):
            return e.dma_start(out=out, in_=in_)

        self.streams[q].append(("o", fn, ds.key, 16))
        self._mark(tok, reads, writes)
        return tok

    def gbegin(self, q, ds):
        if ds.count:
            self._emit_waits(q, {ds.key: ds.count})
        self._g = (ds, [], [])

    def gdma(self, q, out, in_, reads=(), writes=()):
        ds, gw, gr = self._g
        self.dma(q, ds, out, in_, reads=reads, writes=writes, accum=True)
        gw.extend(writes)
        gr.extend(reads)

    def group(self, q, ds, items):
        self.gbegin(q, ds)
        for (out, in_, rd, wr) in items:
            self.gdma(q, out, in_, reads=rd, writes=wr)
        self.gend()

    def gend(self):
        ds, gw, gr = self._g
        for b in gw:
            b.w = (ds.key, ds.count)
        for b in gr:
            b.r[ds.key] = ds.count
        self._g = None

    def barrier(self):
        for eng in ENGS:
            deps = {}
            for d in self.dsems:
                if d.count:
                    deps[d.key] = d.count
            for e in ENGS:
                if e != eng and self.cnt[e]:
                    deps[e] = self.cnt[e]
            self._emit_waits(eng, deps)

    def wait_all(self, eng="sp"):
        deps = {}
        for d in self.dsems:
            if d.count:
                deps[d.key] = d.count
        for e in ENGS:
            if e != eng and self.cnt[e]:
                deps[e] = self.cnt[e]
        self._emit_waits(eng, deps)

    def emit(self):
        nc = self.nc
        sems = self.sems
        streams = self.streams
        with nc.Block() as block:
            def run(e, items):
                for it in items:
                    if it[0] == "w":
                        e.wait_ge(sems[it[1]], it[2])
                    else:
                        ins = it[1](e)
                        ins.then_inc(sems[it[2]], it[3])

            @block.tensor
            def _(e):
                run(e, streams["pe"])

            @block.scalar
            def _(e):
                run(e, streams["act"])

            @block.vector
            def _(e):
                run(e, streams["dve"])

            @block.gpsimd
            def _(e):
                run(e, streams["pool"])

            @block.sync
            def _(e):
                run(e, streams["sp"])
        self.stack.close()
        if self.own:
            self.shared.close()


class Rot:
    def __init__(self, items):
        self.items = items
        self.i = 0

    def next(self):
        it = self.items[self.i]
        self.i = (self.i + 1) % len(self.items)
        return it


def mk_rot(k, name, n, shape, dtype):
    return Rot([(k.sbuf(name, shape, dtype), k.buf()) for _ in range(n)])


def chunks_lhsT(W, cw=128):
    K, N = W.shape
    kci = K // 128
    nch = N // cw
    return W.reshape(kci, 128, nch, cw).transpose(2, 1, 0, 3).reshape(nch, 128, kci * cw)


def rhs_layout(W):
    K, N = W.shape
    return np.ascontiguousarray(W.reshape(K // 128, 128, N).transpose(1, 0, 2))


def gain_cols(g):
    return np.ascontiguousarray(g.reshape(-1, 128).T)


def tri_consts():
    s = np.arange(128)[:, None]
    t = np.arange(128)[None, :]
    c = np.zeros((8, 128, 128), np.float32)
    c[0] = (s <= t)
    c[1] = (s >= t)
    c[2] = (s > t)
    c[3] = (s < t)
    c[4] = np.eye(128)
    c[5] = 1.0
    return c


class Common:
    def __init__(self, k):
        self.k = k
        self.ps = [(k.psum("ps%d" % i, [128, 512]), Buf(x=True)) for i in range(7)]
        self.psrot = Rot(self.ps)
        pst = k.psum("pst", [128, 1024], BF16)
        pstb = Buf(x=True)
        self.pst = Rot([(pst[:, 0:512], pstb), (pst[:, 512:1024], pstb)])
        self.d_misc = k.dsem()

    def nps(self):
        return self.psrot.next()


def emit_sincos(k, cm, posi, posib, invf_ap, sgn_ap, n, cos_out, sin_out, outb, tmp, scale=1.0):
    (ang, angb), (y, yb), (yf, yfb), (t, tb) = tmp[:4]
    (yi, yib) = tmp[4]
    k.op("dve", lambda e: e.tensor_copy(out=ang[0:64, 0:n], in_=posi), reads=[posib], writes=[angb])
    k.op("dve", lambda e: e.tensor_scalar(out=ang[0:64, 0:n], in0=ang[0:64, 0:n], scalar1=invf_ap, scalar2=None, op0=ALU.mult),
         reads=[angb], writes=[angb])
    for which in (0, 1):
        off = 0.25 if which == 0 else 0.0
        k.op("dve", lambda e, off=off: e.tensor_scalar(out=y[0:64, 0:n], in0=ang[0:64, 0:n], scalar1=float(1.0 / TWO_PI), scalar2=off,
                                                        op0=ALU.mult, op1=ALU.add), reads=[angb], writes=[yb])
        k.op("dve", lambda e: e.tensor_copy(out=yi[0:64, 0:n], in_=y[0:64, 0:n]), reads=[yb], writes=[yib])
        k.op("dve", lambda e: e.tensor_copy(out=yf[0:64, 0:n], in_=yi[0:64, 0:n]), reads=[yib], writes=[yfb])
        k.op("dve", lambda e: e.tensor_tensor(out=y[0:64, 0:n], in0=y[0:64, 0:n], in1=yf[0:64, 0:n], op=ALU.subtract),
             reads=[yb, yfb], writes=[yb])
        k.op("dve", lambda e: e.tensor_scalar(out=yf[0:64, 0:n], in0=y[0:64, 0:n], scalar1=0.5, scalar2=None, op0=ALU.is_gt),
             reads=[yb], writes=[yfb])
        k.op("dve", lambda e: e.tensor_tensor(out=y[0:64, 0:n], in0=y[0:64, 0:n], in1=yf[0:64, 0:n], op=ALU.subtract),
             reads=[yb, yfb], writes=[yb])
        k.op("dve", lambda e: e.tensor_scalar(out=yf[0:64, 0:n], in0=y[0:64, 0:n], scalar1=-0.5, scalar2=None, op0=ALU.is_lt),
             reads=[yb], writes=[yfb])
        k.op("dve", lambda e: e.tensor_tensor(out=y[0:64, 0:n], in0=y[0:64, 0:n], in1=yf[0:64, 0:n], op=ALU.add),
             reads=[yb, yfb], writes=[yb])
        k.op("act", lambda e: e.activation(out=t[0:64, 0:n], in_=y[0:64, 0:n], func=AF.Sin, scale=float(TWO_PI * (1 - 1e-6))),
             reads=[yb], writes=[tb])
        if which == 0:
            k.op("dve", lambda e: e.tensor_scalar(out=cos_out, in0=t[0:64, 0:n], scalar1=float(scale), scalar2=None, op0=ALU.mult),
                 reads=[tb], writes=[outb])
        else:
            k.op("dve", lambda e: e.tensor_scalar(out=sin_out, in0=t[0:64, 0:n], scalar1=sgn_ap, scalar2=float(scale),
                                                   op0=ALU.mult, op1=ALU.mult), reads=[tb], writes=[outb])


def phaseA_layout(has_tail, has_head):
    idx = {}
    n = 0

    def add(name, cnt):
        nonlocal n
        idx[name] = n
        n += cnt
    if has_tail:
        add("wo", 8)
        add("wz", 8)
        add("wgate", 48)
        add("wbr", 48)
        add("wout", 16)
        add("f2_w13", 2 * JC)
        add("f2_w2", JC)
        add("pgate", 16)
        add("pproj", 16)
    if has_head:
        add("f1_w13", 2 * JC)
        add("f1_w2", JC)
        add("wcq", 4)
        add("wckv", 4)
        add("wkr", 2)
    g = {}
    m = 0
    for name, cnt in (("mixp", 16), ("ssmn", 8), ("ffn2", 16), ("ple", 16), ("final", 16), ("ffn1", 16), ("mix", 16),
                      ("qn", 4), ("kvn", 4), ("invf", 1), ("sgn", 1)):
        g[name] = m
        m += cnt
    return idx, n, g, m


def build_phaseA(has_tail, has_head, is_last, TOK, nc=None, ext=None, shared=None):
    PASS = min(1024, TOK)
    NPASS = TOK // PASS
    NT = PASS // 512
    idx, NCH, gi, NG = phaseA_layout(has_tail, has_head)
    fused = nc is not None
    if not fused:
        nc = bass.Bass("TRN2", target_bir_lowering=False)
        hT_d = nc.dram_tensor("hT", [D, TOK], F32, kind="ExternalInput").ap()
        WA = nc.dram_tensor("WA", [NCH, 128, 2048], F32, kind="ExternalInput").ap()
        GA = nc.dram_tensor("GA", [128, NG], F32, kind="ExternalInput").ap()
        if has_tail:
            yT_d = nc.dram_tensor("yT", [3072, TOK], BF16, kind="ExternalInput").ap()
            pT_d = nc.dram_tensor("pT", [PLE_DIM, TOK], F32, kind="ExternalInput").ap()
        if has_head:
            pos_d = nc.dram_tensor("pos", [1, TOK], I32, kind="ExternalInput").ap()
            exT_d = nc.dram_tensor("exT", [EX_ROWS, TOK], BF16, kind="ExternalOutput").ap()
        hO_d = nc.dram_tensor("hTo", [D, TOK], F32, kind="ExternalOutput").ap()
    else:
        hT_d, WA, GA, hO_d = ext["hT"], ext["WA"], ext["GA"], ext["hTo"]
        yT_d, pT_d, pos_d, exT_d = ext.get("yT"), ext.get("pT"), ext.get("pos"), ext.get("exT")

    k = KB(nc, shared)
    cm = Common(k)
    nps = cm.nps
    hT = k.sbuf("hT", [128, KC, PASS], F32)
    hb = [[k.buf() for _ in range(NT)] for _ in range(KC)]
    xn = k.sbuf("xn", [128, KC, PASS], BF16)
    xb = [k.buf() for _ in range(NT)]
    ga = k.sbuf("ga", [128, NG], F32)
    gab = k.buf()
    ones = k.sbuf("ones", [128, 128], BF16)
    onesb = k.buf()
    NSLOT = 5
    wsl = [(k.sbuf("wsl", [128, 2048], BF16), k.buf(), k.dsem()) for _ in range(NSLOT)]
    wst = [0]
    sqr = mk_rot(k, "sq", 3, [128, 512], BF16)
    rstd = k.sbuf("rstd", [128, 512], F32)
    rstdb = k.buf()
    f32r = mk_rot(k, "f32t", 3, [128, 512], F32)
    U = k.sbuf("U", [128, 24 * PASS], BF16)
    ybuf = U[:, 0:8 * PASS].rearrange("p (a b) -> p a b", b=PASS)
    merged = U[:, 8 * PASS:24 * PASS].rearrange("p (a b) -> p a b", b=PASS)
    gT = [U[:, i * 4 * PASS:(i + 1) * 4 * PASS].rearrange("p (a b) -> p a b", b=PASS) for i in range(2)]
    pTs = U[:, 8 * PASS:10 * PASS].rearrange("p (a b) -> p a b", b=PASS)
    lat = U[:, 0:8 * PASS].bitcast(F32).rearrange("p (a b) -> p a b", b=PASS)
    latn = U[:, 8 * PASS:12 * PASS].rearrange("p (a b) -> p a b", b=PASS)
    krt = U[:, 12 * PASS:13 * PASS]
    cst = U[:, 13 * PASS:15 * PASS]
    Ub = k.buf()
    ybb = [[k.buf() for _ in range(NT)] for _ in range(8)]
    mgb = [[k.buf() for _ in range(NT)] for _ in range(KC)]
    gTb = [[[k.buf() for _ in range(NT)] for _ in range(4)] for _ in range(2)]
    latb = [[k.buf() for _ in range(NT)] for _ in range(4)]
    latnb = [k.buf() for _ in range(NT)]
    d_in = k.dsem()
    d_out = k.dsem()
    d_y = k.dsem()
    d_p = k.dsem()
    if has_head:
        cosf = k.sbuf("cosf", [64, PASS], F32)
        sinf = k.sbuf("sinf", [64, PASS], F32)
        csb = k.buf()
        posi = k.sbuf("posi", [64, PASS], I32)
        posib = k.buf()
        sct = [(k.sbuf("sct", [64, 512], F32), k.buf()) for _ in range(4)] + [(k.sbuf("sci", [64, 512], I32), k.buf())]

    def sl(t):
        return slice(t * 512, (t + 1) * 512)

    def wload(ci, n):
        i = wst[0]
        wst[0] = (i + 1) % NSLOT
        t, b, ds = wsl[i]
        k.dma("pool", ds, t[:, 0:n], WA[ci, :, 0:n], writes=[b])
        return t, b

    k.group("sp", d_in, [(ga[:], GA, [], [gab])])
    k.op("dve", lambda e: e.memset(ones[:], 1.0), writes=[onesb])

    def rmsnorm(src, srcb, gcol, nkc, dmodel, out, outb_fn, writes_extra=()):
        for t in range(NT):
            pt, pb = nps()
            for kc in range(nkc):
                sq, sqb = sqr.next()
                k.op("act", lambda e, sq=sq, kc=kc, t=t: e.activation(out=sq[:], in_=src[:, kc, sl(t)], func=AF.Square),
                     reads=[srcb[kc][t]], writes=[sqb])
                k.op("pe", lambda e, sq=sq, kc=kc, pt=pt: e.matmul(pt[:], lhsT=ones[:], rhs=sq[:], start=(kc == 0), stop=(kc == nkc - 1)),
                     reads=[onesb, sqb], writes=[pb])
            k.op("act", lambda e, pt=pt: e.activation(out=rstd[:], in_=pt[:], func=AF.Sqrt, scale=1.0 / dmodel, bias=EPS),
                 reads=[pb], writes=[rstdb])
            k.op("dve", lambda e: e.reciprocal(out=rstd[:], in_=rstd[:]), reads=[rstdb], writes=[rstdb])
            for kc in range(nkc):
                k.op("dve", lambda e, kc=kc, t=t: e.scalar_tensor_tensor(
                    out=out[:, kc, sl(t)], in0=src[:, kc, sl(t)], scalar=ga[:, gcol + kc:gcol + kc + 1], in1=rstd[:],
                    op0=ALU.mult, op1=ALU.mult), reads=[srcb[kc][t], gab, rstdb], writes=[outb_fn(kc, t)])

    def lin(ci, ncols_per_kc, nkc, rhs, rhsb_fn, evac, mrows=128):
        w, wb = wload(ci, nkc * ncols_per_kc)
        for t in range(NT):
            pt, pb = nps()

            def mm(e, w=w, pt=pt, t=t):
                r = None
                for kc in range(nkc):
                    r = e.matmul(pt[0:mrows, :], lhsT=w[:, kc * ncols_per_kc:(kc + 1) * ncols_per_kc], rhs=rhs[:, kc, sl(t)],
                                 start=(kc == 0), stop=(kc == nkc - 1))
                return r
            k.op("pe", mm, reads=[wb] + rhsb_fn(t), writes=[pb])
            evac(t, pt, pb)

    def xn_bufs(t):
        return [xb[t]]

    def ffn(gcol, c13, c2):
        rmsnorm(hT, hb, gcol, KC, D, xn, lambda kc, t: xb[t])
        J = 4
        for g in range(JC // J):
            gi_ = g % 2
            for jl in range(J):
                j = g * J + jl
                hold = {}

                def ev_a(t, pt, pb, hold=hold):
                    sa, sab = f32r.next()
                    k.op("act", lambda e, sa=sa, pt=pt: e.activation(out=sa[:], in_=pt[:], func=AF.Silu), reads=[pb], writes=[sab])
                    hold[t] = (sa, sab)
                lin(c13 + 2 * j, 128, KC, xn, xn_bufs, ev_a)

                def ev_b(t, pt, pb, hold=hold, gi_=gi_, jl=jl):
                    sa, sab = hold[t]
                    k.op("dve", lambda e, sa=sa, pt=pt, t=t: e.tensor_tensor(out=gT[gi_][:, jl, sl(t)], in0=sa[:], in1=pt[:], op=ALU.mult),
                         reads=[sab, pb], writes=[gTb[gi_][jl][t]])
                lin(c13 + 2 * j + 1, 128, KC, xn, xn_bufs, ev_b)
            w2s = [wload(c2 + g * J + jl, D) for jl in range(J)]
            for m in range(KC):
                for t in range(NT):
                    po, pob = nps()

                    def mm3(e, po=po, m=m, t=t, gi_=gi_, w2s=w2s):
                        r = None
                        for jl in range(J):
                            r = e.matmul(po[:], lhsT=w2s[jl][0][:, m * 128:(m + 1) * 128], rhs=gT[gi_][:, jl, sl(t)],
                                         start=(jl == 0), stop=(jl == J - 1))
                        return r
                    k.op("pe", mm3, reads=[w[1] for w in w2s] + [gTb[gi_][jl][t] for jl in range(J)], writes=[pob])
                    k.op("dve", lambda e, po=po, m=m, t=t: e.scalar_tensor_tensor(
                        out=hT[:, m, sl(t)], in0=po[:], scalar=0.5, in1=hT[:, m, sl(t)], op0=ALU.mult, op1=ALU.add),
                        reads=[pob, hb[m][t]], writes=[hb[m][t]])

    def region_switch(new_bufs):
        k.barrier()

    for ps_i in range(NPASS):
        t0 = ps_i * PASS
        k.group("sp", d_in, [(hT[:, kc, sl(t)], hT_d[kc * 128:(kc + 1) * 128, t0 + t * 512:t0 + (t + 1) * 512], [], [hb[kc][t]])
                             for kc in range(KC) for t in range(NT)])
        if has_tail:
            region_switch(None)
            rmsnorm(hT, hb, gi["mixp"], KC, D, xn, lambda kc, t: xb[t])
            for br in range(3):
                k.group("sp", d_y, [(ybuf[:, c, sl(t)], yT_d[br * 1024 + c * 128:br * 1024 + (c + 1) * 128, t0 + t * 512:t0 + (t + 1) * 512], [], [ybb[c][t]])
                                    for c in range(8) for t in range(NT)])
                if br < 2:
                    for c in range(8):
                        def ev_g(t, pt, pb, c=c, br=br):
                            sa, sab = f32r.next()
                            k.op("act", lambda e, sa=sa, pt=pt: e.activation(out=sa[:], in_=pt[:], func=(AF.Sigmoid if br == 0 else AF.Silu)),
                                 reads=[pb], writes=[sab])
                            k.op("dve", lambda e, sa=sa, c=c, t=t: e.tensor_tensor(out=ybuf[:, c, sl(t)], in0=ybuf[:, c, sl(t)], in1=sa[:], op=ALU.mult),
                                 reads=[sab, ybb[c][t]], writes=[ybb[c][t]])
                        lin(idx["wo" if br == 0 else "wz"] + c, 128, KC, xn, xn_bufs, ev_g)
                if br == 1:
                    rmsnorm(ybuf, ybb, gi["ssmn"], 8, 1024, ybuf, lambda kc, t: ybb[kc][t])
                for m in range(KC):
                    hold = {}

                    def ev_gate(t, pt, pb, hold=hold):
                        sa, sab = f32r.next()
                        k.op("act", lambda e, sa=sa, pt=pt: e.activation(out=sa[:], in_=pt[:], func=AF.Sigmoid), reads=[pb], writes=[sab])
                        hold[t] = (sa, sab)
                    lin(idx["wgate"] + br * 16 + m, 128, KC, xn, xn_bufs, ev_gate)

                    def ev_br(t, pt, pb, hold=hold, m=m, br=br):
                        sa, sab = hold[t]
                        if br == 0:
                            k.op("dve", lambda e, sa=sa, pt=pt, t=t: e.tensor_tensor(out=merged[:, m, sl(t)], in0=sa[:], in1=pt[:], op=ALU.mult),
                                 reads=[sab, pb], writes=[mgb[m][t]])
                        else:
                            k.op("dve", lambda e, sa=sa, pt=pt: e.tensor_tensor(out=sa[:], in0=sa[:], in1=pt[:], op=ALU.mult),
                                 reads=[sab, pb], writes=[sab])
                            k.op("pool", lambda e, sa=sa, t=t: e.tensor_tensor(out=merged[:, m, sl(t)], in0=merged[:, m, sl(t)], in1=sa[:], op=ALU.add),
                                 reads=[sab, mgb[m][t]], writes=[mgb[m][t]])
                    lin(idx["wbr"] + br * 16 + m, 128, 8, ybuf, lambda t: [ybb[c][t] for c in range(8)], ev_br)
            for m in range(KC):
                def ev_out(t, pt, pb, m=m):
                    k.op("dve", lambda e, pt=pt, t=t: e.tensor_tensor(out=hT[:, m, sl(t)], in0=hT[:, m, sl(t)], in1=pt[:], op=ALU.add),
                         reads=[pb, hb[m][t]], writes=[hb[m][t]])
                lin(idx["wout"] + m, 128, KC, merged, lambda t: [mgb[c][t] for c in range(KC)], ev_out)
            region_switch(None)
            ffn(gi["ffn2"], idx["f2_w13"], idx["f2_w2"])
            region_switch(None)
            rmsnorm(hT, hb, gi["ple"], KC, D, xn, lambda kc, t: xb[t])
            pTb = [k.buf() for _ in range(NT)]
            k.group("pool", d_p, [(pTs[:, c, sl(t)], pT_d[c * 128:(c + 1) * 128, t0 + t * 512:t0 + (t + 1) * 512], [], [pTb[t]])
                                  for c in range(2) for t in range(NT)])
            for m in range(KC):
                hold = {}

                def ev_pg(t, pt, pb, hold=hold):
                    sa, sab = f32r.next()
                    k.op("act", lambda e, sa=sa, pt=pt: e.activation(out=sa[:], in_=pt[:], func=AF.Sigmoid), reads=[pb], writes=[sab])
                    hold[t] = (sa, sab)
                lin(idx["pgate"] + m, 128, KC, xn, xn_bufs, ev_pg)

                def ev_pp(t, pt, pb, hold=hold, m=m):
                    sa, sab = hold[t]
                    k.op("dve", lambda e, sa=sa, pt=pt: e.tensor_tensor(out=sa[:], in0=sa[:], in1=pt[:], op=ALU.mult), reads=[sab, pb], writes=[sab])
                    k.op("pool", lambda e, sa=sa, t=t: e.tensor_tensor(out=hT[:, m, sl(t)], in0=hT[:, m, sl(t)], in1=sa[:], op=ALU.add),
                         reads=[sab, hb[m][t]], writes=[hb[m][t]])
                lin(idx["pproj"] + m, 128, 2, pTs, lambda t: [pTb[t]], ev_pp)
        if is_last:
            rmsnorm(hT, hb, gi["final"], KC, D, hT, lambda kc, t: hb[kc][t])
        if has_head:
            region_switch(None)
            ffn(gi["ffn1"], idx["f1_w13"], idx["f1_w2"])
            region_switch(None)
            rmsnorm(hT, hb, gi["mix"], KC, D, xn, lambda kc, t: xb[t])
            k.group("sp", d_out, [(exT_d[EX_U + kc * 128:EX_U + (kc + 1) * 128, t0 + t * 512:t0 + (t + 1) * 512], xn[:, kc, sl(t)], [xb[t]], [])
                                  for kc in range(KC) for t in range(NT)])
            k.group("sp", d_in, [(posi[:], pos_d[:, t0:t0 + PASS].partition_broadcast(64), [], [posib])])
            for t in range(NT):
                emit_sincos(k, cm, posi[:, sl(t)], posib, ga[0:64, gi["invf"]:gi["invf"] + 1], ga[0:64, gi["sgn"]:gi["sgn"] + 1], 512,
                            cosf[:, sl(t)], sinf[:, sl(t)], csb, sct)
            k.op("act", lambda e: e.copy(out=cst[0:64, 0:PASS], in_=cosf[:]), reads=[csb], writes=[Ub])
            k.op("act", lambda e: e.copy(out=cst[0:64, PASS:2 * PASS], in_=sinf[:]), reads=[csb], writes=[Ub])
            k.group("sp", d_out, [(exT_d[EX_COS:EX_COS + 64, t0:t0 + PASS], cst[0:64, 0:PASS], [Ub], []),
                                  (exT_d[EX_SIN:EX_SIN + 64, t0:t0 + PASS], cst[0:64, PASS:2 * PASS], [Ub], [])])
            for which, wname, gname, exoff in ((0, "wcq", "qn", EX_Q), (1, "wckv", "kvn", EX_KV)):
                for c in range(4):
                    def ev_lat(t, pt, pb, c=c):
                        k.op("act", lambda e, pt=pt, t=t: e.copy(out=lat[:, c, sl(t)], in_=pt[:]), reads=[pb], writes=[latb[c][t]])
                    lin(idx[wname] + c, 128, KC, xn, xn_bufs, ev_lat)
                rmsnorm(lat, latb, gi[gname], 4, 512, latn, lambda kc, t: latnb[t])
                k.group("sp", d_out, [(exT_d[exoff + c * 128:exoff + (c + 1) * 128, t0 + t * 512:t0 + (t + 1) * 512], latn[:, c, sl(t)], [latnb[t]], [])
                                      for c in range(4) for t in range(NT)])
            hold = {}

            def ev_kr(t, pt, pb, hold=hold):
                sa, sab = f32r.next()
                k.op("dve", lambda e, sa=sa, pt=pt, t=t: e.tensor_tensor(out=sa[0:64, :], in0=pt[0:64, :], in1=cosf[:, sl(t)], op=ALU.mult),
                     reads=[pb, csb], writes=[sab])
                hold[t] = (sa, sab)
            lin(idx["wkr"], 64, KC, xn, xn_bufs, ev_kr, mrows=64)
            krb = [k.buf() for _ in range(NT)]

            def ev_krs(t, pt, pb, hold=hold):
                sa, sab = hold[t]
                sb_, sbb = f32r.next()
                k.op("dve", lambda e, sb_=sb_, pt=pt, t=t: e.tensor_tensor(out=sb_[0:64, :], in0=pt[0:64, :], in1=sinf[:, sl(t)], op=ALU.mult),
                     reads=[pb, csb], writes=[sbb])
                k.op("pool", lambda e, sa=sa, sb_=sb_, t=t: e.tensor_tensor(out=krt[0:64, sl(t)], in0=sa[0:64, :], in1=sb_[0:64, :], op=ALU.add),
                     reads=[sab, sbb], writes=[krb[t]])
                k.group("sp", d_out, [(exT_d[EX_KR:EX_KR + 64, t0 + t * 512:t0 + (t + 1) * 512], krt[0:64, sl(t)], [krb[t]], [])])
            lin(idx["wkr"] + 1, 64, KC, xn, xn_bufs, ev_krs, mrows=64)
        k.group("sp", d_out, [(hO_d[kc * 128:(kc + 1) * 128, t0 + t * 512:t0 + (t + 1) * 512], hT[:, kc, sl(t)], [hb[kc][t]], [])
                              for kc in range(KC) for t in range(NT)])
        k.barrier()
    if fused:
        k.barrier()
    else:
        k.wait_all("sp")
    k.emit()
    return nc


def phaseA_weights(inp, L_tail, L_head):
    has_tail = L_tail is not None
    has_head = L_head is not None
    idx, NCH, gi, NG = phaseA_layout(has_tail, has_head)
    WA = np.zeros((NCH, 128, 2048), np.float32)
    GA = np.zeros((128, NG), np.float32)

    def put(name, arr):
        n = arr.shape[0]
        WA[idx[name]:idx[name] + n, :, :arr.shape[2]] = arr

    def ffn_pack(prefix, w13, w2):
        a = chunks_lhsT(w13[:, :DFF])
        b = chunks_lhsT(w13[:, DFF:])
        ab = np.empty((2 * JC, 128, 2048), np.float32)
        ab[0::2] = a
        ab[1::2] = b
        put(prefix + "_w13", ab)
        put(prefix + "_w2", w2.reshape(JC, 128, D))
    if has_tail:
        L = L_tail
        w_in = inp["w_in"][L]
        put("wo", chunks_lhsT(w_in[:, OFF_O:OFF_O + 1024]))
        put("wz", chunks_lhsT(w_in[:, OFF_Z:OFF_Z + 1024]))
        put("wgate", chunks_lhsT(w_in[:, OFF_GATE:OFF_GATE + 3 * D]))
        wbr = np.concatenate([chunks_lhsT(inp["w_branch"][L, i]) for i in range(3)], axis=0)
        put("wbr", wbr)
        put("wout", chunks_lhsT(inp["w_out"][L]))
        ffn_pack("f2", inp["ffn2_w13"][L], inp["ffn2_w2"][L])
        put("pgate", chunks_lhsT(inp["w_ple_gate"][L]))
        put("pproj", chunks_lhsT(inp["w_ple_proj"][L]))
        GA[:, gi["mixp"]:gi["mixp"] + 16] = gain_cols(inp["mix_norm"][L])
        GA[:, gi["ssmn"]:gi["ssmn"] + 8] = gain_cols(inp["ssm_norm"][L])
        GA[:, gi["ffn2"]:gi["ffn2"] + 16] = gain_cols(inp["ffn2_norm"][L])
        GA[:, gi["ple"]:gi["ple"] + 16] = gain_cols(inp["ple_norm"][L])
    GA[:, gi["final"]:gi["final"] + 16] = gain_cols(inp["final_norm"])
    if has_head:
        L = L_head
        w_in = inp["w_in"][L]
        ffn_pack("f1", inp["ffn1_w13"][L], inp["ffn1_w2"][L])
        put("wcq", chunks_lhsT(w_in[:, OFF_CQ:OFF_CQ + 512]))
        put("wckv", chunks_lhsT(w_in[:, OFF_CKV:OFF_CKV + 512]))
        wkr = w_in[:, OFF_KR:OFF_KR + 64]
        wkr_sw = np.concatenate([wkr[:, 32:], wkr[:, :32]], axis=1)
        put("wkr", np.concatenate([chunks_lhsT(wkr, 64), chunks_lhsT(wkr_sw, 64)], axis=0))
        GA[:, gi["ffn1"]:gi["ffn1"] + 16] = gain_cols(inp["ffn1_norm"][L])
        GA[:, gi["mix"]:gi["mix"] + 16] = gain_cols(inp["mix_norm"][L])
        GA[:, gi["qn"]:gi["qn"] + 4] = gain_cols(inp["mla_q_norm"][L])
        GA[:, gi["kvn"]:gi["kvn"] + 4] = gain_cols(inp["mla_kv_norm"][L])
    invf = (10000.0 ** (-np.arange(32, dtype=np.float32) / 32)).astype(np.float32)
    GA[0:64, gi["invf"]] = np.concatenate([invf, invf])
    GA[0:64, gi["sgn"]] = np.concatenate([-np.ones(32, np.float32), np.ones(32, np.float32)])
    return WA, GA


PB_BIG, PB_BFG, PB_MN, PB_DTB, PB_ALOG, PB_DSK, NPB = 0, 2, 4, 260, 268, 276, 532
ARENA_BYTES = 188 * 1024
B_PARTS = (1, 2, 3)
DBG = 99


class Arena:
    def __init__(self, k, nbytes):
        self.t = k.sbuf("arena", [128, nbytes // 2], BF16)
        self.nbytes = nbytes
        self.off = 0

    def reset(self):
        self.off = 0

    def alloc(self, free, dtype):
        sz = 4 if dtype in (F32, I32) else 2
        n = int(np.prod(free))
        self.off = (self.off + 63) // 64 * 64
        start = self.off // 2
        nb = n * sz
        assert self.off + nb <= self.nbytes, ("arena overflow", self.off, nb)
        v = self.t[:, start:start + nb // 2]
        if dtype == F32:
            v = v.bitcast(F32)
        elif dtype == I32:
            v = v.bitcast(I32)
        self.off += nb
        if len(free) == 2:
            v = v.rearrange("p (a b) -> p a b", b=free[1])
        elif len(free) == 3:
            v = v.rearrange("p (a b c) -> p a b c", b=free[1], c=free[2])
        return v


class ARot:
    def __init__(self, k, ar, n, free, dtype):
        self.items = [(ar.alloc(free, dtype), k.buf()) for _ in range(n)]
        self.i = 0

    def next(self):
        it = self.items[self.i]
        self.i = (self.i + 1) % len(self.items)
        return it


def build_phaseB(SEQ_, nc=None, ext=None, shared=None):
    NCk = SEQ_ // 128
    fused = nc is not None
    if not fused:
        nc = bass.Bass("TRN2", target_bir_lowering=False)
        exT_d = nc.dram_tensor("exT", [EX_ROWS, SEQ_], BF16, kind="ExternalInput").ap()
        WB1 = nc.dram_tensor("WB1", [2, 128, 2048], F32, kind="ExternalInput").ap()
        WB1t = nc.dram_tensor("WB1t", [128, 16, 388], F32, kind="ExternalInput").ap()
        PB = nc.dram_tensor("PB", [1, NPB], F32, kind="ExternalInput").ap()
        CW = nc.dram_tensor("CW", [128, 4, 6], F32, kind="ExternalInput").ap()
        WB2 = nc.dram_tensor("WB2", [4, 128, 2048], F32, kind="ExternalInput").ap()
        WB2t = nc.dram_tensor("WB2t", [128, 16, 8], F32, kind="ExternalInput").ap()
        WB3 = nc.dram_tensor("WB3", [2, 5, 128, 512], F32, kind="ExternalInput").ap()
        CONST = nc.dram_tensor("CONST", [8, 128, 128], F32, kind="ExternalInput").ap()
        yT_d = nc.dram_tensor("yT", [768, SEQ_], BF16, kind="ExternalOutput").ap()
        yparts = [yT_d[0:256], yT_d[256:512], yT_d[512:768]]
    else:
        exT_d, WB1, WB1t, PB, CW, WB2, WB2t, WB3, CONST = (ext[n] for n in ("exT", "WB1", "WB1t", "PB", "CW", "WB2", "WB2t", "WB3", "CONST"))
        yparts = ext["yparts"]

    k = KB(nc, shared)
    cm = Common(k)
    nps = cm.nps
    ar = Arena(k, ARENA_BYTES)
    cst = k.sbuf("cst", [128, 8, 128], F32)
    cstb = k.buf()
    TRI = [cst[:, 0, :], cst[:, 1, :]]
    STR = [cst[:, 2, :], cst[:, 3, :]]
    ID32 = cst[:, 4, :]
    ONE32 = cst[:, 5, :]
    MS = [cst[:, 6, :], cst[:, 7, :]]
    idh = k.sbuf("idh", [128, 128], BF16)
    oneh = k.sbuf("oneh", [128, 128], BF16)
    pbt = k.sbuf("pbt", [128, NPB], F32)
    pbb = k.buf()
    d_c = k.dsem()
    d_u = [k.dsem(), k.dsem()]
    d_w = k.dsem()
    d_o = k.dsem()
    k.group("sp", d_c, [(cst[:, 0:6, :], CONST[0:6].rearrange("c p n -> p c n"), [], [cstb]),
                        (pbt[:], PB.partition_broadcast(128), [], [pbb])])
    k.op("act", lambda e: e.copy(out=idh[:], in_=ID32), reads=[cstb], writes=[cstb])
    k.op("act", lambda e: e.copy(out=oneh[:], in_=ONE32), reads=[cstb], writes=[cstb])
    sc = 128.0 ** -0.5
    k.op("dve", lambda e: e.tensor_scalar(out=MS[0], in0=TRI[0], scalar1=sc, scalar2=None, op0=ALU.mult), reads=[cstb], writes=[cstb])
    k.op("dve", lambda e: e.tensor_scalar(out=MS[1], in0=TRI[1], scalar1=sc, scalar2=None, op0=ALU.mult), reads=[cstb], writes=[cstb])

    def pcol(c):
        return pbt[:, c:c + 1]

    def ew(eng, out, in0, in1, op, reads, writes):
        k.op(eng, lambda e: e.tensor_tensor(out=out, in0=in0, in1=in1, op=op), reads=reads, writes=writes)

    def emit_B1():
        ar.reset()
        TB = 256
        w1q = ar.alloc((2048,), BF16)
        w1k = ar.alloc((2048,), BF16)
        w1t = ar.alloc((16, 388), BF16)
        wb_ = k.buf()
        urot = ARot(k, ar, 2, (16, TB), BF16)
        qT = ar.alloc((SEQ_,), BF16)
        kT = ar.alloc((SEQ_,), BF16)
        ktok = ar.alloc((NCk, 128), BF16)
        vext = ar.alloc((NCk, 260), BF16)
        gates = ar.alloc((NCk, 4), F32)
        hbk = ar.alloc((NCk, 256), BF16)
        qb = [k.buf() for _ in range(NCk)]
        kb_ = [k.buf() for _ in range(NCk)]
        ktb = [k.buf() for _ in range(NCk)]
        vb = [k.buf() for _ in range(NCk)]
        gtb = k.buf()
        hbb = [k.buf() for _ in range(NCk)]
        if DBG == 11:
            k.barrier()
            return
        k.group("pool", d_w, [(w1q, WB1[0], [], [wb_]), (w1k, WB1[1], [], [wb_]), (w1t, WB1t, [], [wb_])])
        if DBG == 12:
            k.barrier()
            return
        k.op("pool", lambda e: e.memset(vext, 1.0), writes=vb)
        if DBG == 13:
            k.barrier()
            return
        cpt = TB // 128
        for tt in range(SEQ_ // TB):
            if DBG == 14 and tt == 1:
                k.barrier()
                return
            u, ub = urot.next()
            k.group("sp", d_u[tt % 2], [(u, exT_d[EX_U:EX_U + 2048, tt * TB:(tt + 1) * TB].rearrange("(kc p) t -> p kc t", p=128), [], [ub])])
            cl = list(range(tt * cpt, (tt + 1) * cpt))
            for (w, dst, dbs, eng) in ((w1q, qT, qb, "act"), (w1k, kT, kb_, "dve")):
                pt, pb = nps()

                def mm(e, w=w, pt=pt, u=u):
                    r = None
                    for kc in range(16):
                        r = e.matmul(pt[:, 0:TB], lhsT=w[:, kc * 128:(kc + 1) * 128], rhs=u[:, kc, :], start=(kc == 0), stop=(kc == 15))
                    return r
                k.op("pe", mm, reads=[wb_, ub], writes=[pb])
                if eng == "act":
                    k.op("act", lambda e, pt=pt, dst=dst, tt=tt: e.copy(out=dst[:, tt * TB:(tt + 1) * TB], in_=pt[:, 0:TB]), reads=[pb], writes=[dbs[c] for c in cl])
                else:
                    k.op("dve", lambda e, pt=pt, dst=dst, tt=tt: e.tensor_copy(out=dst[:, tt * TB:(tt + 1) * TB], in_=pt[:, 0:TB]), reads=[pb], writes=[dbs[c] for c in cl])
            if DBG == 15:
                k.barrier()
                return
            for cc in range(cpt):
                c = tt * cpt + cc
                pt, pb = nps()

                def mm2(e, pt=pt, u=u, cc=cc):
                    r = None
                    for kc in range(16):
                        r = e.matmul(pt[:, 0:388], lhsT=u[:, kc, cc * 128:(cc + 1) * 128], rhs=w1t[:, kc, :], start=(kc == 0), stop=(kc == 15))
                    return r
                k.op("pe", mm2, reads=[wb_, ub], writes=[pb])
                if DBG == 16:
                    k.barrier()
                    return
                k.op("act", lambda e, pt=pt, c=c: e.copy(out=ktok[:, c, :], in_=pt[:, 0:128]), reads=[pb], writes=[ktb[c]])
                if DBG == 17:
                    k.barrier()
                    return
                k.op("dve", lambda e, pt=pt, c=c: e.tensor_copy(out=vext[:, c, 0:256], in_=pt[:, 128:384]), reads=[pb], writes=[vb[c]])
                if DBG == 18:
                    k.barrier()
                    return
                k.op("act", lambda e, pt=pt, c=c: e.copy(out=gates[:, c, :], in_=pt[:, 384:388]), reads=[pb], writes=[gtb])
                if DBG == 19:
                    k.barrier()
                    return

        if DBG == 1:
            k.barrier()
            return

        def T():
            return ar.alloc((NCk,), F32)
        G = []

        def gmath(d):
            gd = {}
            tb_ = k.buf()
            tA, lf, li, bsb, gsum, a, cmx, mloc, Mc, einter, ea, es, aprev, aloc, floor_, tmp = [T() for _ in range(16)]
            marr = ar.alloc((NCk + 1,), F32)
            sm = ar.alloc((8,), F32)
            dg = ar.alloc((NCk,), F32)
            apad = ar.alloc((128,), F32)
            k.op("dve", lambda e: e.memset(apad, 0.0), writes=[tb_])
            k.op("dve", lambda e: e.memset(sm, 0.0), writes=[tb_])
            R_ = [tb_]
            k.op("dve", lambda e: e.tensor_scalar(out=sm[:, 0:1], in0=pcol(PB_BFG + d), scalar1=-1.0, scalar2=None, op0=ALU.mult), reads=[pbb], writes=R_)
            k.op("act", lambda e: e.activation(out=tA, in_=gates[:, :, 2 + d], func=AF.Exp, scale=-1.0, bias=sm[:, 0:1]), reads=[gtb] + R_, writes=R_)
            k.op("act", lambda e: e.activation(out=tA, in_=tA, func=AF.Ln, bias=1.0), reads=R_, writes=R_)
            k.op("dve", lambda e: e.tensor_scalar(out=lf, in0=tA, scalar1=-1.0, scalar2=None, op0=ALU.mult), reads=R_, writes=R_)
            k.op("dve", lambda e: e.tensor_scalar(out=li, in0=gates[:, :, d], scalar1=pcol(PB_BIG + d), scalar2=None, op0=ALU.add), reads=[gtb, pbb] + R_, writes=R_)
            p1, p1b = nps()
            p2, p2b = nps()
            k.op("pe", lambda e: e.matmul(p1[:, 0:NCk], lhsT=TRI[d], rhs=lf, start=True, stop=True), reads=[cstb] + R_, writes=[p1b])
            k.op("pe", lambda e: e.matmul(p2[:, 0:NCk], lhsT=ONE32, rhs=lf, start=True, stop=True), reads=[cstb] + R_, writes=[p2b])
            k.op("act", lambda e: e.copy(out=bsb, in_=p1[:, 0:NCk]), reads=[p1b], writes=R_)
            k.op("dve", lambda e: e.tensor_copy(out=gsum, in_=p2[:, 0:NCk]), reads=[p2b], writes=R_)
            ew("dve", a, li, bsb, ALU.subtract, R_, R_)
            k.op("dve", lambda e: e.tensor_copy(out=apad[:, 0:NCk], in_=a), reads=R_, writes=R_)
            p3, p3b = nps()
            k.op("pe", lambda e: e.matmul(p3[:, 0:128], lhsT=apad, rhs=ID32, start=True, stop=True), reads=[cstb] + R_, writes=[p3b])
            k.op("dve", lambda e: e.reduce_max(out=sm[0:NCk, 1:2], in_=p3[0:NCk, 0:128], axis=AX.X), reads=[p3b], writes=R_)
            k.op("dve", lambda e: e.tensor_scalar(out=dg, in0=ID32[:, 0:NCk], scalar1=sm[:, 1:2], scalar2=None, op0=ALU.mult), reads=[cstb] + R_, writes=R_)
            p4, p4b = nps()
            k.op("pe", lambda e: e.matmul(p4[:, 0:NCk], lhsT=ONE32, rhs=dg, start=True, stop=True), reads=[cstb] + R_, writes=[p4b])
            k.op("act", lambda e: e.copy(out=cmx, in_=p4[:, 0:NCk]), reads=[p4b], writes=R_)
            ew("dve", mloc, gsum, cmx, ALU.add, R_, R_)
            k.op("dve", lambda e: e.memset(marr, 0.0), writes=R_)
            order = list(range(NCk)) if d == 0 else list(reversed(range(NCk)))
            for c in order:
                src = c if d == 0 else c + 1
                dst = c + 1 if d == 0 else c
                k.op("dve", lambda e, c=c, src=src, dst=dst: e.scalar_tensor_tensor(
                    out=marr[:, dst:dst + 1], in0=marr[:, src:src + 1], scalar=gsum[:, c:c + 1], in1=mloc[:, c:c + 1], op0=ALU.add, op1=ALU.max),
                    reads=R_, writes=R_)
            m0 = marr[:, 0:NCk] if d == 0 else marr[:, 1:NCk + 1]
            mn = marr[:, 1:NCk + 1] if d == 0 else marr[:, 0:NCk]
            ew("dve", Mc, m0, cmx, ALU.max, R_, R_)
            ew("dve", tmp, m0, Mc, ALU.subtract, R_, R_)
            k.op("act", lambda e: e.activation(out=einter, in_=tmp, func=AF.Exp), reads=R_, writes=R_)
            ew("dve", tmp, a, Mc, ALU.subtract, R_, R_)
            k.op("act", lambda e: e.activation(out=ea, in_=tmp, func=AF.Exp), reads=R_, writes=R_)
            ew("dve", tmp, a, cmx, ALU.subtract, R_, R_)
            k.op("act", lambda e: e.activation(out=es, in_=tmp, func=AF.Exp), reads=R_, writes=R_)
            k.op("dve", lambda e: e.tensor_scalar(out=es, in0=es, scalar1=sc, scalar2=None, op0=ALU.mult), reads=R_, writes=R_)
            ew("dve", tmp, gsum, m0, ALU.add, R_, R_)
            ew("dve", tmp, tmp, mn, ALU.subtract, R_, R_)
            k.op("act", lambda e: e.activation(out=aprev, in_=tmp, func=AF.Exp), reads=R_, writes=R_)
            ew("dve", tmp, mloc, mn, ALU.subtract, R_, R_)
            k.op("act", lambda e: e.activation(out=aloc, in_=tmp, func=AF.Exp), reads=R_, writes=R_)
            ew("dve", tmp, bsb, Mc, ALU.add, R_, R_)
            k.op("act", lambda e: e.activation(out=floor_, in_=tmp, func=AF.Exp, scale=-1.0), reads=R_, writes=R_)
            gd.update(b=tb_, einter=einter, ea=ea, es=es, aprev=aprev, aloc=aloc, floor=floor_)
            G.append(gd)
        gmath(0)
        gmath(1)

        if DBG == 2:
            k.barrier()
            return
        PTr = ARot(k, ar, 2, (128,), BF16)
        vscr = ARot(k, ar, 2, (260,), BF16)
        kscr = ARot(k, ar, 2, (128,), BF16)
        t1r = ARot(k, ar, 2, (260,), F32)
        t2r = ARot(k, ar, 2, (260,), F32)
        t3r = ARot(k, ar, 2, (260,), F32)
        smr = ARot(k, ar, 3, (8,), F32)
        hsr = ARot(k, ar, 2, (256,), F32)
        jkr = ARot(k, ar, 1, (256,), F32)
        ybr = ARot(k, ar, 2, (256,), BF16)
        ystr = ARot(k, ar, 2, (2, 512), BF16)
        C32 = ar.alloc((260,), F32)
        C32b = k.buf()
        Cbfr = ARot(k, ar, 2, (260,), BF16)

        def scan1(d):
            gd = G[d]
            gb = [gd["b"]]
            k.op("pool", lambda e: e.memset(C32, 0.0), writes=[C32b])
            cbf, cbfb = Cbfr.next()
            k.op("pool", lambda e, cbf=cbf: e.memset(cbf, 0.0), writes=[cbfb])
            order = list(range(NCk)) if d == 0 else list(reversed(range(NCk)))
            yst = None
            for c in order:
                cs = slice(c * 128, (c + 1) * 128)
                pS, pSb = nps()
                k.op("pe", lambda e, pS=pS, cs=cs: e.matmul(pS[:, 0:128], lhsT=kT[:, cs], rhs=qT[:, cs], start=True, stop=True),
                     reads=[kb_[c], qb[c]], writes=[pSb])
                PT, PTb = PTr.next()
                k.op("dve", lambda e, PT=PT, pS=pS: e.tensor_tensor(out=PT, in0=pS[:, 0:128], in1=MS[d], op=ALU.mult), reads=[pSb, cstb], writes=[PTb])
                vsc, vscb = vscr.next()
                k.op("pool", lambda e, vsc=vsc, c=c: e.tensor_scalar(out=vsc[:, 0:257], in0=vext[:, c, 0:257], scalar1=gd["ea"][:, c:c + 1], scalar2=None, op0=ALU.mult),
                     reads=[vb[c]] + gb, writes=[vscb])
                pI, pIb = nps()
                k.op("pe", lambda e, pI=pI, PT=PT, vsc=vsc: e.matmul(pI[:, 0:257], lhsT=PT, rhs=vsc[:, 0:257], start=True, stop=True),
                     reads=[PTb, vscb], writes=[pIb])
                pX, pXb = nps()
                k.op("pe", lambda e, pX=pX, cs=cs, cbf=cbf: e.matmul(pX[:, 0:257], lhsT=qT[:, cs], rhs=cbf[:, 0:257], start=True, stop=True),
                     reads=[qb[c], cbfb], writes=[pXb])
                t1, t1b = t1r.next()
                k.op("act", lambda e, t1=t1, pX=pX, c=c: e.activation(out=t1[:, 0:257], in_=pX[:, 0:257], func=AF.Identity, scale=gd["einter"][:, c:c + 1]),
                     reads=[pXb] + gb, writes=[t1b])
                t2, t2b = t2r.next()
                k.op("dve", lambda e, t2=t2, t1=t1, pI=pI: e.tensor_tensor(out=t2[:, 0:257], in0=t1[:, 0:257], in1=pI[:, 0:257], op=ALU.add),
                     reads=[t1b, pIb], writes=[t2b])
                sm, smb = smr.next()
                k.op("dve", lambda e, sm=sm, t2=t2: e.scalar_tensor_tensor(out=sm[:, 0:1], in0=t2[:, 256:257], scalar=-1.0, in1=t2[:, 256:257], op0=ALU.mult, op1=ALU.max),
                     reads=[t2b], writes=[smb])
                k.op("dve", lambda e, sm=sm, c=c: e.tensor_tensor(out=sm[:, 1:2], in0=sm[:, 0:1], in1=gd["floor"][:, c:c + 1], op=ALU.max), reads=[smb] + gb, writes=[smb])
                k.op("dve", lambda e, sm=sm: e.reciprocal(out=sm[:, 2:3], in_=sm[:, 1:2]), reads=[smb], writes=[smb])
                if d == 1:
                    k.op("dve", lambda e, sm=sm, t2=t2, c=c: e.tensor_scalar(out=hbk[:, c, :], in0=t2[:, 0:256], scalar1=sm[:, 2:3], scalar2=None, op0=ALU.mult),
                         reads=[smb, t2b], writes=[hbb[c]])
                else:
                    hs, hsb = hsr.next()
                    k.op("dve", lambda e, hs=hs, sm=sm, t2=t2, c=c: e.scalar_tensor_tensor(out=hs, in0=t2[:, 0:256], scalar=sm[:, 2:3], in1=hbk[:, c, :], op0=ALU.mult, op1=ALU.add),
                         reads=[smb, t2b, hbb[c]], writes=[hsb])
                    jk, jkb = jkr.next()
                    k.op("act", lambda e, jk=jk, hs=hs, sm=sm: e.activation(out=jk, in_=hs, func=AF.Square, accum_out=sm[:, 3:4]), reads=[hsb], writes=[jkb, smb])
                    k.op("dve", lambda e, sm=sm: e.tensor_scalar(out=sm[:, 4:5], in0=sm[:, 3:4], scalar1=1.0 / 256, scalar2=EPS, op0=ALU.mult, op1=ALU.add), reads=[smb], writes=[smb])
                    k.op("act", lambda e, sm=sm: e.activation(out=sm[:, 4:5], in_=sm[:, 4:5], func=AF.Sqrt), reads=[smb], writes=[smb])
                    k.op("dve", lambda e, sm=sm: e.reciprocal(out=sm[:, 5:6], in_=sm[:, 4:5]), reads=[smb], writes=[smb])
                    yb_, ybb_ = ybr.next()
                    k.op("dve", lambda e, yb_=yb_, hs=hs, sm=sm: e.scalar_tensor_tensor(out=yb_, in0=hs, scalar=sm[:, 5:6], in1=pbt[:, PB_MN:PB_MN + 256], op0=ALU.mult, op1=ALU.mult),
                         reads=[hsb, smb, pbb], writes=[ybb_])
                    pT_, pTb_ = cm.pst.next()
                    pbf = pT_[:, 0:256]

                    def tr(e, pbf=pbf, yb_=yb_):
                        e.transpose(pbf[:, 0:128], yb_[:, 0:128], idh[:])
                        return e.transpose(pbf[:, 128:256], yb_[:, 128:256], idh[:])
                    k.op("pe", tr, reads=[ybb_, cstb], writes=[pTb_])
                    if c % 4 == 0:
                        yst = ystr.next()
                    ys, ysb = yst
                    cc = c % 4
                    k.op("act", lambda e, ys=ys, pbf=pbf, cc=cc: e.copy(out=ys[:, :, cc * 128:(cc + 1) * 128], in_=pbf.rearrange("p (a b) -> p a b", b=128)),
                         reads=[pTb_], writes=[ysb])
                    if c % 4 == 3:
                        c0 = c - 3
                        k.group("sp", d_o, [(yparts[0][:, c0 * 128:(c0 + 4) * 128].rearrange("(a p) t -> p a t", p=128), ys, [ysb], [])])
                ksc, kscb = kscr.next()
                k.op("pool", lambda e, ksc=ksc, c=c: e.tensor_scalar(out=ksc, in0=ktok[:, c, :], scalar1=gd["es"][:, c:c + 1], scalar2=None, op0=ALU.mult),
                     reads=[ktb[c]] + gb, writes=[kscb])
                pC, pCb = nps()
                k.op("pe", lambda e, pC=pC, ksc=ksc, c=c: e.matmul(pC[:, 0:257], lhsT=ksc, rhs=vext[:, c, 0:257], start=True, stop=True),
                     reads=[kscb, vb[c]], writes=[pCb])
                t3, t3b = t3r.next()
                k.op("act", lambda e, t3=t3, pC=pC, c=c: e.activation(out=t3[:, 0:257], in_=pC[:, 0:257], func=AF.Identity, scale=gd["aloc"][:, c:c + 1]),
                     reads=[pCb] + gb, writes=[t3b])
                k.op("dve", lambda e, t3=t3, c=c: e.scalar_tensor_tensor(out=C32[:, 0:257], in0=C32[:, 0:257], scalar=gd["aprev"][:, c:c + 1], in1=t3[:, 0:257], op0=ALU.mult, op1=ALU.add),
                     reads=[t3b, C32b] + gb, writes=[C32b])
                cbf, cbfb = Cbfr.next()
                k.op("act", lambda e, cbf=cbf: e.copy(out=cbf[:, 0:257], in_=C32[:, 0:257]), reads=[C32b], writes=[cbfb])
        scan1(1)
        scan1(0)
        k.barrier()

    def emit_B2():
        ar.reset()
        TB = 512
        NTB = SEQ_ // TB
        CT = ar.alloc((SEQ_,), BF16)
        BT = ar.alloc((SEQ_,), BF16)
        Btok = ar.alloc((NCk, 128), BF16)
        xtok = ar.alloc((NCk, 256), BF16)
        dtr = ar.alloc((2, NCk, 4), F32)
        dA = ar.alloc((2, NCk, 4), F32)
        eacs = ar.alloc((2, NCk, 4), F32)
        dte = ar.alloc((2, NCk, 4), F32)
        cdk = ar.alloc((2, NCk, 4), F32)
        cwt = ar.alloc((4, 6), F32)
        ctb = [k.buf() for _ in range(NCk)]
        btb = [k.buf() for _ in range(NCk)]
        bkb = [k.buf() for _ in range(NCk)]
        xkb = [k.buf() for _ in range(NCk)]
        dtb_ = k.buf()
        gmb = k.buf()
        cwb = k.buf()
        mark = ar.off
        w2 = [ar.alloc((2048,), BF16) for _ in range(4)]
        w2t = ar.alloc((16, 8), BF16)
        wb_ = k.buf()
        urot = ARot(k, ar, 2, (16, TB), BF16)
        ring = [ar.alloc((4, TB + 4), F32) for _ in range(3)]
        ringb = [[k.buf() for _ in range(4)] for _ in range(3)]
        accr = ARot(k, ar, 2, (TB,), F32)
        cvr = ARot(k, ar, 3, (TB,), BF16)
        k.group("pool", d_w, [(w2[i], WB2[i], [], [wb_]) for i in range(4)] + [(w2t, WB2t, [], [wb_])])
        k.group("sp", d_c, [(cwt, CW, [], [cwb])])
        for i in range(3):
            k.op("pool", lambda e, i=i: e.memset(ring[i], 0.0), writes=ringb[i])

        def conv_tile(ti):
            r = ring[ti % 3]
            rb = ringb[ti % 3]
            for ch in range(4):
                acc, accb = accr.next()
                k.op("dve", lambda e, acc=acc, r=r, ch=ch: e.tensor_scalar(out=acc, in0=r[:, ch, 0:TB], scalar1=cwt[:, ch, 0:1], scalar2=None, op0=ALU.mult),
                     reads=[rb[ch], cwb], writes=[accb])
                for kk in range(1, 5):
                    eng = "dve"
                    k.op(eng, lambda e, acc=acc, r=r, ch=ch, kk=kk: e.scalar_tensor_tensor(out=acc, in0=r[:, ch, kk:kk + TB], scalar=cwt[:, ch, kk:kk + 1], in1=acc, op0=ALU.mult, op1=ALU.add),
                         reads=[rb[ch], cwb, accb], writes=[accb])
                cv, cvb = cvr.next()
                cl = list(range(ti * 4, ti * 4 + 4))
                if ch == 3:
                    k.op("act", lambda e, acc=acc, ti=ti: e.activation(out=CT[:, ti * TB:(ti + 1) * TB], in_=acc, func=AF.Silu, bias=cwt[:, 3, 5:6]),
                         reads=[accb, cwb], writes=[ctb[c] for c in cl])
                    continue
                if ch == 2:
                    k.op("act", lambda e, acc=acc, ti=ti: e.activation(out=BT[:, ti * TB:(ti + 1) * TB], in_=acc, func=AF.Silu, bias=cwt[:, 2, 5:6]),
                         reads=[accb, cwb], writes=[btb[c] for c in cl])
                    src = BT[:, ti * TB:(ti + 1) * TB]
                    srcb = [btb[c] for c in cl]
                else:
                    k.op("act", lambda e, acc=acc, cv=cv, ch=ch: e.activation(out=cv, in_=acc, func=AF.Silu, bias=cwt[:, ch, 5:6]),
                         reads=[accb, cwb], writes=[cvb])
                    src = cv
                    srcb = [cvb]
                pT_, pTb_ = cm.pst.next()
                pbf = pT_[:, 0:512]

                def tr(e, pbf=pbf, src=src):
                    r_ = None
                    for cc in range(4):
                        r_ = e.transpose(pbf[:, cc * 128:(cc + 1) * 128], src[:, cc * 128:(cc + 1) * 128], idh[:])
                    return r_
                k.op("pe", tr, reads=srcb + [cstb], writes=[pTb_])
                if ch == 2:
                    k.op("act", lambda e, pbf=pbf, ti=ti: e.copy(out=Btok[:, ti * 4:ti * 4 + 4, :], in_=pbf.rearrange("p (a b) -> p a b", b=128)),
                         reads=[pTb_], writes=[bkb[c] for c in cl])
                else:
                    k.op("act", lambda e, pbf=pbf, ti=ti, ch=ch: e.copy(out=xtok[:, ti * 4:ti * 4 + 4, ch * 128:(ch + 1) * 128], in_=pbf.rearrange("p (a b) -> p a b", b=128)),
                         reads=[pTb_], writes=[xkb[c] for c in cl])

        for tt in range(NTB):
            u, ub = urot.next()
            k.group("sp", d_u[tt % 2], [(u, exT_d[EX_U:EX_U + 2048, tt * TB:(tt + 1) * TB].rearrange("(kc p) t -> p kc t", p=128), [], [ub])])
            r = ring[tt % 3]
            rb = ringb[tt % 3]
            if tt + 1 < NTB or True:
                pass
            for ch in range(4):
                pt, pb = nps()

                def mm(e, pt=pt, u=u, ch=ch):
                    r_ = None
                    for kc in range(16):
                        r_ = e.matmul(pt[:, 0:TB], lhsT=w2[ch][:, kc * 128:(kc + 1) * 128], rhs=u[:, kc, :], start=(kc == 0), stop=(kc == 15))
                    return r_
                k.op("pe", mm, reads=[wb_, ub], writes=[pb])
                k.op("act", lambda e, pt=pt, r=r, ch=ch: e.copy(out=r[:, ch, 2:TB + 2], in_=pt[:, 0:TB]), reads=[pb], writes=[rb[ch]])
                if tt > 0:
                    rp = ring[(tt - 1) % 3]
                    k.op("dve", lambda e, pt=pt, rp=rp, ch=ch: e.tensor_copy(out=rp[:, ch, TB + 2:TB + 4], in_=pt[:, 0:2]), reads=[pb], writes=[ringb[(tt - 1) % 3][ch]])
                if tt + 1 < NTB:
                    rn = ring[(tt + 1) % 3]
                    k.op("dve", lambda e, pt=pt, rn=rn, ch=ch: e.tensor_copy(out=rn[:, ch, 0:2], in_=pt[:, TB - 2:TB]), reads=[pb], writes=[ringb[(tt + 1) % 3][ch]])
                else:
                    k.op("dve", lambda e, r=r, ch=ch: e.memset(r[:, ch, TB + 2:TB + 4], 0.0), writes=[rb[ch]])
            if tt == 0:
                for ch in range(4):
                    k.op("dve", lambda e, r=r, ch=ch: e.memset(r[:, ch, 0:2], 0.0), writes=[rb[ch]])
            for cc in range(4):
                c = tt * 4 + cc
                pt, pb = nps()

                def mm2(e, pt=pt, u=u, cc=cc):
                    r_ = None
                    for kc in range(16):
                        r_ = e.matmul(pt[:, 0:8], lhsT=u[:, kc, cc * 128:(cc + 1) * 128], rhs=w2t[:, kc, :], start=(kc == 0), stop=(kc == 15))
                    return r_
                k.op("pe", mm2, reads=[wb_, ub], writes=[pb])
                k.op("dve", lambda e, pt=pt, c=c: e.tensor_copy(out=dtr[:, :, c, :], in_=pt[:, 0:8].rearrange("p (a b) -> p a b", b=4)), reads=[pb], writes=[dtb_])
            if tt > 0:
                conv_tile(tt - 1)
        conv_tile(NTB - 1)

        t_a = ar.alloc((2, NCk, 4), F32)
        t_b = ar.alloc((2, NCk, 4), F32)
        acs = ar.alloc((2, NCk, 4), F32)
        tot = ar.alloc((2, NCk, 4), F32)
        Abc = ar.alloc((8,), F32)
        R_ = [gmb]
        n2 = NCk * 4
        dtbias = pbt[:, PB_DTB:PB_DTB + 8].rearrange("p (a b) -> p a b", b=4).unsqueeze(2).broadcast_to([128, 2, NCk, 4])
        k.op("dve", lambda e: e.tensor_tensor(out=dtr, in0=dtr, in1=dtbias, op=ALU.add), reads=[dtb_, pbb], writes=[dtb_])
        k.op("dve", lambda e: e.scalar_tensor_tensor(out=t_a, in0=dtr, scalar=-1.0, in1=dtr, op0=ALU.mult, op1=ALU.max), reads=[dtb_], writes=R_)
        k.op("act", lambda e: e.activation(out=t_a, in_=t_a, func=AF.Exp, scale=-1.0), reads=R_, writes=R_)
        k.op("act", lambda e: e.activation(out=t_a, in_=t_a, func=AF.Ln, bias=1.0), reads=R_, writes=R_)
        k.op("dve", lambda e: e.scalar_tensor_tensor(out=dtr, in0=dtr, scalar=0.0, in1=t_a, op0=ALU.max, op1=ALU.add), reads=R_ + [dtb_], writes=[dtb_])
        k.op("act", lambda e: e.activation(out=Abc, in_=pbt[:, PB_ALOG:PB_ALOG + 8], func=AF.Exp), reads=[pbb], writes=R_)
        k.op("dve", lambda e: e.tensor_scalar(out=Abc, in0=Abc, scalar1=-1.0, scalar2=None, op0=ALU.mult), reads=R_, writes=R_)
        Abb = Abc.rearrange("p (a b) -> p a b", b=4).unsqueeze(2).broadcast_to([128, 2, NCk, 4])
        k.op("dve", lambda e: e.tensor_tensor(out=dA, in0=dtr, in1=Abb, op=ALU.mult), reads=R_ + [dtb_], writes=R_)
        for d in range(2):
            p1, p1b = nps()
            p2, p2b = nps()
            k.op("pe", lambda e, p1=p1, d=d: e.matmul(p1[:, 0:n2], lhsT=TRI[d], rhs=dA[:, d].rearrange("p a b -> p (a b)"), start=True, stop=True), reads=[cstb] + R_, writes=[p1b])
            k.op("pe", lambda e, p2=p2, d=d: e.matmul(p2[:, 0:n2], lhsT=ONE32, rhs=dA[:, d].rearrange("p a b -> p (a b)"), start=True, stop=True), reads=[cstb] + R_, writes=[p2b])
            k.op("act", lambda e, p1=p1, d=d: e.copy(out=acs[:, d].rearrange("p a b -> p (a b)"), in_=p1[:, 0:n2]), reads=[p1b], writes=R_)
            k.op("dve", lambda e, p2=p2, d=d: e.tensor_copy(out=tot[:, d].rearrange("p a b -> p (a b)"), in_=p2[:, 0:n2]), reads=[p2b], writes=R_)
        k.op("act", lambda e: e.activation(out=eacs, in_=acs, func=AF.Exp), reads=R_, writes=R_)
        k.op("act", lambda e: e.activation(out=cdk, in_=tot, func=AF.Exp), reads=R_, writes=R_)
        ew("dve", t_b, tot, acs, ALU.subtract, R_, R_)
        k.op("act", lambda e: e.activation(out=t_b, in_=t_b, func=AF.Exp), reads=R_, writes=R_)
        ew("dve", dte, t_b, dtr, ALU.mult, R_ + [dtb_], R_)
        k.barrier()

        ar.off = mark
        yacc = ar.alloc((NCk, 256), F32)
        yab = [k.buf() for _ in range(NCk)]
        Gr = ARot(k, ar, 2, (128,), F32)
        Lr = ARot(k, ar, 3, (128,), F32)
        decr = ARot(k, ar, 3, (128,), F32)
        Wr = ARot(k, ar, 4, (128,), BF16)
        xdr = ARot(k, ar, 2, (256,), BF16)
        xddr = ARot(k, ar, 2, (256,), BF16)
        tmpr = ARot(k, ar, 2, (256,), F32)
        ytr = ARot(k, ar, 2, (256,), F32)
        ybr = ARot(k, ar, 2, (256,), BF16)
        ystr = ARot(k, ar, 2, (2, 512), BF16)
        S32 = ar.alloc((256,), F32)
        S32b = k.buf()
        Sbfr = ARot(k, ar, 2, (256,), BF16)

        def scan2(d):
            k.op("pool", lambda e: e.memset(S32, 0.0), writes=[S32b])
            sbf, sbfb = Sbfr.next()
            k.op("pool", lambda e, sbf=sbf: e.memset(sbf, 0.0), writes=[sbfb])
            order = list(range(NCk)) if d == 0 else list(reversed(range(NCk)))
            yst = None
            for c in order:
                cs = slice(c * 128, (c + 1) * 128)
                pCB, pCBb = nps()
                k.op("pe", lambda e, pCB=pCB, cs=cs: e.matmul(pCB[:, 0:128], lhsT=BT[:, cs], rhs=CT[:, cs], start=True, stop=True), reads=[btb[c], ctb[c]], writes=[pCBb])
                Gm, Gb = Gr.next()
                k.op("dve", lambda e, Gm=Gm, pCB=pCB: e.tensor_tensor(out=Gm, in0=pCB[:, 0:128], in1=TRI[d], op=ALU.mult), reads=[pCBb, cstb], writes=[Gb])
                xd, xdb = xdr.next()
                k.op("pool", lambda e, xd=xd, c=c: e.tensor_tensor(out=xd.rearrange("p (a b) -> p a b", b=64), in0=xtok[:, c, :].rearrange("p (a b) -> p a b", b=64),
                                                                  in1=dtr[:, d, c, :].unsqueeze(2).broadcast_to([128, 4, 64]), op=ALU.mult),
                     reads=[xkb[c], dtb_], writes=[xdb])
                pY, pYb = nps()
                for hl in range(4):
                    Lm, Lb = Lr.next()
                    k.op("pool", lambda e, Lm=Lm, c=c, hl=hl: e.tensor_scalar(out=Lm, in0=STR[d], scalar1=dA[:, d, c, hl:hl + 1], scalar2=None, op0=ALU.mult),
                         reads=[cstb, gmb], writes=[Lb])
                    pSg, pSgb = nps()
                    k.op("pe", lambda e, pSg=pSg, Lm=Lm: e.matmul(pSg[:, 0:128], lhsT=Lm, rhs=TRI[d], start=True, stop=True), reads=[Lb, cstb], writes=[pSgb])
                    dec, decb = decr.next()
                    k.op("act", lambda e, dec=dec, pSg=pSg: e.activation(out=dec, in_=pSg[:, 0:128], func=AF.Exp), reads=[pSgb], writes=[decb])
                    Wm, Wb = Wr.next()
                    k.op("dve", lambda e, Wm=Wm, Gm=Gm, dec=dec: e.tensor_tensor(out=Wm, in0=Gm, in1=dec, op=ALU.mult), reads=[Gb, decb], writes=[Wb])
                    k.op("pe", lambda e, pY=pY, Wm=Wm, xd=xd, hl=hl: e.matmul(pY[:, hl * 64:(hl + 1) * 64], lhsT=Wm, rhs=xd[:, hl * 64:(hl + 1) * 64], start=True, stop=True),
                         reads=[Wb, xdb], writes=[pYb])
                pO, pOb = nps()
                k.op("pe", lambda e, pO=pO, cs=cs, sbf=sbf: e.matmul(pO[:, 0:256], lhsT=CT[:, cs], rhs=sbf, start=True, stop=True), reads=[ctb[c], sbfb], writes=[pOb])
                tmp, tmpb = tmpr.next()
                k.op("dve", lambda e, tmp=tmp, pO=pO, c=c: e.tensor_tensor(out=tmp.rearrange("p (a b) -> p a b", b=64), in0=pO[:, 0:256].rearrange("p (a b) -> p a b", b=64),
                                                                        in1=eacs[:, d, c, :].unsqueeze(2).broadcast_to([128, 4, 64]), op=ALU.mult),
                     reads=[pOb, gmb], writes=[tmpb])
                if d == 1:
                    k.op("dve", lambda e, tmp=tmp, pY=pY, c=c: e.tensor_tensor(out=yacc[:, c, :], in0=tmp, in1=pY[:, 0:256], op=ALU.add), reads=[tmpb, pYb], writes=[yab[c]])
                else:
                    yt, ytb = ytr.next()
                    k.op("dve", lambda e, yt=yt, tmp=tmp, pY=pY: e.tensor_tensor(out=yt, in0=tmp, in1=pY[:, 0:256], op=ALU.add), reads=[tmpb, pYb], writes=[ytb])
                    k.op("pool", lambda e, yt=yt, c=c: e.tensor_tensor(out=yt, in0=yt, in1=yacc[:, c, :], op=ALU.add), reads=[ytb, yab[c]], writes=[ytb])
                    k.op("pool", lambda e, tmp=tmp, c=c: e.tensor_tensor(out=tmp, in0=xtok[:, c, :], in1=pbt[:, PB_DSK:PB_DSK + 256], op=ALU.mult), reads=[xkb[c], pbb, tmpb], writes=[tmpb])
                    yb_, ybb_ = ybr.next()
                    k.op("dve", lambda e, yb_=yb_, yt=yt, tmp=tmp: e.tensor_tensor(out=yb_, in0=yt, in1=tmp, op=ALU.add), reads=[ytb, tmpb], writes=[ybb_])
                    pT_, pTb_ = cm.pst.next()
                    pbf = pT_[:, 0:256]

                    def tr2(e, pbf=pbf, yb_=yb_):
                        e.transpose(pbf[:, 0:128], yb_[:, 0:128], idh[:])
                        return e.transpose(pbf[:, 128:256], yb_[:, 128:256], idh[:])
                    k.op("pe", tr2, reads=[ybb_, cstb], writes=[pTb_])
                    if c % 4 == 0:
                        yst = ystr.next()
                    ys, ysb = yst
                    cc = c % 4
                    k.op("act", lambda e, ys=ys, pbf=pbf, cc=cc: e.copy(out=ys[:, :, cc * 128:(cc + 1) * 128], in_=pbf.rearrange("p (a b) -> p a b", b=128)),
                         reads=[pTb_], writes=[ysb])
                    if c % 4 == 3:
                        c0 = c - 3
                        k.group("sp", d_o, [(yparts[1][:, c0 * 128:(c0 + 4) * 128].rearrange("(a p) t -> p a t", p=128), ys, [ysb], [])])
                xdd, xddb = xddr.next()
                k.op("pool", lambda e, xdd=xdd, c=c: e.tensor_tensor(out=xdd.rearrange("p (a b) -> p a b", b=64), in0=xtok[:, c, :].rearrange("p (a b) -> p a b", b=64),
                                                                   in1=dte[:, d, c, :].unsqueeze(2).broadcast_to([128, 4, 64]), op=ALU.mult),
                     reads=[xkb[c], gmb], writes=[xddb])
                pSt, pStb = nps()
                k.op("pe", lambda e, pSt=pSt, c=c, xdd=xdd: e.matmul(pSt[:, 0:256], lhsT=Btok[:, c, :], rhs=xdd, start=True, stop=True), reads=[bkb[c], xddb], writes=[pStb])
                k.op("dve", lambda e, c=c: e.tensor_tensor(out=S32.rearrange("p (a b) -> p a b", b=64), in0=S32.rearrange("p (a b) -> p a b", b=64),
                                                           in1=cdk[:, d, c, :].unsqueeze(2).broadcast_to([128, 4, 64]), op=ALU.mult), reads=[S32b, gmb], writes=[S32b])
                k.op("dve", lambda e, pSt=pSt: e.tensor_tensor(out=S32, in0=S32, in1=pSt[:, 0:256], op=ALU.add), reads=[S32b, pStb], writes=[S32b])
                sbf, sbfb = Sbfr.next()
                k.op("act", lambda e, sbf=sbf: e.copy(out=sbf, in_=S32), reads=[S32b], writes=[sbfb])
        scan2(1)
        scan2(0)
        k.barrier()

    def emit_B3():
        TB = 512
        NTB = SEQ_ // TB
        qscale = 192.0 ** -0.5

        def head(hh):
            ar.reset()
            wq = [ar.alloc((512,), BF16) for _ in range(5)]
            wb_ = k.buf()
            qN = ar.alloc((SEQ_,), BF16)
            qR = ar.alloc((SEQ_,), BF16)
            kN = ar.alloc((SEQ_,), BF16)
            kR = ar.alloc((SEQ_,), BF16)
            V = ar.alloc((NCk, 128), BF16)
            qNb = [k.buf() for _ in range(NTB)]
            qRb = [k.buf() for _ in range(NTB)]
            kNb = [k.buf() for _ in range(NCk)]
            kRb = k.buf()
            Vb = [k.buf() for _ in range(NCk)]
            latr = ARot(k, ar, 2, (4, TB), BF16)
            csr = ARot(k, ar, 2, (2, TB), BF16)
            sqr = ARot(k, ar, 3, (TB,), BF16)
            f32r = ARot(k, ar, 4, (TB,), F32)
            PTr = ARot(k, ar, 3, (TB,), BF16)
            yor = ARot(k, ar, 2, (TB,), BF16)
            kmax = ar.alloc((4,), F32)
            kmb = k.buf()
            k.group("pool", d_w, [(wq[i], WB3[hh, i], [], [wb_]) for i in range(5)])
            k.op("pool", lambda e: e.memset(kR, 1.0), writes=[kRb])
            k.group("sp", d_c, [(kR[0:64, :], exT_d[EX_KR:EX_KR + 64, :], [], [kRb])])
            k.op("dve", lambda e: e.memset(kmax, 0.0), writes=[kmb])
            for tt in range(NTB):
                ts = slice(tt * TB, (tt + 1) * TB)
                lt, ltb = latr.next()
                k.group("sp", d_u[tt % 2], [(lt, exT_d[EX_KV:EX_KV + 512, ts].rearrange("(kc p) t -> p kc t", p=128), [], [ltb])])
                cl = list(range(tt * 4, tt * 4 + 4))
                pt, pb = nps()

                def mm(e, pt=pt, lt=lt):
                    r_ = None
                    for kc in range(4):
                        r_ = e.matmul(pt[:, 0:TB], lhsT=wq[3][:, kc * 128:(kc + 1) * 128], rhs=lt[:, kc, :], start=(kc == 0), stop=(kc == 3))
                    return r_
                k.op("pe", mm, reads=[wb_, ltb], writes=[pb])
                k.op("act", lambda e, pt=pt, ts=ts: e.copy(out=kN[:, ts], in_=pt[:, 0:TB]), reads=[pb], writes=[kNb[c] for c in cl])
                for cc in range(4):
                    c = tt * 4 + cc
                    pv, pvb = nps()

                    def mmv(e, pv=pv, lt=lt, cc=cc):
                        r_ = None
                        for kc in range(4):
                            r_ = e.matmul(pv[:, 0:128], lhsT=lt[:, kc, cc * 128:(cc + 1) * 128], rhs=wq[4][:, kc * 128:(kc + 1) * 128], start=(kc == 0), stop=(kc == 3))
                        return r_
                    k.op("pe", mmv, reads=[wb_, ltb], writes=[pvb])
                    k.op("dve", lambda e, pv=pv, c=c: e.tensor_copy(out=V[:, c, :], in_=pv[:, 0:128]), reads=[pvb], writes=[Vb[c]])
                s1, s1b = sqr.next()
                s2, s2b = sqr.next()
                k.op("act", lambda e, s1=s1, ts=ts: e.activation(out=s1, in_=kN[:, ts], func=AF.Square), reads=[kNb[c] for c in cl], writes=[s1b])
                k.op("act", lambda e, s2=s2, ts=ts: e.activation(out=s2[0:64, :], in_=kR[0:64, ts], func=AF.Square), reads=[kRb], writes=[s2b])
                pn, pnb = nps()

                def mmn(e, pn=pn, s1=s1, s2=s2):
                    e.matmul(pn[:, 0:TB], lhsT=oneh[:], rhs=s1, start=True, stop=False)
                    return e.matmul(pn[:, 0:TB], lhsT=oneh[0:64, :], rhs=s2[0:64, :], start=False, stop=True)
                k.op("pe", mmn, reads=[s1b, s2b, cstb], writes=[pnb])
                k.op("dve", lambda e, pn=pn: e.reduce_max(out=kmax[:, 1:2], in_=pn[:, 0:TB], axis=AX.X), reads=[pnb, kmb], writes=[kmb])
                k.op("dve", lambda e: e.tensor_tensor(out=kmax[:, 0:1], in0=kmax[:, 0:1], in1=kmax[:, 1:2], op=ALU.max), reads=[kmb], writes=[kmb])
            k.op("act", lambda e: e.activation(out=kmax[:, 2:3], in_=kmax[:, 0:1], func=AF.Sqrt), reads=[kmb], writes=[kmb])
            for tt in range(NTB):
                ts = slice(tt * TB, (tt + 1) * TB)
                lt, ltb = latr.next()
                cs_, csb = csr.next()
                k.group("sp", d_u[tt % 2], [(lt, exT_d[EX_Q:EX_Q + 512, ts].rearrange("(kc p) t -> p kc t", p=128), [], [ltb]),
                                             (cs_[0:64, 0, :], exT_d[EX_COS:EX_COS + 64, ts], [], [csb]),
                                             (cs_[0:64, 1, :], exT_d[EX_SIN:EX_SIN + 64, ts], [], [csb])])
                pt, pb = nps()

                def mmq(e, pt=pt, lt=lt):
                    r_ = None
                    for kc in range(4):
                        r_ = e.matmul(pt[:, 0:TB], lhsT=wq[0][:, kc * 128:(kc + 1) * 128], rhs=lt[:, kc, :], start=(kc == 0), stop=(kc == 3))
                    return r_
                k.op("pe", mmq, reads=[wb_, ltb], writes=[pb])
                k.op("act", lambda e, pt=pt, ts=ts: e.activation(out=qN[:, ts], in_=pt[:, 0:TB], func=AF.Copy, scale=qscale), reads=[pb], writes=[qNb[tt]])
                pr, prb = nps()
                pw, pwb = nps()
                for (wi, pp, ppb) in ((1, pr, prb), (2, pw, pwb)):
                    def mmr(e, pp=pp, lt=lt, wi=wi):
                        r_ = None
                        for kc in range(4):
                            r_ = e.matmul(pp[0:64, 0:TB], lhsT=wq[wi][:, kc * 64:(kc + 1) * 64], rhs=lt[:, kc, :], start=(kc == 0), stop=(kc == 3))
                        return r_
                    k.op("pe", mmr, reads=[wb_, ltb], writes=[ppb])
                a1, a1b = f32r.next()
                a2, a2b = f32r.next()
                k.op("dve", lambda e, a1=a1, pr=pr, cs_=cs_: e.tensor_tensor(out=a1[0:64, :], in0=pr[0:64, 0:TB], in1=cs_[0:64, 0, :], op=ALU.mult), reads=[prb, csb], writes=[a1b])
                k.op("dve", lambda e, a2=a2, pw=pw, cs_=cs_: e.tensor_tensor(out=a2[0:64, :], in0=pw[0:64, 0:TB], in1=cs_[0:64, 1, :], op=ALU.mult), reads=[pwb, csb], writes=[a2b])
                k.op("pool", lambda e, a1=a1, a2=a2: e.tensor_tensor(out=a1[0:64, :], in0=a1[0:64, :], in1=a2[0:64, :], op=ALU.add), reads=[a1b, a2b], writes=[a1b])
                k.op("act", lambda e, a1=a1, ts=ts: e.activation(out=qR[0:64, ts], in_=a1[0:64, :], func=AF.Copy, scale=qscale), reads=[a1b], writes=[qRb[tt]])
                s1, s1b = sqr.next()
                s2, s2b = sqr.next()
                k.op("act", lambda e, s1=s1, ts=ts: e.activation(out=s1, in_=qN[:, ts], func=AF.Square), reads=[qNb[tt]], writes=[s1b])
                k.op("act", lambda e, s2=s2, ts=ts: e.activation(out=s2[0:64, :], in_=qR[0:64, ts], func=AF.Square), reads=[qRb[tt]], writes=[s2b])
                pn, pnb = nps()

                def mmn2(e, pn=pn, s1=s1, s2=s2):
                    e.matmul(pn[:, 0:TB], lhsT=oneh[:], rhs=s1, start=True, stop=False)
                    return e.matmul(pn[:, 0:TB], lhsT=oneh[0:64, :], rhs=s2[0:64, :], start=False, stop=True)
                k.op("pe", mmn2, reads=[s1b, s2b, cstb], writes=[pnb])
                a3, a3b = f32r.next()
                k.op("act", lambda e, a3=a3, pn=pn: e.activation(out=a3[64:65, :], in_=pn[64:65, 0:TB], func=AF.Sqrt), reads=[pnb], writes=[a3b])
                k.op("dve", lambda e, a3=a3, ts=ts: e.tensor_scalar(out=qR[64:65, ts], in0=a3[64:65, :], scalar1=kmax[64:65, 2:3], scalar2=-1.0, op0=ALU.mult, op1=ALU.mult),
                     reads=[a3b, kmb, qRb[tt]], writes=[qRb[tt]])
            accs = [[cm.ps[3], cm.ps[4]], [cm.ps[5], cm.ps[6]]]
            srot = Rot([cm.ps[i] for i in range(3)])
            for qg in range(NTB):
                ts = slice(qg * TB, (qg + 1) * TB)
                (pO, pOb), (pL, pLb) = accs[qg % 2]
                for kb in range(NCk):
                    ks = slice(kb * 128, (kb + 1) * 128)
                    pS, pSb = srot.next()

                    def mms(e, pS=pS, ks=ks, ts=ts):
                        e.matmul(pS[:, 0:TB], lhsT=kN[:, ks], rhs=qN[:, ts], start=True, stop=False)
                        return e.matmul(pS[:, 0:TB], lhsT=kR[0:65, ks], rhs=qR[0:65, ts], start=False, stop=True)
                    k.op("pe", mms, reads=[kNb[kb], kRb, qNb[qg], qRb[qg]], writes=[pSb])
                    PT, PTb = PTr.next()
                    k.op("act", lambda e, PT=PT, pS=pS: e.activation(out=PT, in_=pS[:, 0:TB], func=AF.Exp), reads=[pSb], writes=[PTb])

                    def mmo(e, pO=pO, pL=pL, PT=PT, kb=kb):
                        e.matmul(pO[:, 0:TB], lhsT=V[:, kb, :], rhs=PT, start=(kb == 0), stop=(kb == NCk - 1))
                        return e.matmul(pL[:, 0:TB], lhsT=oneh[:], rhs=PT, start=(kb == 0), stop=(kb == NCk - 1))
                    k.op("pe", mmo, reads=[Vb[kb], PTb, cstb], writes=[pOb, pLb])
                rl, rlb = f32r.next()
                k.op("dve", lambda e, rl=rl, pL=pL: e.reciprocal(out=rl, in_=pL[:, 0:TB]), reads=[pLb], writes=[rlb])
                yo, yob = yor.next()
                k.op("dve", lambda e, yo=yo, pO=pO, rl=rl: e.tensor_tensor(out=yo, in0=pO[:, 0:TB], in1=rl, op=ALU.mult), reads=[pOb, rlb], writes=[yob])
                k.group("sp", d_o, [(yparts[2][hh * 128:(hh + 1) * 128, ts], yo, [yob], [])])
            k.barrier()
        head(0)
        head(1)

    if 1 in B_PARTS:
        emit_B1()
    if 2 in B_PARTS:
        emit_B2()
    if 3 in B_PARTS:
        emit_B3()
    if fused:
        k.barrier()
    else:
        k.wait_all("sp")
    k.emit()
    return nc


def phaseB_weights(inp, L, g):
    w_in = inp["w_in"][L]
    out = {}
    wq = w_in[:, OFF_Q + g * 128:OFF_Q + (g + 1) * 128]
    wk = w_in[:, OFF_K + g * 128:OFF_K + (g + 1) * 128]
    wv = w_in[:, OFF_V + g * 256:OFF_V + (g + 1) * 256]
    gc = [OFF_IG + g, OFF_IG + 4 + g, OFF_FG + g, OFF_FG + 4 + g]
    out["WB1"] = np.ascontiguousarray(np.concatenate([chunks_lhsT(wq), chunks_lhsT(wk)], axis=0))
    out["WB1t"] = rhs_layout(np.concatenate([wk, wv, w_in[:, gc]], axis=1))
    pb = np.zeros((1, NPB), np.float32)
    pb[0, PB_BIG:PB_BIG + 2] = inp["mlstm_b_igate"][L][:, g]
    pb[0, PB_BFG:PB_BFG + 2] = inp["mlstm_b_fgate"][L][:, g]
    pb[0, PB_MN:PB_MN + 256] = inp["mlstm_norm"][L][g * 256:(g + 1) * 256]
    pb[0, PB_DTB:PB_DTB + 8] = inp["ssm_dt_bias"][L][:, 4 * g:4 * g + 4].reshape(8)
    pb[0, PB_ALOG:PB_ALOG + 8] = inp["ssm_a_log"][L][:, 4 * g:4 * g + 4].reshape(8)
    pb[0, PB_DSK:PB_DSK + 256] = np.repeat(inp["ssm_d"][L][4 * g:4 * g + 4], 64)
    out["PB"] = pb
    grp = g // 2
    chans = np.concatenate([np.arange(256 * g, 256 * g + 256), 1024 + grp * 128 + np.arange(128), 1280 + grp * 128 + np.arange(128)])
    cw = np.zeros((128, 4, 6), np.float32)
    cwl = inp["conv_w"][L][:, chans]
    cw[:, :, 0:5] = cwl.reshape(5, 4, 128).transpose(2, 1, 0)
    cw[:, :, 5] = inp["conv_b"][L][chans].reshape(4, 128).T
    out["CW"] = cw
    out["WB2"] = np.ascontiguousarray(chunks_lhsT(w_in[:, OFF_XBC + chans]))
    dtc = [OFF_DT + d * 16 + 4 * g + hl for d in range(2) for hl in range(4)]
    out["WB2t"] = rhs_layout(w_in[:, dtc])
    w3 = np.zeros((2, 5, 128, 512), np.float32)
    for hh in range(2):
        h = 2 * g + hh
        uq = inp["mla_w_uq"][L][:, h * 192:(h + 1) * 192]
        ukv = inp["mla_w_ukv"][L][:, h * 256:(h + 1) * 256]
        w3[hh, 0] = chunks_lhsT(uq[:, 0:128])[0]
        rot = uq[:, 128:192]
        w3[hh, 1, :, 0:256] = chunks_lhsT(rot, 64)[0]
        w3[hh, 2, :, 0:256] = chunks_lhsT(np.concatenate([rot[:, 32:], rot[:, :32]], axis=1), 64)[0]
        w3[hh, 3] = chunks_lhsT(ukv[:, 0:128])[0]
        w3[hh, 4] = rhs_layout(ukv[:, 128:256]).reshape(128, 512)
    out["WB3"] = w3
    out["CONST"] = tri_consts()
    return out


_PROG = {}


def _prog(key, fn):
    if key not in _PROG:
        _PROG[key] = fn()
    return _PROG[key]


def build_fused(SEQ_, DEPTH_):
    TOK = SEQ_ // 4
    nc = bass.Bass("TRN2", target_bir_lowering=False)

    def din(name, shape, dt):
        return nc.dram_tensor(name, list(shape), dt, kind="ExternalInput").ap()
    xT = din("xT", [D, SEQ_], F32)
    pT = din("pT", [DEPTH_, PLE_DIM, SEQ_], F32)
    pos = din("pos", [1, SEQ_], I32)
    WAs, GAs = [], []
    for i in range(DEPTH_ + 1):
        _, nch, _, ng = phaseA_layout(i > 0, i < DEPTH_)
        WAs.append(din("WA%d" % i, [nch, 128, 2048], F32))
        GAs.append(din("GA%d" % i, [128, ng], F32))
    nB = DEPTH_ * 4
    WB1 = din("WB1", [nB, 2, 128, 2048], F32)
    WB1t = din("WB1t", [nB, 128, 16, 388], F32)
    PB = din("PB", [nB, 1, NPB], F32)
    CW = din("CW", [nB, 128, 4, 6], F32)
    WB2 = din("WB2", [nB, 4, 128, 2048], F32)
    WB2t = din("WB2t", [nB, 128, 16, 8], F32)
    WB3 = din("WB3", [nB, 2, 5, 128, 512], F32)
    CONST = din("CONST", [8, 128, 128], F32)
    outT = nc.dram_tensor("outT", [D, SEQ_], F32, kind="ExternalOutput").ap()
    hT = nc.dram_tensor("hT_int", [D, SEQ_], F32, kind="Internal").ap()
    exT = nc.dram_tensor("exT_int", [EX_ROWS, SEQ_], BF16, kind="Internal").ap()
    yT = nc.dram_tensor("yT_int", [3072, SEQ_], BF16, kind="Internal").ap()
    shared = Shared(nc)
    for i in range(DEPTH_ + 1):
        has_tail = i > 0
        has_head = i < DEPTH_
        for q in range(4):
            cs = slice(q * TOK, (q + 1) * TOK)
            ext = dict(hT=(xT if i == 0 else hT)[:, cs], WA=WAs[i], GA=GAs[i], hTo=(hT if has_head else outT)[:, cs],
                       yT=yT[:, cs], pT=(pT[i - 1][:, cs] if has_tail else None), pos=pos[:, cs], exT=exT[:, cs])
            build_phaseA(has_tail, has_head, not has_head, TOK, nc=nc, ext=ext, shared=shared)
        if has_head:
            for g in range(4):
                j = i * 4 + g
                ext = dict(exT=exT, WB1=WB1[j], WB1t=WB1t[j], PB=PB[j], CW=CW[j], WB2=WB2[j], WB2t=WB2t[j], WB3=WB3[j], CONST=CONST,
                           yparts=[yT[br * 1024 + g * 256:br * 1024 + (g + 1) * 256] for br in range(3)])
                build_phaseB(SEQ_, nc=nc, ext=ext, shared=shared)
    shared.close()
    return nc


FUSED = True


def kernel(**inp):
    if FUSED:
        return kernel_fused(**inp)
    return kernel_unfused(**inp)


def kernel_fused(**inp):
    inp = {kk: np.asarray(v) for kk, v in inp.items()}
    nc = _prog(("F", SEQ, DEPTH), lambda: build_fused(SEQ, DEPTH))
    common = {}
    for i in range(DEPTH + 1):
        WA, GA = phaseA_weights(inp, i - 1 if i > 0 else None, i if i < DEPTH else None)
        common["WA%d" % i] = WA
        common["GA%d" % i] = GA
    wb = [phaseB_weights(inp, L, g) for L in range(DEPTH) for g in range(4)]
    for name in ("WB1", "WB1t", "PB", "CW", "WB2", "WB2t", "WB3"):
        common[name] = np.ascontiguousarray(np.stack([w[name] for w in wb], axis=0))
    common["CONST"] = tri_consts()
    in_maps = []
    for b in range(BATCH):
        m = dict(common)
        m["xT"] = np.ascontiguousarray(inp["x"][b].T)
        m["pT"] = np.ascontiguousarray(inp["p"][:, b].transpose(0, 2, 1))
        m["pos"] = np.ascontiguousarray(inp["positions"][b:b + 1]).astype(np.int32)
        in_maps.append(m)
    res = run_bass_kernel_spmd(nc, in_maps, core_ids=list(range(BATCH)))
    out = np.empty((BATCH, SEQ, D), np.float32)
    for b in range(BATCH):
        out[b] = np.asarray(res.results[b]["outT"]).T
    return out


def kernel_unfused(**inp):
    inp = {kk: np.asarray(v) for kk, v in inp.items()}
    TOK = SEQ // 4
    ncores = BATCH * 4
    cores = list(range(ncores))
    x = inp["x"]
    hT = [np.ascontiguousarray(x[c // 4, (c % 4) * TOK:(c % 4 + 1) * TOK].T) for c in cores]
    yT = None
    out = None
    for i in range(DEPTH + 1):
        has_tail = i > 0
        has_head = i < DEPTH
        nc = _prog(("A", has_tail, has_head, TOK), lambda: build_phaseA(has_tail, has_head, not has_head, TOK))
        WA, GA = phaseA_weights(inp, i - 1 if has_tail else None, i if has_head else None)
        in_maps = []
        for c in cores:
            b, q = c // 4, c % 4
            m = {"hT": hT[c], "WA": WA, "GA": GA}
            if has_tail:
                m["yT"] = yT[c]
                m["pT"] = np.ascontiguousarray(inp["p"][i - 1, b, q * TOK:(q + 1) * TOK].T)
            if has_head:
                m["pos"] = np.ascontiguousarray(inp["positions"][b:b + 1, q * TOK:(q + 1) * TOK]).astype(np.int32)
            in_maps.append(m)
        res = run_bass_kernel_spmd(nc, in_maps, core_ids=cores)
        hT = [np.asarray(res.results[c]["hTo"]) for c in cores]
        if not has_head:
            out = np.empty((BATCH, SEQ, D), np.float32)
            for c in cores:
                out[c // 4, (c % 4) * TOK:(c % 4 + 1) * TOK] = hT[c].T
            break
        ex = [np.asarray(res.results[c]["exT"]) for c in cores]
        exb = [np.ascontiguousarray(np.concatenate(ex[b * 4:(b + 1) * 4], axis=1)) for b in range(BATCH)]
        ncB = _prog(("B", SEQ), lambda: build_phaseB(SEQ))
        in_maps = []
        for c in cores:
            b, g = c // 4, c % 4
            m = phaseB_weights(inp, i, g)
            m["exT"] = exb[b]
            in_maps.append(m)
        res = run_bass_kernel_spmd(ncB, in_maps, core_ids=cores)
        yB = [np.asarray(res.results[c]["yT"]) for c in cores]
        yT = []
        for c in cores:
            b, q = c // 4, c % 4
            rows = []
            for br in range(3):
                for g in range(4):
                    rows.append(yB[b * 4 + g][br * 256:(br + 1) * 256, q * TOK:(q + 1) * TOK])
            yT.append(np.ascontiguousarray(np.concatenate(rows, axis=0)))
    return out
```

```python
import contextlib
import math
import numpy as np
import ml_dtypes
import concourse.bass as bass
import concourse.mybir as mybir
from concourse.bass_utils import run_bass_kernel_spmd

F32 = mybir.dt.float32
BF16 = mybir.dt.bfloat16
I32 = mybir.dt.int32
AF = mybir.ActivationFunctionType
ALU = mybir.AluOpType
AX = mybir.AxisListType

D = 2048
DFF = 5632
KC = D // 128
JC = DFF // 128
SEQ = 8192
BATCH = 2
DEPTH = 4
EPS = 1e-6
CH = 128
PLE_DIM = 256
OFF_Q, OFF_K, OFF_V, OFF_O, OFF_IG, OFF_FG = 0, 512, 1024, 2048, 3072, 3080
OFF_Z, OFF_XBC, OFF_DT = 3088, 4112, 5648
OFF_CQ, OFF_CKV, OFF_KR, OFF_GATE = 5680, 6192, 6704, 6768
EX_U, EX_Q, EX_KV, EX_KR, EX_COS, EX_SIN, EX_ROWS = 0, 2048, 2560, 3072, 3136, 3200, 3264
TWO_PI = 2.0 * math.pi
ENGS = ("pe", "act", "dve", "pool", "sp")


class Buf:
    __slots__ = ("name", "w", "r", "x")

    def __init__(self, name=None, x=False):
        self.name = name
        self.w = None
        self.r = {}
        self.x = x


class DSem:
    __slots__ = ("key", "count")

    def __init__(self, key):
        self.key = key
        self.count = 0


class Shared:
    def __init__(self, nc):
        self.nc = nc
        self.stack = contextlib.ExitStack()
        self.cnt = {e: 0 for e in ENGS}
        self.waited = {e: {} for e in ENGS}
        self.sems = {}
        for e in ENGS:
            self.sems[e] = self.stack.enter_context(nc.semaphore("s_" + e))
        self.dpool = []
        self.ntens = 0

    def close(self):
        self.stack.close()


class KB:
    def __init__(self, nc, shared=None):
        self.nc = nc
        self.stack = contextlib.ExitStack()
        self.streams = {e: [] for e in ENGS}
        self.own = shared is None
        if shared is None:
            shared = Shared(nc)
        self.shared = shared
        self.cnt = shared.cnt
        self.waited = shared.waited
        self.sems = shared.sems
        self.dsems = []
        self.sb_bytes = 0
        self.n_ops = 0

    def sbuf(self, name, shape, dtype):
        self.shared.ntens += 1
        t = self.stack.enter_context(self.nc.sbuf_tensor("%s_%d" % (name, self.shared.ntens), list(shape), dtype))
        sz = 4 if dtype in (F32, I32) else 2
        self.sb_bytes += int(np.prod(shape[1:])) * sz
        return t

    def psum(self, name, shape, dtype=F32):
        self.shared.ntens += 1
        return self.stack.enter_context(self.nc.psum_tensor("%s_%d" % (name, self.shared.ntens), list(shape), dtype))

    def buf(self, name=None):
        return Buf(name)

    def dsem(self):
        i = len(self.dsems)
        pool = self.shared.dpool
        if i >= len(pool):
            key = "d%d" % i
            self.sems[key] = self.shared.stack.enter_context(self.nc.semaphore("sd_%d" % i))
            pool.append(DSem(key))
        d = pool[i]
        self.dsems.append(d)
        return d

    def _deps(self, reads, writes, accum=None):
        deps = {}
        for b in reads:
            if b.w is not None:
                kk, v = b.w
                if deps.get(kk, 0) < v:
                    deps[kk] = v
        for b in writes:
            if b.w is not None and b.w[0] != accum:
                kk, v = b.w
                if deps.get(kk, 0) < v:
                    deps[kk] = v
            for kk, v in b.r.items():
                if deps.get(kk, 0) < v:
                    deps[kk] = v
        return deps

    def _emit_waits(self, eng, deps):
        wd = self.waited[eng]
        for kk, v in deps.items():
            if kk == eng and eng in ("pe", "sp"):
                continue
            if wd.get(kk, 0) >= v:
                continue
            wd[kk] = v
            self.streams[eng].append(("w", kk, v))

    def _mark(self, tok, reads, writes):
        kk, v = tok
        for b in reads:
            if b.r.get(kk, 0) < v:
                b.r[kk] = v
        for b in writes:
            b.w = tok
            b.r = {}

    def op(self, eng, fn, reads=(), writes=()):
        xr = [b for b in reads if b.x]
        if xr:
            writes = list(writes) + xr
        deps = self._deps(reads, writes)
        self._emit_waits(eng, deps)
        self.cnt[eng] += 1
        tok = (eng, self.cnt[eng])
        self.streams[eng].append(("o", fn, eng, 1))
        self._mark(tok, reads, writes)
        self.n_ops += 1
        return tok

    def dma(self, q, ds, out, in_, reads=(), writes=(), accum=False):
        deps = self._deps(reads, writes, accum=(ds.key if accum else None))
        self._emit_waits(q, deps)
        ds.count += 16
        tok = (ds.key, ds.count)

        def fn(e, out=out, in_=in_):
            return e.dma_start(out=out, in_=in_)

        self.streams[q].append(("o", fn, ds.key, 16))
        self._mark(tok, reads, writes)
        return tok

    def gbegin(self, q, ds):
        if ds.count:
            self._emit_waits(q, {ds.key: ds.count})
        self._g = (ds, [], [])

    def gdma(self, q, out, in_, reads=(), writes=()):
        ds, gw, gr = self._g
        self.dma(q, ds, out, in_, reads=reads, writes=writes, accum=True)
        gw.extend(writes)
        gr.extend(reads)

    def group(self, q, ds, items):
        self.gbegin(q, ds)
        for (out, in_, rd, wr) in items:
            self.gdma(q, out, in_, reads=rd, writes=wr)
        self.gend()

    def gend(self):
        ds, gw, gr = self._g
        for b in gw:
            b.w = (ds.key, ds.count)
        for b in gr:
            b.r[ds.key] = ds.count
        self._g = None

    def barrier(self):
        for eng in ENGS:
            deps = {}
            for d in self.dsems:
                if d.count:
                    deps[d.key] = d.count
            for e in ENGS:
                if e != eng and self.cnt[e]:
                    deps[e] = self.cnt[e]
            self._emit_waits(eng, deps)

    def wait_all(self, eng="sp"):
        deps = {}
        for d in self.dsems:
            if d.count:
                deps[d.key] = d.count
        for e in ENGS:
            if e != eng and self.cnt[e]:
                deps[e] = self.cnt[e]
        self._emit_waits(eng, deps)

    def emit(self):
        nc = self.nc
        sems = self.sems
        streams = self.streams
        with nc.Block() as block:
            def run(e, items):
                for it in items:
                    if it[0] == "w":
                        e.wait_ge(sems[it[1]], it[2])
                    else:
                        ins = it[1](e)
                        ins.then_inc(sems[it[2]], it[3])

            @block.tensor
            def _(e):
                run(e, streams["pe"])

            @block.scalar
            def _(e):
                run(e, streams["act"])

            @block.vector
            def _(e):
                run(e, streams["dve"])

            @block.gpsimd
            def _(e):
                run(e, streams["pool"])

            @block.sync
            def _(e):
                run(e, streams["sp"])
        self.stack.close()
        if self.own:
            self.shared.close()


class Rot:
    def __init__(self, items):
        self.items = items
        self.i = 0

    def next(self):
        it = self.items[self.i]
        self.i = (self.i + 1) % len(self.items)
        return it


def mk_rot(k, name, n, shape, dtype):
    return Rot([(k.sbuf(name, shape, dtype), k.buf()) for _ in range(n)])


def chunks_lhsT(W, cw=128):
    K, N = W.shape
    kci = K // 128
    nch = N // cw
    return W.reshape(kci, 128, nch, cw).transpose(2, 1, 0, 3).reshape(nch, 128, kci * cw)


def rhs_layout(W):
    K, N = W.shape
    return np.ascontiguousarray(W.reshape(K // 128, 128, N).transpose(1, 0, 2))


def gain_cols(g):
    return np.ascontiguousarray(g.reshape(-1, 128).T)


def tri_consts():
    s = np.arange(128)[:, None]
    t = np.arange(128)[None, :]
    c = np.zeros((8, 128, 128), np.float32)
    c[0] = (s <= t)
    c[1] = (s >= t)
    c[2] = (s > t)
    c[3] = (s < t)
    c[4] = np.eye(128)
    c[5] = 1.0
    return c


class Common:
    def __init__(self, k):
        self.k = k
        self.ps = [(k.psum("ps%d" % i, [128, 512]), Buf(x=True)) for i in range(7)]
        self.psrot = Rot(self.ps)
        pst = k.psum("pst", [128, 1024], BF16)
        pstb = Buf(x=True)
        self.pst = Rot([(pst[:, 0:512], pstb), (pst[:, 512:1024], pstb)])
        self.d_misc = k.dsem()

    def nps(self):
        return self.psrot.next()


def emit_sincos(k, cm, posi, posib, invf_ap, sgn_ap, n, cos_out, sin_out, outb, tmp, scale=1.0):
    (ang, angb), (y, yb), (yf, yfb), (t, tb) = tmp[:4]
    (yi, yib) = tmp[4]
    k.op("dve", lambda e: e.tensor_copy(out=ang[0:64, 0:n], in_=posi), reads=[posib], writes=[angb])
    k.op("dve", lambda e: e.tensor_scalar(out=ang[0:64, 0:n], in0=ang[0:64, 0:n], scalar1=invf_ap, scalar2=None, op0=ALU.mult),
         reads=[angb], writes=[angb])
    for which in (0, 1):
        off = 0.25 if which == 0 else 0.0
        k.op("dve", lambda e, off=off: e.tensor_scalar(out=y[0:64, 0:n], in0=ang[0:64, 0:n], scalar1=float(1.0 / TWO_PI), scalar2=off,
                                                        op0=ALU.mult, op1=ALU.add), reads=[angb], writes=[yb])
        k.op("dve", lambda e: e.tensor_copy(out=yi[0:64, 0:n], in_=y[0:64, 0:n]), reads=[yb], writes=[yib])
        k.op("dve", lambda e: e.tensor_copy(out=yf[0:64, 0:n], in_=yi[0:64, 0:n]), reads=[yib], writes=[yfb])
        k.op("dve", lambda e: e.tensor_tensor(out=y[0:64, 0:n], in0=y[0:64, 0:n], in1=yf[0:64, 0:n], op=ALU.subtract),
             reads=[yb, yfb], writes=[yb])
        k.op("dve", lambda e: e.tensor_scalar(out=yf[0:64, 0:n], in0=y[0:64, 0:n], scalar1=0.5, scalar2=None, op0=ALU.is_gt),
             reads=[yb], writes=[yfb])
        k.op("dve", lambda e: e.tensor_tensor(out=y[0:64, 0:n], in0=y[0:64, 0:n], in1=yf[0:64, 0:n], op=ALU.subtract),
             reads=[yb, yfb], writes=[yb])
        k.op("dve", lambda e: e.tensor_scalar(out=yf[0:64, 0:n], in0=y[0:64, 0:n], scalar1=-0.5, scalar2=None, op0=ALU.is_lt),
             reads=[yb], writes=[yfb])
        k.op("dve", lambda e: e.tensor_tensor(out=y[0:64, 0:n], in0=y[0:64, 0:n], in1=yf[0:64, 0:n], op=ALU.add),
             reads=[yb, yfb], writes=[yb])
        k.op("act", lambda e: e.activation(out=t[0:64, 0:n], in_=y[0:64, 0:n], func=AF.Sin, scale=float(TWO_PI * (1 - 1e-6))),
             reads=[yb], writes=[tb])
        if which == 0:
            k.op("dve", lambda e: e.tensor_scalar(out=cos_out, in0=t[0:64, 0:n], scalar1=float(scale), scalar2=None, op0=ALU.mult),
                 reads=[tb], writes=[outb])
        else:
            k.op("dve", lambda e: e.tensor_scalar(out=sin_out, in0=t[0:64, 0:n], scalar1=sgn_ap, scalar2=float(scale),
                                                   op0=ALU.mult, op1=ALU.mult), reads=[tb], writes=[outb])


def phaseA_layout(has_tail, has_head):
    idx = {}
    n = 0

    def add(name, cnt):
        nonlocal n
        idx[name] = n
        n += cnt
    if has_tail:
        add("wo", 8)
        add("wz", 8)
        add("wgate", 48)
        add("wbr", 48)
        add("wout", 16)
        add("f2_w13", 2 * JC)
        add("f2_w2", JC)
        add("pgate", 16)
        add("pproj", 16)
    if has_head:
        add("f1_w13", 2 * JC)
        add("f1_w2", JC)
        add("wcq", 4)
        add("wckv", 4)
        add("wkr", 2)
    g = {}
    m = 0
    for name, cnt in (("mixp", 16), ("ssmn", 8), ("ffn2", 16), ("ple", 16), ("final", 16), ("ffn1", 16), ("mix", 16),
                      ("qn", 4), ("kvn", 4), ("invf", 1), ("sgn", 1)):
        g[name] = m
        m += cnt
    return idx, n, g, m


def build_phaseA(has_tail, has_head, is_last, TOK, nc=None, ext=None, shared=None):
    PASS = min(1024, TOK)
    NPASS = TOK // PASS
    NT = PASS // 512
    idx, NCH, gi, NG = phaseA_layout(has_tail, has_head)
    fused = nc is not None
    if not fused:
        nc = bass.Bass("TRN2", target_bir_lowering=False)
        hT_d = nc.dram_tensor("hT", [D, TOK], F32, kind="ExternalInput").ap()
        WA = nc.dram_tensor("WA", [NCH, 128, 2048], F32, kind="ExternalInput").ap()
        GA = nc.dram_tensor("GA", [128, NG], F32, kind="ExternalInput").ap()
        if has_tail:
            yT_d = nc.dram_tensor("yT", [3072, TOK], BF16, kind="ExternalInput").ap()
            pT_d = nc.dram_tensor("pT", [PLE_DIM, TOK], F32, kind="ExternalInput").ap()
        if has_head:
            pos_d = nc.dram_tensor("pos", [1, TOK], I32, kind="ExternalInput").ap()
            exT_d = nc.dram_tensor("exT", [EX_ROWS, TOK], BF16, kind="ExternalOutput").ap()
        hO_d = nc.dram_tensor("hTo", [D, TOK], F32, kind="ExternalOutput").ap()
    else:
        hT_d, WA, GA, hO_d = ext["hT"], ext["WA"], ext["GA"], ext["hTo"]
        yT_d, pT_d, pos_d, exT_d = ext.get("yT"), ext.get("pT"), ext.get("pos"), ext.get("exT")

    k = KB(nc, shared)
    cm = Common(k)
    nps = cm.nps
    hT = k.sbuf("hT", [128, KC, PASS], F32)
    hb = [[k.buf() for _ in range(NT)] for _ in range(KC)]
    xn = k.sbuf("xn", [128, KC, PASS], BF16)
    xb = [k.buf() for _ in range(NT)]
    ga = k.sbuf("ga", [128, NG], F32)
    gab = k.buf()
    ones = k.sbuf("ones", [128, 128], BF16)
    onesb = k.buf()
    NSLOT = 5
    wsl = [(k.sbuf("wsl", [128, 2048], BF16), k.buf(), k.dsem()) for _ in range(NSLOT)]
    wst = [0]
    sqr = mk_rot(k, "sq", 3, [128, 512], BF16)
    rstd = k.sbuf("rstd", [128, 512], F32)
    rstdb = k.buf()
    f32r = mk_rot(k, "f32t", 3, [128, 512], F32)
    U = k.sbuf("U", [128, 24 * PASS], BF16)
    ybuf = U[:, 0:8 * PASS].rearrange("p (a b) -> p a b", b=PASS)
    merged = U[:, 8 * PASS:24 * PASS].rearrange("p (a b) -> p a b", b=PASS)
    gT = [U[:, i * 4 * PASS:(i + 1) * 4 * PASS].rearrange("p (a b) -> p a b", b=PASS) for i in range(2)]
    pTs = U[:, 8 * PASS:10 * PASS].rearrange("p (a b) -> p a b", b=PASS)
    lat = U[:, 0:8 * PASS].bitcast(F32).rearrange("p (a b) -> p a b", b=PASS)
    latn = U[:, 8 * PASS:12 * PASS].rearrange("p (a b) -> p a b", b=PASS)
    krt = U[:, 12 * PASS:13 * PASS]
    cst = U[:, 13 * PASS:15 * PASS]
    Ub = k.buf()
    ybb = [[k.buf() for _ in range(NT)] for _ in range(8)]
    mgb = [[k.buf() for _ in range(NT)] for _ in range(KC)]
    gTb = [[[k.buf() for _ in range(NT)] for _ in range(4)] for _ in range(2)]
    latb = [[k.buf() for _ in range(NT)] for _ in range(4)]
    latnb = [k.buf() for _ in range(NT)]
    d_in = k.dsem()
    d_out = k.dsem()
    d_y = k.dsem()
    d_p = k.dsem()
    if has_head:
        cosf = k.sbuf("cosf", [64, PASS], F32)
        sinf = k.sbuf("sinf", [64, PASS], F32)
        csb = k.buf()
        posi = k.sbuf("posi", [64, PASS], I32)
        posib = k.buf()
        sct = [(k.sbuf("sct", [64, 512], F32), k.buf()) for _ in range(4)] + [(k.sbuf("sci", [64, 512], I32), k.buf())]

    def sl(t):
        return slice(t * 512, (t + 1) * 512)

    def wload(ci, n):
        i = wst[0]
        wst[0] = (i + 1) % NSLOT
        t, b, ds = wsl[i]
        k.dma("pool", ds, t[:, 0:n], WA[ci, :, 0:n], writes=[b])
        return t, b

    k.group("sp", d_in, [(ga[:], GA, [], [gab])])
    k.op("dve", lambda e: e.memset(ones[:], 1.0), writes=[onesb])

    def rmsnorm(src, srcb, gcol, nkc, dmodel, out, outb_fn, writes_extra=()):
        for t in range(NT):
            pt, pb = nps()
            for kc in range(nkc):
                sq, sqb = sqr.next()
                k.op("act", lambda e, sq=sq, kc=kc, t=t: e.activation(out=sq[:], in_=src[:, kc, sl(t)], func=AF.Square),
                     reads=[srcb[kc][t]], writes=[sqb])
                k.op("pe", lambda e, sq=sq, kc=kc, pt=pt: e.matmul(pt[:], lhsT=ones[:], rhs=sq[:], start=(kc == 0), stop=(kc == nkc - 1)),
                     reads=[onesb, sqb], writes=[pb])
            k.op("act", lambda e, pt=pt: e.activation(out=rstd[:], in_=pt[:], func=AF.Sqrt, scale=1.0 / dmodel, bias=EPS),
                 reads=[pb], writes=[rstdb])
            k.op("dve", lambda e: e.reciprocal(out=rstd[:], in_=rstd[:]), reads=[rstdb], writes=[rstdb])
            for kc in range(nkc):
                k.op("dve", lambda e, kc=kc, t=t: e.scalar_tensor_tensor(
                    out=out[:, kc, sl(t)], in0=src[:, kc, sl(t)], scalar=ga[:, gcol + kc:gcol + kc + 1], in1=rstd[:],
                    op0=ALU.mult, op1=ALU.mult), reads=[srcb[kc][t], gab, rstdb], writes=[outb_fn(kc, t)])

    def lin(ci, ncols_per_kc, nkc, rhs, rhsb_fn, evac, mrows=128):
        w, wb = wload(ci, nkc * ncols_per_kc)
        for t in range(NT):
            pt, pb = nps()

            def mm(e, w=w, pt=pt, t=t):
                r = None
                for kc in range(nkc):
                    r = e.matmul(pt[0:mrows, :], lhsT=w[:, kc * ncols_per_kc:(kc + 1) * ncols_per_kc], rhs=rhs[:, kc, sl(t)],
                                 start=(kc == 0), stop=(kc == nkc - 1))
                return r
            k.op("pe", mm, reads=[wb] + rhsb_fn(t), writes=[pb])
            evac(t, pt, pb)

    def xn_bufs(t):
        return [xb[t]]

    def ffn(gcol, c13, c2):
        rmsnorm(hT, hb, gcol, KC, D, xn, lambda kc, t: xb[t])
        J = 4
        for g in range(JC // J):
            gi_ = g % 2
            for jl in range(J):
                j = g * J + jl
                hold = {}

                def ev_a(t, pt, pb, hold=hold):
                    sa, sab = f32r.next()
                    k.op("act", lambda e, sa=sa, pt=pt: e.activation(out=sa[:], in_=pt[:], func=AF.Silu), reads=[pb], writes=[sab])
                    hold[t] = (sa, sab)
                lin(c13 + 2 * j, 128, KC, xn, xn_bufs, ev_a)

                def ev_b(t, pt, pb, hold=hold, gi_=gi_, jl=jl):
                    sa, sab = hold[t]
                    k.op("dve", lambda e, sa=sa, pt=pt, t=t: e.tensor_tensor(out=gT[gi_][:, jl, sl(t)], in0=sa[:], in1=pt[:], op=ALU.mult),
                         reads=[sab, pb], writes=[gTb[gi_][jl][t]])
                lin(c13 + 2 * j + 1, 128, KC, xn, xn_bufs, ev_b)
            w2s = [wload(c2 + g * J + jl, D) for jl in range(J)]
            for m in range(KC):
                for t in range(NT):
                    po, pob = nps()

                    def mm3(e, po=po, m=m, t=t, gi_=gi_, w2s=w2s):
                        r = None
                        for jl in range(J):
                            r = e.matmul(po[:], lhsT=w2s[jl][0][:, m * 128:(m + 1) * 128], rhs=gT[gi_][:, jl, sl(t)],
                                         start=(jl == 0), stop=(jl == J - 1))
                        return r
                    k.op("pe", mm3, reads=[w[1] for w in w2s] + [gTb[gi_][jl][t] for jl in range(J)], writes=[pob])
                    k.op("dve", lambda e, po=po, m=m, t=t: e.scalar_tensor_tensor(
                        out=hT[:, m, sl(t)], in0=po[:], scalar=0.5, in1=hT[:, m, sl(t)], op0=ALU.mult, op1=ALU.add),
                        reads=[pob, hb[m][t]], writes=[hb[m][t]])

    def region_switch(new_bufs):
        k.barrier()

    for ps_i in range(NPASS):
        t0 = ps_i * PASS
        k.group("sp", d_in, [(hT[:, kc, sl(t)], hT_d[kc * 128:(kc + 1) * 128, t0 + t * 512:t0 + (t + 1) * 512], [], [hb[kc][t]])
                             for kc in range(KC) for t in range(NT)])
        if has_tail:
            region_switch(None)
            rmsnorm(hT, hb, gi["mixp"], KC, D, xn, lambda kc, t: xb[t])
            for br in range(3):
                k.group("sp", d_y, [(ybuf[:, c, sl(t)], yT_d[br * 1024 + c * 128:br * 1024 + (c + 1) * 128, t0 + t * 512:t0 + (t + 1) * 512], [], [ybb[c][t]])
                                    for c in range(8) for t in range(NT)])
                if br < 2:
                    for c in range(8):
                        def ev_g(t, pt, pb, c=c, br=br):
                            sa, sab = f32r.next()
                            k.op("act", lambda e, sa=sa, pt=pt: e.activation(out=sa[:], in_=pt[:], func=(AF.Sigmoid if br == 0 else AF.Silu)),
                                 reads=[pb], writes=[sab])
                            k.op("dve", lambda e, sa=sa, c=c, t=t: e.tensor_tensor(out=ybuf[:, c, sl(t)], in0=ybuf[:, c, sl(t)], in1=sa[:], op=ALU.mult),
                                 reads=[sab, ybb[c][t]], writes=[ybb[c][t]])
                        lin(idx["wo" if br == 0 else "wz"] + c, 128, KC, xn, xn_bufs, ev_g)
                if br == 1:
                    rmsnorm(ybuf, ybb, gi["ssmn"], 8, 1024, ybuf, lambda kc, t: ybb[kc][t])
                for m in range(KC):
                    hold = {}

                    def ev_gate(t, pt, pb, hold=hold):
                        sa, sab = f32r.next()
                        k.op("act", lambda e, sa=sa, pt=pt: e.activation(out=sa[:], in_=pt[:], func=AF.Sigmoid), reads=[pb], writes=[sab])
                        hold[t] = (sa, sab)
                    lin(idx["wgate"] + br * 16 + m, 128, KC, xn, xn_bufs, ev_gate)

                    def ev_br(t, pt, pb, hold=hold, m=m, br=br):
                        sa, sab = hold[t]
                        if br == 0:
                            k.op("dve", lambda e, sa=sa, pt=pt, t=t: e.tensor_tensor(out=merged[:, m, sl(t)], in0=sa[:], in1=pt[:], op=ALU.mult),
                                 reads=[sab, pb], writes=[mgb[m][t]])
                        else:
                            k.op("dve", lambda e, sa=sa, pt=pt: e.tensor_tensor(out=sa[:], in0=sa[:], in1=pt[:], op=ALU.mult),
                                 reads=[sab, pb], writes=[sab])
                            k.op("dve", lambda e, sa=sa, t=t: e.tensor_tensor(out=merged[:, m, sl(t)], in0=merged[:, m, sl(t)], in1=sa[:], op=ALU.add),
                                 reads=[sab, mgb[m][t]], writes=[mgb[m][t]])
                    lin(idx["wbr"] + br * 16 + m, 128, 8, ybuf, lambda t: [ybb[c][t] for c in range(8)], ev_br)
            for m in range(KC):
                def ev_out(t, pt, pb, m=m):
                    k.op("dve", lambda e, pt=pt, t=t: e.tensor_tensor(out=hT[:, m, sl(t)], in0=hT[:, m, sl(t)], in1=pt[:], op=ALU.add),
                         reads=[pb, hb[m][t]], writes=[hb[m][t]])
                lin(idx["wout"] + m, 128, KC, merged, lambda t: [mgb[c][t] for c in range(KC)], ev_out)
            region_switch(None)
            ffn(gi["ffn2"], idx["f2_w13"], idx["f2_w2"])
            region_switch(None)
            rmsnorm(hT, hb, gi["ple"], KC, D, xn, lambda kc, t: xb[t])
            pTb = [k.buf() for _ in range(NT)]
            k.group("pool", d_p, [(pTs[:, c, sl(t)], pT_d[c * 128:(c + 1) * 128, t0 + t * 512:t0 + (t + 1) * 512], [], [pTb[t]])
                                  for c in range(2) for t in range(NT)])
            for m in range(KC):
                hold = {}

                def ev_pg(t, pt, pb, hold=hold):
                    sa, sab = f32r.next()
                    k.op("act", lambda e, sa=sa, pt=pt: e.activation(out=sa[:], in_=pt[:], func=AF.Sigmoid), reads=[pb], writes=[sab])
                    hold[t] = (sa, sab)
                lin(idx["pgate"] + m, 128, KC, xn, xn_bufs, ev_pg)

                def ev_pp(t, pt, pb, hold=hold, m=m):
                    sa, sab = hold[t]
                    k.op("dve", lambda e, sa=sa, pt=pt: e.tensor_tensor(out=sa[:], in0=sa[:], in1=pt[:], op=ALU.mult), reads=[sab, pb], writes=[sab])
                    k.op("dve", lambda e, sa=sa, t=t: e.tensor_tensor(out=hT[:, m, sl(t)], in0=hT[:, m, sl(t)], in1=sa[:], op=ALU.add),
                         reads=[sab, hb[m][t]], writes=[hb[m][t]])
                lin(idx["pproj"] + m, 128, 2, pTs, lambda t: [pTb[t]], ev_pp)
        if is_last:
            rmsnorm(hT, hb, gi["final"], KC, D, hT, lambda kc, t: hb[kc][t])
        if has_head:
            region_switch(None)
            ffn(gi["ffn1"], idx["f1_w13"], idx["f1_w2"])
            region_switch(None)
            rmsnorm(hT, hb, gi["mix"], KC, D, xn, lambda kc, t: xb[t])
            k.group("sp", d_out, [(exT_d[EX_U + kc * 128:EX_U + (kc + 1) * 128, t0 + t * 512:t0 + (t + 1) * 512], xn[:, kc, sl(t)], [xb[t]], [])
                                  for kc in range(KC) for t in range(NT)])
            k.group("sp", d_in, [(posi[:], pos_d[:, t0:t0 + PASS].partition_broadcast(64), [], [posib])])
            for t in range(NT):
                emit_sincos(k, cm, posi[:, sl(t)], posib, ga[0:64, gi["invf"]:gi["invf"] + 1], ga[0:64, gi["sgn"]:gi["sgn"] + 1], 512,
                            cosf[:, sl(t)], sinf[:, sl(t)], csb, sct)
            k.op("act", lambda e: e.copy(out=cst[0:64, 0:PASS], in_=cosf[:]), reads=[csb], writes=[Ub])
            k.op("act", lambda e: e.copy(out=cst[0:64, PASS:2 * PASS], in_=sinf[:]), reads=[csb], writes=[Ub])
            k.group("sp", d_out, [(exT_d[EX_COS:EX_COS + 64, t0:t0 + PASS], cst[0:64, 0:PASS], [Ub], []),
                                  (exT_d[EX_SIN:EX_SIN + 64, t0:t0 + PASS], cst[0:64, PASS:2 * PASS], [Ub], [])])
            for which, wname, gname, exoff in ((0, "wcq", "qn", EX_Q), (1, "wckv", "kvn", EX_KV)):
                for c in range(4):
                    def ev_lat(t, pt, pb, c=c):
                        k.op("act", lambda e, pt=pt, t=t: e.copy(out=lat[:, c, sl(t)], in_=pt[:]), reads=[pb], writes=[latb[c][t]])
                    lin(idx[wname] + c, 128, KC, xn, xn_bufs, ev_lat)
                rmsnorm(lat, latb, gi[gname], 4, 512, latn, lambda kc, t: latnb[t])
                k.group("sp", d_out, [(exT_d[exoff + c * 128:exoff + (c + 1) * 128, t0 + t * 512:t0 + (t + 1) * 512], latn[:, c, sl(t)], [latnb[t]], [])
                                      for c in range(4) for t in range(NT)])
            hold = {}

            def ev_kr(t, pt, pb, hold=hold):
                sa, sab = f32r.next()
                k.op("dve", lambda e, sa=sa, pt=pt, t=t: e.tensor_tensor(out=sa[0:64, :], in0=pt[0:64, :], in1=cosf[:, sl(t)], op=ALU.mult),
                     reads=[pb, csb], writes=[sab])
                hold[t] = (sa, sab)
            lin(idx["wkr"], 64, KC, xn, xn_bufs, ev_kr, mrows=64)
            krb = [k.buf() for _ in range(NT)]

            def ev_krs(t, pt, pb, hold=hold):
                sa, sab = hold[t]
                sb_, sbb = f32r.next()
                k.op("dve", lambda e, sb_=sb_, pt=pt, t=t: e.tensor_tensor(out=sb_[0:64, :], in0=pt[0:64, :], in1=sinf[:, sl(t)], op=ALU.mult),
                     reads=[pb, csb], writes=[sbb])
                k.op("dve", lambda e, sa=sa, sb_=sb_, t=t: e.tensor_tensor(out=krt[0:64, sl(t)], in0=sa[0:64, :], in1=sb_[0:64, :], op=ALU.add),
                     reads=[sab, sbb], writes=[krb[t]])
                k.group("sp", d_out, [(exT_d[EX_KR:EX_KR + 64, t0 + t * 512:t0 + (t + 1) * 512], krt[0:64, sl(t)], [krb[t]], [])])
            lin(idx["wkr"] + 1, 64, KC, xn, xn_bufs, ev_krs, mrows=64)
        k.group("sp", d_out, [(hO_d[kc * 128:(kc + 1) * 128, t0 + t * 512:t0 + (t + 1) * 512], hT[:, kc, sl(t)], [hb[kc][t]], [])
                              for kc in range(KC) for t in range(NT)])
        k.barrier()
    if fused:
        k.barrier()
    else:
        k.wait_all("sp")
    k.emit()
    return nc


def phaseA_weights(inp, L_tail, L_head):
    has_tail = L_tail is not None
    has_head = L_head is not None
    idx, NCH, gi, NG = phaseA_layout(has_tail, has_head)
    WA = np.zeros((NCH, 128, 2048), np.float32)
    GA = np.zeros((128, NG), np.float32)

    def put(name, arr):
        n = arr.shape[0]
        WA[idx[name]:idx[name] + n, :, :arr.shape[2]] = arr

    def ffn_pack(prefix, w13, w2):
        a = chunks_lhsT(w13[:, :DFF])
        b = chunks_lhsT(w13[:, DFF:])
        ab = np.empty((2 * JC, 128, 2048), np.float32)
        ab[0::2] = a
        ab[1::2] = b
        put(prefix + "_w13", ab)
        put(prefix + "_w2", w2.reshape(JC, 128, D))
    if has_tail:
        L = L_tail
        w_in = inp["w_in"][L]
        put("wo", chunks_lhsT(w_in[:, OFF_O:OFF_O + 1024]))
        put("wz", chunks_lhsT(w_in[:, OFF_Z:OFF_Z + 1024]))
        put("wgate", chunks_lhsT(w_in[:, OFF_GATE:OFF_GATE + 3 * D]))
        wbr = np.concatenate([chunks_lhsT(inp["w_branch"][L, i]) for i in range(3)], axis=0)
        put("wbr", wbr)
        put("wout", chunks_lhsT(inp["w_out"][L]))
        ffn_pack("f2", inp["ffn2_w13"][L], inp["ffn2_w2"][L])
        put("pgate", chunks_lhsT(inp["w_ple_gate"][L]))
        put("pproj", chunks_lhsT(inp["w_ple_proj"][L]))
        GA[:, gi["mixp"]:gi["mixp"] + 16] = gain_cols(inp["mix_norm"][L])
        GA[:, gi["ssmn"]:gi["ssmn"] + 8] = gain_cols(inp["ssm_norm"][L])
        GA[:, gi["ffn2"]:gi["ffn2"] + 16] = gain_cols(inp["ffn2_norm"][L])
        GA[:, gi["ple"]:gi["ple"] + 16] = gain_cols(inp["ple_norm"][L])
    GA[:, gi["final"]:gi["final"] + 16] = gain_cols(inp["final_norm"])
    if has_head:
        L = L_head
        w_in = inp["w_in"][L]
        ffn_pack("f1", inp["ffn1_w13"][L], inp["ffn1_w2"][L])
        put("wcq", chunks_lhsT(w_in[:, OFF_CQ:OFF_CQ + 512]))
        put("wckv", chunks_lhsT(w_in[:, OFF_CKV:OFF_CKV + 512]))
        wkr = w_in[:, OFF_KR:OFF_KR + 64]
        wkr_sw = np.concatenate([wkr[:, 32:], wkr[:, :32]], axis=1)
        put("wkr", np.concatenate([chunks_lhsT(wkr, 64), chunks_lhsT(wkr_sw, 64)], axis=0))
        GA[:, gi["ffn1"]:gi["ffn1"] + 16] = gain_cols(inp["ffn1_norm"][L])
        GA[:, gi["mix"]:gi["mix"] + 16] = gain_cols(inp["mix_norm"][L])
        GA[:, gi["qn"]:gi["qn"] + 4] = gain_cols(inp["mla_q_norm"][L])
        GA[:, gi["kvn"]:gi["kvn"] + 4] = gain_cols(inp["mla_kv_norm"][L])
    invf = (10000.0 ** (-np.arange(32, dtype=np.float32) / 32)).astype(np.float32)
    GA[0:64, gi["invf"]] = np.concatenate([invf, invf])
    GA[0:64, gi["sgn"]] = np.concatenate([-np.ones(32, np.float32), np.ones(32, np.float32)])
    return WA, GA


PB_BIG, PB_BFG, PB_MN, PB_DTB, PB_ALOG, PB_DSK, NPB = 0, 2, 4, 260, 268, 276, 532
ARENA_BYTES = 188 * 1024
B_PARTS = (1, 2, 3)
DBG = 99


class Arena:
    def __init__(self, k, nbytes):
        self.t = k.sbuf("arena", [128, nbytes // 2], BF16)
        self.nbytes = nbytes
        self.off = 0
        self.limit = None
        self.resume = 0

    def reset(self):
        self.off = 0
        self.limit = None

    def reclaim(self, lo, hi):
        self.resume = self.off
        self.limit = hi
        self.off = lo

    def alloc(self, free, dtype):
        sz = 4 if dtype in (F32, I32) else 2
        n = int(np.prod(free))
        self.off = (self.off + 63) // 64 * 64
        nb = n * sz
        if self.limit is not None and self.off + nb > self.limit:
            self.off = (self.resume + 63) // 64 * 64
            self.limit = None
        start = self.off // 2
        assert self.off + nb <= self.nbytes, ("arena overflow", self.off, nb)
        v = self.t[:, start:start + nb // 2]
        if dtype == F32:
            v = v.bitcast(F32)
        elif dtype == I32:
            v = v.bitcast(I32)
        self.off += nb
        if len(free) == 2:
            v = v.rearrange("p (a b) -> p a b", b=free[1])
        elif len(free) == 3:
            v = v.rearrange("p (a b c) -> p a b c", b=free[1], c=free[2])
        return v


class ARot:
    def __init__(self, k, ar, n, free, dtype):
        self.items = [(ar.alloc(free, dtype), k.buf()) for _ in range(n)]
        self.i = 0

    def next(self):
        it = self.items[self.i]
        self.i = (self.i + 1) % len(self.items)
        return it


def build_phaseB(SEQ_, nc=None, ext=None, shared=None):
    NCk = SEQ_ // 128
    fused = nc is not None
    if not fused:
        nc = bass.Bass("TRN2", target_bir_lowering=False)
        exT_d = nc.dram_tensor("exT", [EX_ROWS, SEQ_], BF16, kind="ExternalInput").ap()
        WB1 = nc.dram_tensor("WB1", [2, 128, 2048], F32, kind="ExternalInput").ap()
        WB1t = nc.dram_tensor("WB1t", [128, 16, 388], F32, kind="ExternalInput").ap()
        PB = nc.dram_tensor("PB", [1, NPB], F32, kind="ExternalInput").ap()
        CW = nc.dram_tensor("CW", [128, 4, 6], F32, kind="ExternalInput").ap()
        WB2 = nc.dram_tensor("WB2", [4, 128, 2048], F32, kind="ExternalInput").ap()
        WB2t = nc.dram_tensor("WB2t", [128, 16, 8], F32, kind="ExternalInput").ap()
        WB3 = nc.dram_tensor("WB3", [2, 5, 128, 512], F32, kind="ExternalInput").ap()
        CONST = nc.dram_tensor("CONST", [8, 128, 128], F32, kind="ExternalInput").ap()
        yT_d = nc.dram_tensor("yT", [768, SEQ_], BF16, kind="ExternalOutput").ap()
        yparts = [yT_d[0:256], yT_d[256:512], yT_d[512:768]]
    else:
        exT_d, WB1, WB1t, PB, CW, WB2, WB2t, WB3, CONST = (ext[n] for n in ("exT", "WB1", "WB1t", "PB", "CW", "WB2", "WB2t", "WB3", "CONST"))
        yparts = ext["yparts"]

    k = KB(nc, shared)
    cm = Common(k)
    nps = cm.nps
    ar = Arena(k, ARENA_BYTES)
    cst = k.sbuf("cst", [128, 8, 128], F32)
    cstb = k.buf()
    TRI = [cst[:, 0, :], cst[:, 1, :]]
    STR = [cst[:, 2, :], cst[:, 3, :]]
    ID32 = cst[:, 4, :]
    ONE32 = cst[:, 5, :]
    MS = [cst[:, 6, :], cst[:, 7, :]]
    idh = k.sbuf("idh", [128, 128], BF16)
    oneh = k.sbuf("oneh", [128, 128], BF16)
    pbt = k.sbuf("pbt", [128, NPB], F32)
    pbb = k.buf()
    d_c = k.dsem()
    d_u = [k.dsem(), k.dsem()]
    d_w = k.dsem()
    d_o = k.dsem()
    k.group("sp", d_c, [(cst[:, 0:6, :], CONST[0:6].rearrange("c p n -> p c n"), [], [cstb]),
                        (pbt[:], PB.partition_broadcast(128), [], [pbb])])
    k.op("act", lambda e: e.copy(out=idh[:], in_=ID32), reads=[cstb], writes=[cstb])
    k.op("act", lambda e: e.copy(out=oneh[:], in_=ONE32), reads=[cstb], writes=[cstb])
    sc = 128.0 ** -0.5
    k.op("dve", lambda e: e.tensor_scalar(out=MS[0], in0=TRI[0], scalar1=sc, scalar2=None, op0=ALU.mult), reads=[cstb], writes=[cstb])
    k.op("dve", lambda e: e.tensor_scalar(out=MS[1], in0=TRI[1], scalar1=sc, scalar2=None, op0=ALU.mult), reads=[cstb], writes=[cstb])

    def pcol(c):
        return pbt[:, c:c + 1]

    def ew(eng, out, in0, in1, op, reads, writes):
        k.op(eng, lambda e: e.tensor_tensor(out=out, in0=in0, in1=in1, op=op), reads=reads, writes=writes)

    def emit_B1():
        ar.reset()
        TB = 256
        w1q = ar.alloc((2048,), BF16)
        w1k = ar.alloc((2048,), BF16)
        w1t = ar.alloc((16, 388), BF16)
        wb_ = k.buf()
        urot = ARot(k, ar, 2, (16, TB), BF16)
        proj_end = ar.off
        qT = ar.alloc((SEQ_,), BF16)
        kT = ar.alloc((SEQ_,), BF16)
        ktok = ar.alloc((NCk, 128), BF16)
        vext = ar.alloc((NCk, 260), BF16)
        gates = ar.alloc((NCk, 4), F32)
        hbk = ar.alloc((NCk, 256), BF16)
        qb = [k.buf() for _ in range(NCk)]
        kb_ = [k.buf() for _ in range(NCk)]
        ktb = [k.buf() for _ in range(NCk)]
        vb = [k.buf() for _ in range(NCk)]
        gtb = k.buf()
        hbb = [k.buf() for _ in range(NCk)]
        if DBG == 11:
            k.barrier()
            return
        k.group("pool", d_w, [(w1q, WB1[0], [], [wb_]), (w1k, WB1[1], [], [wb_]), (w1t, WB1t, [], [wb_])])
        if DBG == 12:
            k.barrier()
            return
        k.op("pool", lambda e: e.memset(vext, 1.0), writes=vb)
        if DBG == 13:
            k.barrier()
            return
        cpt = TB // 128
        for tt in range(SEQ_ // TB):
            if DBG == 14 and tt == 1:
                k.barrier()
                return
            u, ub = urot.next()
            k.group("sp", d_u[tt % 2], [(u, exT_d[EX_U:EX_U + 2048, tt * TB:(tt + 1) * TB].rearrange("(kc p) t -> p kc t", p=128), [], [ub])])
            cl = list(range(tt * cpt, (tt + 1) * cpt))
            for (w, dst, dbs, eng) in ((w1q, qT, qb, "act"), (w1k, kT, kb_, "dve")):
                pt, pb = nps()

                def mm(e, w=w, pt=pt, u=u):
                    r = None
                    for kc in range(16):
                        r = e.matmul(pt[:, 0:TB], lhsT=w[:, kc * 128:(kc + 1) * 128], rhs=u[:, kc, :], start=(kc == 0), stop=(kc == 15))
                    return r
                k.op("pe", mm, reads=[wb_, ub], writes=[pb])
                if eng == "act":
                    k.op("act", lambda e, pt=pt, dst=dst, tt=tt: e.copy(out=dst[:, tt * TB:(tt + 1) * TB], in_=pt[:, 0:TB]), reads=[pb], writes=[dbs[c] for c in cl])
                else:
                    k.op("dve", lambda e, pt=pt, dst=dst, tt=tt: e.tensor_copy(out=dst[:, tt * TB:(tt + 1) * TB], in_=pt[:, 0:TB]), reads=[pb], writes=[dbs[c] for c in cl])
            if DBG == 15:
                k.barrier()
                return
            for cc in range(cpt):
                c = tt * cpt + cc
                pt, pb = nps()

                def mm2(e, pt=pt, u=u, cc=cc):
                    r = None
                    for kc in range(16):
                        r = e.matmul(pt[:, 0:388], lhsT=u[:, kc, cc * 128:(cc + 1) * 128], rhs=w1t[:, kc, :], start=(kc == 0), stop=(kc == 15))
                    return r
                k.op("pe", mm2, reads=[wb_, ub], writes=[pb])
                if DBG == 16:
                    k.barrier()
                    return
                k.op("act", lambda e, pt=pt, c=c: e.copy(out=ktok[:, c, :], in_=pt[:, 0:128]), reads=[pb], writes=[ktb[c]])
                if DBG == 17:
                    k.barrier()
                    return
                k.op("dve", lambda e, pt=pt, c=c: e.tensor_copy(out=vext[:, c, 0:256], in_=pt[:, 128:384]), reads=[pb], writes=[vb[c]])
                if DBG == 18:
                    k.barrier()
                    return
                k.op("act", lambda e, pt=pt, c=c: e.copy(out=gates[:, c, :], in_=pt[:, 384:388]), reads=[pb], writes=[gtb])
                if DBG == 19:
                    k.barrier()
                    return

        if DBG == 1:
            k.barrier()
            return
        k.barrier()
        ar.reclaim(0, proj_end)

        def T():
            return ar.alloc((NCk,), F32)
        G = []

        def gmath(d):
            gd = {}
            tb_ = k.buf()
            tA, lf, li, bsb, gsum, a, cmx, mloc, Mc, einter, ea, es, aprev, aloc, floor_, tmp = [T() for _ in range(16)]
            marr = ar.alloc((NCk + 1,), F32)
            sm = ar.alloc((8,), F32)
            dg = ar.alloc((NCk,), F32)
            apad = ar.alloc((128,), F32)
            k.op("dve", lambda e: e.memset(apad, 0.0), writes=[tb_])
            k.op("dve", lambda e: e.memset(sm, 0.0), writes=[tb_])
            R_ = [tb_]
            k.op("dve", lambda e: e.tensor_scalar(out=sm[:, 0:1], in0=pcol(PB_BFG + d), scalar1=-1.0, scalar2=None, op0=ALU.mult), reads=[pbb], writes=R_)
            k.op("act", lambda e: e.activation(out=tA, in_=gates[:, :, 2 + d], func=AF.Exp, scale=-1.0, bias=sm[:, 0:1]), reads=[gtb] + R_, writes=R_)
            k.op("act", lambda e: e.activation(out=tA, in_=tA, func=AF.Ln, bias=1.0), reads=R_, writes=R_)
            k.op("dve", lambda e: e.tensor_scalar(out=lf, in0=tA, scalar1=-1.0, scalar2=None, op0=ALU.mult), reads=R_, writes=R_)
            k.op("dve", lambda e: e.tensor_scalar(out=li, in0=gates[:, :, d], scalar1=pcol(PB_BIG + d), scalar2=None, op0=ALU.add), reads=[gtb, pbb] + R_, writes=R_)
            p1, p1b = nps()
            p2, p2b = nps()
            k.op("pe", lambda e: e.matmul(p1[:, 0:NCk], lhsT=TRI[d], rhs=lf, start=True, stop=True), reads=[cstb] + R_, writes=[p1b])
            k.op("pe", lambda e: e.matmul(p2[:, 0:NCk], lhsT=ONE32, rhs=lf, start=True, stop=True), reads=[cstb] + R_, writes=[p2b])
            k.op("act", lambda e: e.copy(out=bsb, in_=p1[:, 0:NCk]), reads=[p1b], writes=R_)
            k.op("dve", lambda e: e.tensor_copy(out=gsum, in_=p2[:, 0:NCk]), reads=[p2b], writes=R_)
            ew("dve", a, li, bsb, ALU.subtract, R_, R_)
            k.op("dve", lambda e: e.tensor_copy(out=apad[:, 0:NCk], in_=a), reads=R_, writes=R_)
            p3, p3b = nps()
            k.op("pe", lambda e: e.matmul(p3[:, 0:128], lhsT=apad, rhs=ID32, start=True, stop=True), reads=[cstb] + R_, writes=[p3b])
            k.op("dve", lambda e: e.reduce_max(out=sm[0:NCk, 1:2], in_=p3[0:NCk, 0:128], axis=AX.X), reads=[p3b], writes=R_)
            k.op("dve", lambda e: e.tensor_scalar(out=dg, in0=ID32[:, 0:NCk], scalar1=sm[:, 1:2], scalar2=None, op0=ALU.mult), reads=[cstb] + R_, writes=R_)
            p4, p4b = nps()
            k.op("pe", lambda e: e.matmul(p4[:, 0:NCk], lhsT=ONE32, rhs=dg, start=True, stop=True), reads=[cstb] + R_, writes=[p4b])
            k.op("act", lambda e: e.copy(out=cmx, in_=p4[:, 0:NCk]), reads=[p4b], writes=R_)
            ew("dve", mloc, gsum, cmx, ALU.add, R_, R_)
            k.op("dve", lambda e: e.memset(marr, 0.0), writes=R_)
            order = list(range(NCk)) if d == 0 else list(reversed(range(NCk)))
            for c in order:
                src = c if d == 0 else c + 1
                dst = c + 1 if d == 0 else c
                k.op("dve", lambda e, c=c, src=src, dst=dst: e.scalar_tensor_tensor(
                    out=marr[:, dst:dst + 1], in0=marr[:, src:src + 1], scalar=gsum[:, c:c + 1], in1=mloc[:, c:c + 1], op0=ALU.add, op1=ALU.max),
                    reads=R_, writes=R_)
            m0 = marr[:, 0:NCk] if d == 0 else marr[:, 1:NCk + 1]
            mn = marr[:, 1:NCk + 1] if d == 0 else marr[:, 0:NCk]
            ew("dve", Mc, m0, cmx, ALU.max, R_, R_)
            ew("dve", tmp, m0, Mc, ALU.subtract, R_, R_)
            k.op("act", lambda e: e.activation(out=einter, in_=tmp, func=AF.Exp), reads=R_, writes=R_)
            ew("dve", tmp, a, Mc, ALU.subtract, R_, R_)
            k.op("act", lambda e: e.activation(out=ea, in_=tmp, func=AF.Exp), reads=R_, writes=R_)
            ew("dve", tmp, a, cmx, ALU.subtract, R_, R_)
            k.op("act", lambda e: e.activation(out=es, in_=tmp, func=AF.Exp), reads=R_, writes=R_)
            k.op("dve", lambda e: e.tensor_scalar(out=es, in0=es, scalar1=sc, scalar2=None, op0=ALU.mult), reads=R_, writes=R_)
            ew("dve", tmp, gsum, m0, ALU.add, R_, R_)
            ew("dve", tmp, tmp, mn, ALU.subtract, R_, R_)
            k.op("act", lambda e: e.activation(out=aprev, in_=tmp, func=AF.Exp), reads=R_, writes=R_)
            ew("dve", tmp, mloc, mn, ALU.subtract, R_, R_)
            k.op("act", lambda e: e.activation(out=aloc, in_=tmp, func=AF.Exp), reads=R_, writes=R_)
            ew("dve", tmp, bsb, Mc, ALU.add, R_, R_)
            k.op("act", lambda e: e.activation(out=floor_, in_=tmp, func=AF.Exp, scale=-1.0), reads=R_, writes=R_)
            gd.update(b=tb_, einter=einter, ea=ea, es=es, aprev=aprev, aloc=aloc, floor=floor_)
            G.append(gd)
        gmath(0)
        gmath(1)

        if DBG == 2:
            k.barrier()
            return
        PTr = ARot(k, ar, 4, (128,), BF16)
        vscr = ARot(k, ar, 4, (260,), BF16)
        kscr = ARot(k, ar, 4, (128,), BF16)
        t1r = ARot(k, ar, 3, (260,), F32)
        t2r = ARot(k, ar, 3, (260,), F32)
        t3r = ARot(k, ar, 2, (260,), F32)
        smr = ARot(k, ar, 4, (8,), F32)
        hsr = ARot(k, ar, 2, (256,), F32)
        jkr = ARot(k, ar, 1, (256,), F32)
        ybr = ARot(k, ar, 2, (256,), BF16)
        visited = set()

        def mk_scan1(d):
            gd = G[d]
            gb = [gd["b"]]
            C32 = ar.alloc((260,), F32)
            C32b = k.buf()
            Cbfr = ARot(k, ar, 2, (260,), BF16)
            ystr = ARot(k, ar, 2, (2, 512), BF16)
            st = {}
            k.op("pool", lambda e: e.memset(C32, 0.0), writes=[C32b])
            cbf0, cbfb0 = Cbfr.next()
            k.op("pool", lambda e: e.memset(cbf0, 0.0), writes=[cbfb0])
            st["cbf"] = (cbf0, cbfb0)
            st["yst"] = None

            def step(c):
                cbf, cbfb = st["cbf"]
                yst = st["yst"]
                first = c not in visited
                visited.add(c)
                cs = slice(c * 128, (c + 1) * 128)
                pS, pSb = nps()
                k.op("pe", lambda e, pS=pS, cs=cs: e.matmul(pS[:, 0:128], lhsT=kT[:, cs], rhs=qT[:, cs], start=True, stop=True),
                     reads=[kb_[c], qb[c]], writes=[pSb])
                PT, PTb = PTr.next()
                k.op("dve", lambda e, PT=PT, pS=pS: e.tensor_tensor(out=PT, in0=pS[:, 0:128], in1=MS[d], op=ALU.mult), reads=[pSb, cstb], writes=[PTb])
                vsc, vscb = vscr.next()
                k.op("pool", lambda e, vsc=vsc, c=c: e.tensor_scalar(out=vsc[:, 0:257], in0=vext[:, c, 0:257], scalar1=gd["ea"][:, c:c + 1], scalar2=None, op0=ALU.mult),
                     reads=[vb[c]] + gb, writes=[vscb])
                pI, pIb = nps()
                k.op("pe", lambda e, pI=pI, PT=PT, vsc=vsc: e.matmul(pI[:, 0:257], lhsT=PT, rhs=vsc[:, 0:257], start=True, stop=True),
                     reads=[PTb, vscb], writes=[pIb])
                pX, pXb = nps()
                k.op("pe", lambda e, pX=pX, cs=cs, cbf=cbf: e.matmul(pX[:, 0:257], lhsT=qT[:, cs], rhs=cbf[:, 0:257], start=True, stop=True),
                     reads=[qb[c], cbfb], writes=[pXb])
                t1, t1b = t1r.next()
                k.op("act", lambda e, t1=t1, pX=pX, c=c: e.activation(out=t1[:, 0:257], in_=pX[:, 0:257], func=AF.Identity, scale=gd["einter"][:, c:c + 1]),
                     reads=[pXb] + gb, writes=[t1b])
                t2, t2b = t2r.next()
                k.op("dve", lambda e, t2=t2, t1=t1, pI=pI: e.tensor_tensor(out=t2[:, 0:257], in0=t1[:, 0:257], in1=pI[:, 0:257], op=ALU.add),
                     reads=[t1b, pIb], writes=[t2b])
                sm, smb = smr.next()
                k.op("dve", lambda e, sm=sm, t2=t2: e.scalar_tensor_tensor(out=sm[:, 0:1], in0=t2[:, 256:257], scalar=-1.0, in1=t2[:, 256:257], op0=ALU.mult, op1=ALU.max),
                     reads=[t2b], writes=[smb])
                k.op("dve", lambda e, sm=sm, c=c: e.tensor_tensor(out=sm[:, 1:2], in0=sm[:, 0:1], in1=gd["floor"][:, c:c + 1], op=ALU.max), reads=[smb] + gb, writes=[smb])
                k.op("dve", lambda e, sm=sm: e.reciprocal(out=sm[:, 2:3], in_=sm[:, 1:2]), reads=[smb], writes=[smb])
                if first:
                    k.op("dve", lambda e, sm=sm, t2=t2, c=c: e.tensor_scalar(out=hbk[:, c, :], in0=t2[:, 0:256], scalar1=sm[:, 2:3], scalar2=None, op0=ALU.mult),
                         reads=[smb, t2b], writes=[hbb[c]])
                else:
                    hs, hsb = hsr.next()
                    k.op("dve", lambda e, hs=hs, sm=sm, t2=t2, c=c: e.scalar_tensor_tensor(out=hs, in0=t2[:, 0:256], scalar=sm[:, 2:3], in1=hbk[:, c, :], op0=ALU.mult, op1=ALU.add),
                         reads=[smb, t2b, hbb[c]], writes=[hsb])
                    jk, jkb = jkr.next()
                    k.op("act", lambda e, jk=jk, hs=hs, sm=sm: e.activation(out=jk, in_=hs, func=AF.Square, accum_out=sm[:, 3:4]), reads=[hsb], writes=[jkb, smb])
                    k.op("dve", lambda e, sm=sm: e.tensor_scalar(out=sm[:, 4:5], in0=sm[:, 3:4], scalar1=1.0 / 256, scalar2=EPS, op0=ALU.mult, op1=ALU.add), reads=[smb], writes=[smb])
                    k.op("act", lambda e, sm=sm: e.activation(out=sm[:, 4:5], in_=sm[:, 4:5], func=AF.Sqrt), reads=[smb], writes=[smb])
                    k.op("dve", lambda e, sm=sm: e.reciprocal(out=sm[:, 5:6], in_=sm[:, 4:5]), reads=[smb], writes=[smb])
                    yb_, ybb_ = ybr.next()
                    k.op("dve", lambda e, yb_=yb_, hs=hs, sm=sm: e.scalar_tensor_tensor(out=yb_, in0=hs, scalar=sm[:, 5:6], in1=pbt[:, PB_MN:PB_MN + 256], op0=ALU.mult, op1=ALU.mult),
                         reads=[hsb, smb, pbb], writes=[ybb_])
                    pT_, pTb_ = cm.pst.next()
                    pbf = pT_[:, 0:256]

                    def tr(e, pbf=pbf, yb_=yb_):
                        e.transpose(pbf[:, 0:128], yb_[:, 0:128], idh[:])
                        return e.transpose(pbf[:, 128:256], yb_[:, 128:256], idh[:])
                    k.op("pe", tr, reads=[ybb_, cstb], writes=[pTb_])
                    if c % 4 == (0 if d == 0 else 3):
                        yst = ystr.next()
                        st["yst"] = yst
                    ys, ysb = yst
                    cc = c % 4
                    k.op("act", lambda e, ys=ys, pbf=pbf, cc=cc: e.copy(out=ys[:, :, cc * 128:(cc + 1) * 128], in_=pbf.rearrange("p (a b) -> p a b", b=128)),
                         reads=[pTb_], writes=[ysb])
                    if c % 4 == (3 if d == 0 else 0):
                        c0 = c - c % 4
                        k.group("sp", d_o, [(yparts[0][:, c0 * 128:(c0 + 4) * 128].rearrange("(a p) t -> p a t", p=128), ys, [ysb], [])])
                ksc, kscb = kscr.next()
                k.op("pool", lambda e, ksc=ksc, c=c: e.tensor_scalar(out=ksc, in0=ktok[:, c, :], scalar1=gd["es"][:, c:c + 1], scalar2=None, op0=ALU.mult),
                     reads=[ktb[c]] + gb, writes=[kscb])
                pC, pCb = nps()
                k.op("pe", lambda e, pC=pC, ksc=ksc, c=c: e.matmul(pC[:, 0:257], lhsT=ksc, rhs=vext[:, c, 0:257], start=True, stop=True),
                     reads=[kscb, vb[c]], writes=[pCb])
                t3, t3b = t3r.next()
                k.op("act", lambda e, t3=t3, pC=pC, c=c: e.activation(out=t3[:, 0:257], in_=pC[:, 0:257], func=AF.Identity, scale=gd["aloc"][:, c:c + 1]),
                     reads=[pCb] + gb, writes=[t3b])
                k.op("dve", lambda e, t3=t3, c=c: e.scalar_tensor_tensor(out=C32[:, 0:257], in0=C32[:, 0:257], scalar=gd["aprev"][:, c:c + 1], in1=t3[:, 0:257], op0=ALU.mult, op1=ALU.add),
                     reads=[t3b, C32b] + gb, writes=[C32b])
                cbf, cbfb = Cbfr.next()
                k.op("act", lambda e, cbf=cbf: e.copy(out=cbf[:, 0:257], in_=C32[:, 0:257]), reads=[C32b], writes=[cbfb])
                st["cbf"] = (cbf, cbfb)
            return step
        sb1 = mk_scan1(1)
        sf1 = mk_scan1(0)
        for i_ in range(NCk):
            sb1(NCk - 1 - i_)
            sf1(i_)
        k.barrier()

    def emit_B2():
        ar.reset()
        TB = 512
        NTB = SEQ_ // TB
        CT = ar.alloc((SEQ_,), BF16)
        BT = ar.alloc((SEQ_,), BF16)
        Btok = ar.alloc((NCk, 128), BF16)
        xtok = ar.alloc((NCk, 256), BF16)
        dtr = ar.alloc((2, NCk, 4), F32)
        dA = ar.alloc((2, NCk, 4), F32)
        eacs = ar.alloc((2, NCk, 4), F32)
        dte = ar.alloc((2, NCk, 4), F32)
        cdk = ar.alloc((2, NCk, 4), F32)
        cwt = ar.alloc((4, 6), F32)
        ctb = [k.buf() for _ in range(NCk)]
        btb = [k.buf() for _ in range(NCk)]
        bkb = [k.buf() for _ in range(NCk)]
        xkb = [k.buf() for _ in range(NCk)]
        dtb_ = k.buf()
        gmb = k.buf()
        cwb = k.buf()
        mark = ar.off
        w2 = [ar.alloc((2048,), BF16) for _ in range(4)]
        w2t = ar.alloc((16, 8), BF16)
        wb_ = k.buf()
        urot = ARot(k, ar, 2, (16, TB), BF16)
        ring = [ar.alloc((4, TB + 4), F32) for _ in range(3)]
        ringb = [[k.buf() for _ in range(4)] for _ in range(3)]
        accr = ARot(k, ar, 2, (TB,), F32)
        cvr = ARot(k, ar, 3, (TB,), BF16)
        k.group("pool", d_w, [(w2[i], WB2[i], [], [wb_]) for i in range(4)] + [(w2t, WB2t, [], [wb_])])
        k.group("sp", d_c, [(cwt, CW, [], [cwb])])
        for i in range(3):
            k.op("pool", lambda e, i=i: e.memset(ring[i], 0.0), writes=ringb[i])

        def conv_tile(ti):
            r = ring[ti % 3]
            rb = ringb[ti % 3]
            for ch in range(4):
                acc, accb = accr.next()
                k.op("dve", lambda e, acc=acc, r=r, ch=ch: e.tensor_scalar(out=acc, in0=r[:, ch, 0:TB], scalar1=cwt[:, ch, 0:1], scalar2=None, op0=ALU.mult),
                     reads=[rb[ch], cwb], writes=[accb])
                for kk in range(1, 5):
                    eng = "dve"
                    k.op(eng, lambda e, acc=acc, r=r, ch=ch, kk=kk: e.scalar_tensor_tensor(out=acc, in0=r[:, ch, kk:kk + TB], scalar=cwt[:, ch, kk:kk + 1], in1=acc, op0=ALU.mult, op1=ALU.add),
                         reads=[rb[ch], cwb, accb], writes=[accb])
                cv, cvb = cvr.next()
                cl = list(range(ti * 4, ti * 4 + 4))
                if ch == 3:
                    k.op("act", lambda e, acc=acc, ti=ti: e.activation(out=CT[:, ti * TB:(ti + 1) * TB], in_=acc, func=AF.Silu, bias=cwt[:, 3, 5:6]),
                         reads=[accb, cwb], writes=[ctb[c] for c in cl])
                    continue
                if ch == 2:
                    k.op("act", lambda e, acc=acc, ti=ti: e.activation(out=BT[:, ti * TB:(ti + 1) * TB], in_=acc, func=AF.Silu, bias=cwt[:, 2, 5:6]),
                         reads=[accb, cwb], writes=[btb[c] for c in cl])
                    src = BT[:, ti * TB:(ti + 1) * TB]
                    srcb = [btb[c] for c in cl]
                else:
                    k.op("act", lambda e, acc=acc, cv=cv, ch=ch: e.activation(out=cv, in_=acc, func=AF.Silu, bias=cwt[:, ch, 5:6]),
                         reads=[accb, cwb], writes=[cvb])
                    src = cv
                    srcb = [cvb]
                pT_, pTb_ = cm.pst.next()
                pbf = pT_[:, 0:512]

                def tr(e, pbf=pbf, src=src):
                    r_ = None
                    for cc in range(4):
                        r_ = e.transpose(pbf[:, cc * 128:(cc + 1) * 128], src[:, cc * 128:(cc + 1) * 128], idh[:])
                    return r_
                k.op("pe", tr, reads=srcb + [cstb], writes=[pTb_])
                if ch == 2:
                    k.op("act", lambda e, pbf=pbf, ti=ti: e.copy(out=Btok[:, ti * 4:ti * 4 + 4, :], in_=pbf.rearrange("p (a b) -> p a b", b=128)),
                         reads=[pTb_], writes=[bkb[c] for c in cl])
                else:
                    k.op("act", lambda e, pbf=pbf, ti=ti, ch=ch: e.copy(out=xtok[:, ti * 4:ti * 4 + 4, ch * 128:(ch + 1) * 128], in_=pbf.rearrange("p (a b) -> p a b", b=128)),
                         reads=[pTb_], writes=[xkb[c] for c in cl])

        for tt in range(NTB):
            u, ub = urot.next()
            k.group("sp", d_u[tt % 2], [(u, exT_d[EX_U:EX_U + 2048, tt * TB:(tt + 1) * TB].rearrange("(kc p) t -> p kc t", p=128), [], [ub])])
            r = ring[tt % 3]
            rb = ringb[tt % 3]
            if tt + 1 < NTB or True:
                pass
            for ch in range(4):
                pt, pb = nps()

                def mm(e, pt=pt, u=u, ch=ch):
                    r_ = None
                    for kc in range(16):
                        r_ = e.matmul(pt[:, 0:TB], lhsT=w2[ch][:, kc * 128:(kc + 1) * 128], rhs=u[:, kc, :], start=(kc == 0), stop=(kc == 15))
                    return r_
                k.op("pe", mm, reads=[wb_, ub], writes=[pb])
                k.op("act", lambda e, pt=pt, r=r, ch=ch: e.copy(out=r[:, ch, 2:TB + 2], in_=pt[:, 0:TB]), reads=[pb], writes=[rb[ch]])
                if tt > 0:
                    rp = ring[(tt - 1) % 3]
                    k.op("dve", lambda e, pt=pt, rp=rp, ch=ch: e.tensor_copy(out=rp[:, ch, TB + 2:TB + 4], in_=pt[:, 0:2]), reads=[pb], writes=[ringb[(tt - 1) % 3][ch]])
                if tt + 1 < NTB:
                    rn = ring[(tt + 1) % 3]
                    k.op("dve", lambda e, pt=pt, rn=rn, ch=ch: e.tensor_copy(out=rn[:, ch, 0:2], in_=pt[:, TB - 2:TB]), reads=[pb], writes=[ringb[(tt + 1) % 3][ch]])
                else:
                    k.op("dve", lambda e, r=r, ch=ch: e.memset(r[:, ch, TB + 2:TB + 4], 0.0), writes=[rb[ch]])
            if tt == 0:
                for ch in range(4):
                    k.op("dve", lambda e, r=r, ch=ch: e.memset(r[:, ch, 0:2], 0.0), writes=[rb[ch]])
            for cc in range(4):
                c = tt * 4 + cc
                pt, pb = nps()

                def mm2(e, pt=pt, u=u, cc=cc):
                    r_ = None
                    for kc in range(16):
                        r_ = e.matmul(pt[:, 0:8], lhsT=u[:, kc, cc * 128:(cc + 1) * 128], rhs=w2t[:, kc, :], start=(kc == 0), stop=(kc == 15))
                    return r_
                k.op("pe", mm2, reads=[wb_, ub], writes=[pb])
                k.op("dve", lambda e, pt=pt, c=c: e.tensor_copy(out=dtr[:, :, c, :], in_=pt[:, 0:8].rearrange("p (a b) -> p a b", b=4)), reads=[pb], writes=[dtb_])
            if tt > 0:
                conv_tile(tt - 1)
        conv_tile(NTB - 1)

        t_a = ar.alloc((2, NCk, 4), F32)
        t_b = ar.alloc((2, NCk, 4), F32)
        acs = ar.alloc((2, NCk, 4), F32)
        tot = ar.alloc((2, NCk, 4), F32)
        Abc = ar.alloc((8,), F32)
        R_ = [gmb]
        n2 = NCk * 4
        dtbias = pbt[:, PB_DTB:PB_DTB + 8].rearrange("p (a b) -> p a b", b=4).unsqueeze(2).broadcast_to([128, 2, NCk, 4])
        k.op("dve", lambda e: e.tensor_tensor(out=dtr, in0=dtr, in1=dtbias, op=ALU.add), reads=[dtb_, pbb], writes=[dtb_])
        k.op("dve", lambda e: e.scalar_tensor_tensor(out=t_a, in0=dtr, scalar=-1.0, in1=dtr, op0=ALU.mult, op1=ALU.max), reads=[dtb_], writes=R_)
        k.op("act", lambda e: e.activation(out=t_a, in_=t_a, func=AF.Exp, scale=-1.0), reads=R_, writes=R_)
        k.op("act", lambda e: e.activation(out=t_a, in_=t_a, func=AF.Ln, bias=1.0), reads=R_, writes=R_)
        k.op("dve", lambda e: e.scalar_tensor_tensor(out=dtr, in0=dtr, scalar=0.0, in1=t_a, op0=ALU.max, op1=ALU.add), reads=R_ + [dtb_], writes=[dtb_])
        k.op("act", lambda e: e.activation(out=Abc, in_=pbt[:, PB_ALOG:PB_ALOG + 8], func=AF.Exp), reads=[pbb], writes=R_)
        k.op("dve", lambda e: e.tensor_scalar(out=Abc, in0=Abc, scalar1=-1.0, scalar2=None, op0=ALU.mult), reads=R_, writes=R_)
        Abb = Abc.rearrange("p (a b) -> p a b", b=4).unsqueeze(2).broadcast_to([128, 2, NCk, 4])
        k.op("dve", lambda e: e.tensor_tensor(out=dA, in0=dtr, in1=Abb, op=ALU.mult), reads=R_ + [dtb_], writes=R_)
        for d in range(2):
            p1, p1b = nps()
            p2, p2b = nps()
            k.op("pe", lambda e, p1=p1, d=d: e.matmul(p1[:, 0:n2], lhsT=TRI[d], rhs=dA[:, d].rearrange("p a b -> p (a b)"), start=True, stop=True), reads=[cstb] + R_, writes=[p1b])
            k.op("pe", lambda e, p2=p2, d=d: e.matmul(p2[:, 0:n2], lhsT=ONE32, rhs=dA[:, d].rearrange("p a b -> p (a b)"), start=True, stop=True), reads=[cstb] + R_, writes=[p2b])
            k.op("act", lambda e, p1=p1, d=d: e.copy(out=acs[:, d].rearrange("p a b -> p (a b)"), in_=p1[:, 0:n2]), reads=[p1b], writes=R_)
            k.op("dve", lambda e, p2=p2, d=d: e.tensor_copy(out=tot[:, d].rearrange("p a b -> p (a b)"), in_=p2[:, 0:n2]), reads=[p2b], writes=R_)
        k.op("act", lambda e: e.activation(out=eacs, in_=acs, func=AF.Exp), reads=R_, writes=R_)
        k.op("act", lambda e: e.activation(out=cdk, in_=tot, func=AF.Exp), reads=R_, writes=R_)
        ew("dve", t_b, tot, acs, ALU.subtract, R_, R_)
        k.op("act", lambda e: e.activation(out=t_b, in_=t_b, func=AF.Exp), reads=R_, writes=R_)
        ew("dve", dte, t_b, dtr, ALU.mult, R_ + [dtb_], R_)
        k.barrier()

        ar.off = mark
        yacc = ar.alloc((NCk, 256), F32)
        yab = [k.buf() for _ in range(NCk)]
        Gr = ARot(k, ar, 4, (128,), F32)
        Lr = ARot(k, ar, 4, (128,), F32)
        decr = ARot(k, ar, 4, (128,), F32)
        Wr = ARot(k, ar, 6, (128,), BF16)
        xdr = ARot(k, ar, 4, (256,), BF16)
        xddr = ARot(k, ar, 4, (256,), BF16)
        tmpr = ARot(k, ar, 4, (256,), F32)
        ytr = ARot(k, ar, 2, (256,), F32)
        ybr = ARot(k, ar, 2, (256,), BF16)
        visited2 = set()

        def mk_scan2(d):
            S32 = ar.alloc((256,), F32)
            S32b = k.buf()
            Sbfr = ARot(k, ar, 2, (256,), BF16)
            ystr = ARot(k, ar, 2, (2, 512), BF16)
            st = {}
            k.op("pool", lambda e: e.memset(S32, 0.0), writes=[S32b])
            sbf0, sbfb0 = Sbfr.next()
            k.op("pool", lambda e: e.memset(sbf0, 0.0), writes=[sbfb0])
            st["sbf"] = (sbf0, sbfb0)
            st["yst"] = None

            def step(c):
                sbf, sbfb = st["sbf"]
                yst = st["yst"]
                first = c not in visited2
                visited2.add(c)
                cs = slice(c * 128, (c + 1) * 128)
                pCB, pCBb = nps()
                k.op("pe", lambda e, pCB=pCB, cs=cs: e.matmul(pCB[:, 0:128], lhsT=BT[:, cs], rhs=CT[:, cs], start=True, stop=True), reads=[btb[c], ctb[c]], writes=[pCBb])
                Gm, Gb = Gr.next()
                k.op("dve", lambda e, Gm=Gm, pCB=pCB: e.tensor_tensor(out=Gm, in0=pCB[:, 0:128], in1=TRI[d], op=ALU.mult), reads=[pCBb, cstb], writes=[Gb])
                xd, xdb = xdr.next()
                k.op("pool", lambda e, xd=xd, c=c: e.tensor_tensor(out=xd.rearrange("p (a b) -> p a b", b=64), in0=xtok[:, c, :].rearrange("p (a b) -> p a b", b=64),
                                                                  in1=dtr[:, d, c, :].unsqueeze(2).broadcast_to([128, 4, 64]), op=ALU.mult),
                     reads=[xkb[c], dtb_], writes=[xdb])
                pY, pYb = nps()
                for hl in range(4):
                    Lm, Lb = Lr.next()
                    k.op("pool", lambda e, Lm=Lm, c=c, hl=hl: e.tensor_scalar(out=Lm, in0=STR[d], scalar1=dA[:, d, c, hl:hl + 1], scalar2=None, op0=ALU.mult),
                         reads=[cstb, gmb], writes=[Lb])
                    pSg, pSgb = nps()
                    k.op("pe", lambda e, pSg=pSg, Lm=Lm: e.matmul(pSg[:, 0:128], lhsT=Lm, rhs=TRI[d], start=True, stop=True), reads=[Lb, cstb], writes=[pSgb])
                    dec, decb = decr.next()
                    k.op("act", lambda e, dec=dec, pSg=pSg: e.activation(out=dec, in_=pSg[:, 0:128], func=AF.Exp), reads=[pSgb], writes=[decb])
                    Wm, Wb = Wr.next()
                    k.op("dve", lambda e, Wm=Wm, Gm=Gm, dec=dec: e.tensor_tensor(out=Wm, in0=Gm, in1=dec, op=ALU.mult), reads=[Gb, decb], writes=[Wb])
                    k.op("pe", lambda e, pY=pY, Wm=Wm, xd=xd, hl=hl: e.matmul(pY[:, hl * 64:(hl + 1) * 64], lhsT=Wm, rhs=xd[:, hl * 64:(hl + 1) * 64], start=True, stop=True),
                         reads=[Wb, xdb], writes=[pYb])
                pO, pOb = nps()
                k.op("pe", lambda e, pO=pO, cs=cs, sbf=sbf: e.matmul(pO[:, 0:256], lhsT=CT[:, cs], rhs=sbf, start=True, stop=True), reads=[ctb[c], sbfb], writes=[pOb])
                tmp, tmpb = tmpr.next()
                k.op("dve", lambda e, tmp=tmp, pO=pO, c=c: e.tensor_tensor(out=tmp.rearrange("p (a b) -> p a b", b=64), in0=pO[:, 0:256].rearrange("p (a b) -> p a b", b=64),
                                                                        in1=eacs[:, d, c, :].unsqueeze(2).broadcast_to([128, 4, 64]), op=ALU.mult),
                     reads=[pOb, gmb], writes=[tmpb])
                if first:
                    k.op("dve", lambda e, tmp=tmp, pY=pY, c=c: e.tensor_tensor(out=yacc[:, c, :], in0=tmp, in1=pY[:, 0:256], op=ALU.add), reads=[tmpb, pYb], writes=[yab[c]])
                else:
                    yt, ytb = ytr.next()
                    k.op("dve", lambda e, yt=yt, tmp=tmp, pY=pY: e.tensor_tensor(out=yt, in0=tmp, in1=pY[:, 0:256], op=ALU.add), reads=[tmpb, pYb], writes=[ytb])
                    k.op("pool", lambda e, yt=yt, c=c: e.tensor_tensor(out=yt, in0=yt, in1=yacc[:, c, :], op=ALU.add), reads=[ytb, yab[c]], writes=[ytb])
                    k.op("pool", lambda e, tmp=tmp, c=c: e.tensor_tensor(out=tmp, in0=xtok[:, c, :], in1=pbt[:, PB_DSK:PB_DSK + 256], op=ALU.mult), reads=[xkb[c], pbb, tmpb], writes=[tmpb])
                    yb_, ybb_ = ybr.next()
                    k.op("dve", lambda e, yb_=yb_, yt=yt, tmp=tmp: e.tensor_tensor(out=yb_, in0=yt, in1=tmp, op=ALU.add), reads=[ytb, tmpb], writes=[ybb_])
                    pT_, pTb_ = cm.pst.next()
                    pbf = pT_[:, 0:256]

                    def tr2(e, pbf=pbf, yb_=yb_):
                        e.transpose(pbf[:, 0:128], yb_[:, 0:128], idh[:])
                        return e.transpose(pbf[:, 128:256], yb_[:, 128:256], idh[:])
                    k.op("pe", tr2, reads=[ybb_, cstb], writes=[pTb_])
                    if c % 4 == (0 if d == 0 else 3):
                        yst = ystr.next()
                        st["yst"] = yst
                    ys, ysb = yst
                    cc = c % 4
                    k.op("act", lambda e, ys=ys, pbf=pbf, cc=cc: e.copy(out=ys[:, :, cc * 128:(cc + 1) * 128], in_=pbf.rearrange("p (a b) -> p a b", b=128)),
                         reads=[pTb_], writes=[ysb])
                    if c % 4 == (3 if d == 0 else 0):
                        c0 = c - c % 4
                        k.group("sp", d_o, [(yparts[1][:, c0 * 128:(c0 + 4) * 128].rearrange("(a p) t -> p a t", p=128), ys, [ysb], [])])
                xdd, xddb = xddr.next()
                k.op("pool", lambda e, xdd=xdd, c=c: e.tensor_tensor(out=xdd.rearrange("p (a b) -> p a b", b=64), in0=xtok[:, c, :].rearrange("p (a b) -> p a b", b=64),
                                                                   in1=dte[:, d, c, :].unsqueeze(2).broadcast_to([128, 4, 64]), op=ALU.mult),
                     reads=[xkb[c], gmb], writes=[xddb])
                pSt, pStb = nps()
                k.op("pe", lambda e, pSt=pSt, c=c, xdd=xdd: e.matmul(pSt[:, 0:256], lhsT=Btok[:, c, :], rhs=xdd, start=True, stop=True), reads=[bkb[c], xddb], writes=[pStb])
                k.op("dve", lambda e, c=c: e.tensor_tensor(out=S32.rearrange("p (a b) -> p a b", b=64), in0=S32.rearrange("p (a b) -> p a b", b=64),
                                                           in1=cdk[:, d, c, :].unsqueeze(2).broadcast_to([128, 4, 64]), op=ALU.mult), reads=[S32b, gmb], writes=[S32b])
                k.op("dve", lambda e, pSt=pSt: e.tensor_tensor(out=S32, in0=S32, in1=pSt[:, 0:256], op=ALU.add), reads=[S32b, pStb], writes=[S32b])
                sbf, sbfb = Sbfr.next()
                k.op("act", lambda e, sbf=sbf: e.copy(out=sbf, in_=S32), reads=[S32b], writes=[sbfb])
                st["sbf"] = (sbf, sbfb)
            return step
        sb2 = mk_scan2(1)
        sf2 = mk_scan2(0)
        for i_ in range(NCk):
            sb2(NCk - 1 - i_)
            sf2(i_)
        k.barrier()

    def emit_B3():
        TB = 512
        NTB = SEQ_ // TB
        qscale = 192.0 ** -0.5

        def head(hh):
            ar.reset()
            wq = [ar.alloc((512,), BF16) for _ in range(5)]
            wb_ = k.buf()
            qN = ar.alloc((SEQ_,), BF16)
            qR = ar.alloc((SEQ_,), BF16)
            kN = ar.alloc((SEQ_,), BF16)
            kR = ar.alloc((SEQ_,), BF16)
            V = ar.alloc((NCk, 128), BF16)
            qNb = [k.buf() for _ in range(NTB)]
            qRb = [k.buf() for _ in range(NTB)]
            kNb = [k.buf() for _ in range(NCk)]
            kRb = k.buf()
            Vb = [k.buf() for _ in range(NCk)]
            latr = ARot(k, ar, 2, (4, TB), BF16)
            csr = ARot(k, ar, 2, (2, TB), BF16)
            sqr = ARot(k, ar, 3, (TB,), BF16)
            f32r = ARot(k, ar, 4, (TB,), F32)
            PTr = ARot(k, ar, 3, (TB,), BF16)
            yor = ARot(k, ar, 2, (TB,), BF16)
            kmax = ar.alloc((4,), F32)
            kmb = k.buf()
            k.group("pool", d_w, [(wq[i], WB3[hh, i], [], [wb_]) for i in range(5)])
            k.op("pool", lambda e: e.memset(kR, 1.0), writes=[kRb])
            k.group("sp", d_c, [(kR[0:64, :], exT_d[EX_KR:EX_KR + 64, :], [], [kRb])])
            k.op("dve", lambda e: e.memset(kmax, 0.0), writes=[kmb])
            for tt in range(NTB):
                ts = slice(tt * TB, (tt + 1) * TB)
                lt, ltb = latr.next()
                k.group("sp", d_u[tt % 2], [(lt, exT_d[EX_KV:EX_KV + 512, ts].rearrange("(kc p) t -> p kc t", p=128), [], [ltb])])
                cl = list(range(tt * 4, tt * 4 + 4))
                pt, pb = nps()

                def mm(e, pt=pt, lt=lt):
                    r_ = None
                    for kc in range(4):
                        r_ = e.matmul(pt[:, 0:TB], lhsT=wq[3][:, kc * 128:(kc + 1) * 128], rhs=lt[:, kc, :], start=(kc == 0), stop=(kc == 3))
                    return r_
                k.op("pe", mm, reads=[wb_, ltb], writes=[pb])
                k.op("act", lambda e, pt=pt, ts=ts: e.copy(out=kN[:, ts], in_=pt[:, 0:TB]), reads=[pb], writes=[kNb[c] for c in cl])
                for cc in range(4):
                    c = tt * 4 + cc
                    pv, pvb = nps()

                    def mmv(e, pv=pv, lt=lt, cc=cc):
                        r_ = None
                        for kc in range(4):
                            r_ = e.matmul(pv[:, 0:128], lhsT=lt[:, kc, cc * 128:(cc + 1) * 128], rhs=wq[4][:, kc * 128:(kc + 1) * 128], start=(kc == 0), stop=(kc == 3))
                        return r_
                    k.op("pe", mmv, reads=[wb_, ltb], writes=[pvb])
                    k.op("dve", lambda e, pv=pv, c=c: e.tensor_copy(out=V[:, c, :], in_=pv[:, 0:128]), reads=[pvb], writes=[Vb[c]])
                s1, s1b = sqr.next()
                s2, s2b = sqr.next()
                k.op("act", lambda e, s1=s1, ts=ts: e.activation(out=s1, in_=kN[:, ts], func=AF.Square), reads=[kNb[c] for c in cl], writes=[s1b])
                k.op("act", lambda e, s2=s2, ts=ts: e.activation(out=s2[0:64, :], in_=kR[0:64, ts], func=AF.Square), reads=[kRb], writes=[s2b])
                pn, pnb = nps()

                def mmn(e, pn=pn, s1=s1, s2=s2):
                    e.matmul(pn[:, 0:TB], lhsT=oneh[:], rhs=s1, start=True, stop=False)
                    return e.matmul(pn[:, 0:TB], lhsT=oneh[0:64, :], rhs=s2[0:64, :], start=False, stop=True)
                k.op("pe", mmn, reads=[s1b, s2b, cstb], writes=[pnb])
                k.op("dve", lambda e, pn=pn: e.reduce_max(out=kmax[:, 1:2], in_=pn[:, 0:TB], axis=AX.X), reads=[pnb, kmb], writes=[kmb])
                k.op("dve", lambda e: e.tensor_tensor(out=kmax[:, 0:1], in0=kmax[:, 0:1], in1=kmax[:, 1:2], op=ALU.max), reads=[kmb], writes=[kmb])
            k.op("act", lambda e: e.activation(out=kmax[:, 2:3], in_=kmax[:, 0:1], func=AF.Sqrt), reads=[kmb], writes=[kmb])
            for tt in range(NTB):
                ts = slice(tt * TB, (tt + 1) * TB)
                lt, ltb = latr.next()
                cs_, csb = csr.next()
                k.group("sp", d_u[tt % 2], [(lt, exT_d[EX_Q:EX_Q + 512, ts].rearrange("(kc p) t -> p kc t", p=128), [], [ltb]),
                                             (cs_[0:64, 0, :], exT_d[EX_COS:EX_COS + 64, ts], [], [csb]),
                                             (cs_[0:64, 1, :], exT_d[EX_SIN:EX_SIN + 64, ts], [], [csb])])
                pt, pb = nps()

                def mmq(e, pt=pt, lt=lt):
                    r_ = None
                    for kc in range(4):
                        r_ = e.matmul(pt[:, 0:TB], lhsT=wq[0][:, kc * 128:(kc + 1) * 128], rhs=lt[:, kc, :], start=(kc == 0), stop=(kc == 3))
                    return r_
                k.op("pe", mmq, reads=[wb_, ltb], writes=[pb])
                k.op("act", lambda e, pt=pt, ts=ts: e.activation(out=qN[:, ts], in_=pt[:, 0:TB], func=AF.Copy, scale=qscale), reads=[pb], writes=[qNb[tt]])
                pr, prb = nps()
                pw, pwb = nps()
                for (wi, pp, ppb) in ((1, pr, prb), (2, pw, pwb)):
                    def mmr(e, pp=pp, lt=lt, wi=wi):
                        r_ = None
                        for kc in range(4):
                            r_ = e.matmul(pp[0:64, 0:TB], lhsT=wq[wi][:, kc * 64:(kc + 1) * 64], rhs=lt[:, kc, :], start=(kc == 0), stop=(kc == 3))
                        return r_
                    k.op("pe", mmr, reads=[wb_, ltb], writes=[ppb])
                a1, a1b = f32r.next()
                a2, a2b = f32r.next()
                k.op("dve", lambda e, a1=a1, pr=pr, cs_=cs_: e.tensor_tensor(out=a1[0:64, :], in0=pr[0:64, 0:TB], in1=cs_[0:64, 0, :], op=ALU.mult), reads=[prb, csb], writes=[a1b])
                k.op("dve", lambda e, a2=a2, pw=pw, cs_=cs_: e.tensor_tensor(out=a2[0:64, :], in0=pw[0:64, 0:TB], in1=cs_[0:64, 1, :], op=ALU.mult), reads=[pwb, csb], writes=[a2b])
                k.op("pool", lambda e, a1=a1, a2=a2: e.tensor_tensor(out=a1[0:64, :], in0=a1[0:64, :], in1=a2[0:64, :], op=ALU.add), reads=[a1b, a2b], writes=[a1b])
                k.op("act", lambda e, a1=a1, ts=ts: e.activation(out=qR[0:64, ts], in_=a1[0:64, :], func=AF.Copy, scale=qscale), reads=[a1b], writes=[qRb[tt]])
                s1, s1b = sqr.next()
                s2, s2b = sqr.next()
                k.op("act", lambda e, s1=s1, ts=ts: e.activation(out=s1, in_=qN[:, ts], func=AF.Square), reads=[qNb[tt]], writes=[s1b])
                k.op("act", lambda e, s2=s2, ts=ts: e.activation(out=s2[0:64, :], in_=qR[0:64, ts], func=AF.Square), reads=[qRb[tt]], writes=[s2b])
                pn, pnb = nps()

                def mmn2(e, pn=pn, s1=s1, s2=s2):
                    e.matmul(pn[:, 0:TB], lhsT=oneh[:], rhs=s1, start=True, stop=False)
                    return e.matmul(pn[:, 0:TB], lhsT=oneh[0:64, :], rhs=s2[0:64, :], start=False, stop=True)
                k.op("pe", mmn2, reads=[s1b, s2b, cstb], writes=[pnb])
                a3, a3b = f32r.next()
                k.op("act", lambda e, a3=a3, pn=pn: e.activation(out=a3[64:65, :], in_=pn[64:65, 0:TB], func=AF.Sqrt), reads=[pnb], writes=[a3b])
                k.op("dve", lambda e, a3=a3, ts=ts: e.tensor_scalar(out=qR[64:65, ts], in0=a3[64:65, :], scalar1=kmax[64:65, 2:3], scalar2=-1.0, op0=ALU.mult, op1=ALU.mult),
                     reads=[a3b, kmb, qRb[tt]], writes=[qRb[tt]])
            accs = [[cm.ps[3], cm.ps[4]], [cm.ps[5], cm.ps[6]]]
            srot = Rot([cm.ps[i] for i in range(3)])
            for qg in range(NTB):
                ts = slice(qg * TB, (qg + 1) * TB)
                (pO, pOb), (pL, pLb) = accs[qg % 2]
                def issue_s(kb, ts=ts, qg=qg):
                    ks = slice(kb * 128, (kb + 1) * 128)
                    pS, pSb = srot.next()

                    def mms(e, pS=pS, ks=ks, ts=ts):
                        e.matmul(pS[:, 0:TB], lhsT=kN[:, ks], rhs=qN[:, ts], start=True, stop=False)
                        return e.matmul(pS[:, 0:TB], lhsT=kR[0:65, ks], rhs=qR[0:65, ts], start=False, stop=True)
                    k.op("pe", mms, reads=[kNb[kb], kRb, qNb[qg], qRb[qg]], writes=[pSb])
                    return pS, pSb
                nxt = issue_s(0)
                for kb in range(NCk):
                    pS, pSb = nxt
                    if kb + 1 < NCk:
                        nxt = issue_s(kb + 1)
                    PT, PTb = PTr.next()
                    k.op("act", lambda e, PT=PT, pS=pS: e.activation(out=PT, in_=pS[:, 0:TB], func=AF.Exp), reads=[pSb], writes=[PTb])

                    def mmo(e, pO=pO, pL=pL, PT=PT, kb=kb):
                        e.matmul(pO[:, 0:TB], lhsT=V[:, kb, :], rhs=PT, start=(kb == 0), stop=(kb == NCk - 1))
                        return e.matmul(pL[:, 0:TB], lhsT=oneh[:], rhs=PT, start=(kb == 0), stop=(kb == NCk - 1))
                    k.op("pe", mmo, reads=[Vb[kb], PTb, cstb], writes=[pOb, pLb])
                rl, rlb = f32r.next()
                k.op("dve", lambda e, rl=rl, pL=pL: e.reciprocal(out=rl, in_=pL[:, 0:TB]), reads=[pLb], writes=[rlb])
                yo, yob = yor.next()
                k.op("dve", lambda e, yo=yo, pO=pO, rl=rl: e.tensor_tensor(out=yo, in0=pO[:, 0:TB], in1=rl, op=ALU.mult), reads=[pOb, rlb], writes=[yob])
                k.group("sp", d_o, [(yparts[2][hh * 128:(hh + 1) * 128, ts], yo, [yob], [])])
            k.barrier()
        head(0)
        head(1)

    if 1 in B_PARTS:
        emit_B1()
    if 2 in B_PARTS:
        emit_B2()
    if 3 in B_PARTS:
        emit_B3()
    if fused:
        k.barrier()
    else:
        k.wait_all("sp")
    k.emit()
    return nc


def phaseB_weights(inp, L, g):
    w_in = inp["w_in"][L]
    out = {}
    wq = w_in[:, OFF_Q + g * 128:OFF_Q + (g + 1) * 128]
    wk = w_in[:, OFF_K + g * 128:OFF_K + (g + 1) * 128]
    wv = w_in[:, OFF_V + g * 256:OFF_V + (g + 1) * 256]
    gc = [OFF_IG + g, OFF_IG + 4 + g, OFF_FG + g, OFF_FG + 4 + g]
    out["WB1"] = np.ascontiguousarray(np.concatenate([chunks_lhsT(wq), chunks_lhsT(wk)], axis=0))
    out["WB1t"] = rhs_layout(np.concatenate([wk, wv, w_in[:, gc]], axis=1))
    pb = np.zeros((1, NPB), np.float32)
    pb[0, PB_BIG:PB_BIG + 2] = inp["mlstm_b_igate"][L][:, g]
    pb[0, PB_BFG:PB_BFG + 2] = inp["mlstm_b_fgate"][L][:, g]
    pb[0, PB_MN:PB_MN + 256] = inp["mlstm_norm"][L][g * 256:(g + 1) * 256]
    pb[0, PB_DTB:PB_DTB + 8] = inp["ssm_dt_bias"][L][:, 4 * g:4 * g + 4].reshape(8)
    pb[0, PB_ALOG:PB_ALOG + 8] = inp["ssm_a_log"][L][:, 4 * g:4 * g + 4].reshape(8)
    pb[0, PB_DSK:PB_DSK + 256] = np.repeat(inp["ssm_d"][L][4 * g:4 * g + 4], 64)
    out["PB"] = pb
    grp = g // 2
    chans = np.concatenate([np.arange(256 * g, 256 * g + 256), 1024 + grp * 128 + np.arange(128), 1280 + grp * 128 + np.arange(128)])
    cw = np.zeros((128, 4, 6), np.float32)
    cwl = inp["conv_w"][L][:, chans]
    cw[:, :, 0:5] = cwl.reshape(5, 4, 128).transpose(2, 1, 0)
    cw[:, :, 5] = inp["conv_b"][L][chans].reshape(4, 128).T
    out["CW"] = cw
    out["WB2"] = np.ascontiguousarray(chunks_lhsT(w_in[:, OFF_XBC + chans]))
    dtc = [OFF_DT + d * 16 + 4 * g + hl for d in range(2) for hl in range(4)]
    out["WB2t"] = rhs_layout(w_in[:, dtc])
    w3 = np.zeros((2, 5, 128, 512), np.float32)
    for hh in range(2):
        h = 2 * g + hh
        uq = inp["mla_w_uq"][L][:, h * 192:(h + 1) * 192]
        ukv = inp["mla_w_ukv"][L][:, h * 256:(h + 1) * 256]
        w3[hh, 0] = chunks_lhsT(uq[:, 0:128])[0]
        rot = uq[:, 128:192]
        w3[hh, 1, :, 0:256] = chunks_lhsT(rot, 64)[0]
        w3[hh, 2, :, 0:256] = chunks_lhsT(np.concatenate([rot[:, 32:], rot[:, :32]], axis=1), 64)[0]
        w3[hh, 3] = chunks_lhsT(ukv[:, 0:128])[0]
        w3[hh, 4] = rhs_layout(ukv[:, 128:256]).reshape(128, 512)
    out["WB3"] = w3
    out["CONST"] = tri_consts()
    return out


_PROG = {}


def _prog(key, fn):
    if key not in _PROG:
        _PROG[key] = fn()
    return _PROG[key]


def build_fused(SEQ_, DEPTH_):
    TOK = SEQ_ // 4
    nc = bass.Bass("TRN2", target_bir_lowering=False)

    def din(name, shape, dt):
        return nc.dram_tensor(name, list(shape), dt, kind="ExternalInput").ap()
    xT = din("xT", [D, SEQ_], F32)
    pT = din("pT", [DEPTH_, PLE_DIM, SEQ_], F32)
    pos = din("pos", [1, SEQ_], I32)
    WAs, GAs = [], []
    for i in range(DEPTH_ + 1):
        _, nch, _, ng = phaseA_layout(i > 0, i < DEPTH_)
        WAs.append(din("WA%d" % i, [nch, 128, 2048], F32))
        GAs.append(din("GA%d" % i, [128, ng], F32))
    nB = DEPTH_ * 4
    WB1 = din("WB1", [nB, 2, 128, 2048], F32)
    WB1t = din("WB1t", [nB, 128, 16, 388], F32)
    PB = din("PB", [nB, 1, NPB], F32)
    CW = din("CW", [nB, 128, 4, 6], F32)
    WB2 = din("WB2", [nB, 4, 128, 2048], F32)
    WB2t = din("WB2t", [nB, 128, 16, 8], F32)
    WB3 = din("WB3", [nB, 2, 5, 128, 512], F32)
    CONST = din("CONST", [8, 128, 128], F32)
    outT = nc.dram_tensor("outT", [D, SEQ_], F32, kind="ExternalOutput").ap()
    hT = nc.dram_tensor("hT_int", [D, SEQ_], F32, kind="Internal").ap()
    exT = nc.dram_tensor("exT_int", [EX_ROWS, SEQ_], BF16, kind="Internal").ap()
    yT = nc.dram_tensor("yT_int", [3072, SEQ_], BF16, kind="Internal").ap()
    shared = Shared(nc)
    for i in range(DEPTH_ + 1):
        has_tail = i > 0
        has_head = i < DEPTH_
        for q in range(4):
            cs = slice(q * TOK, (q + 1) * TOK)
            ext = dict(hT=(xT if i == 0 else hT)[:, cs], WA=WAs[i], GA=GAs[i], hTo=(hT if has_head else outT)[:, cs],
                       yT=yT[:, cs], pT=(pT[i - 1][:, cs] if has_tail else None), pos=pos[:, cs], exT=exT[:, cs])
            build_phaseA(has_tail, has_head, not has_head, TOK, nc=nc, ext=ext, shared=shared)
        if has_head:
            for g in range(4):
                j = i * 4 + g
                ext = dict(exT=exT, WB1=WB1[j], WB1t=WB1t[j], PB=PB[j], CW=CW[j], WB2=WB2[j], WB2t=WB2t[j], WB3=WB3[j], CONST=CONST,
                           yparts=[yT[br * 1024 + g * 256:br * 1024 + (g + 1) * 256] for br in range(3)])
                build_phaseB(SEQ_, nc=nc, ext=ext, shared=shared)
    shared.close()
    return nc


FUSED = True


def kernel(**inp):
    if FUSED:
        return kernel_fused(**inp)
    return kernel_unfused(**inp)


def kernel_fused(**inp):
    inp = {kk: np.asarray(v) for kk, v in inp.items()}
    nc = _prog(("F", SEQ, DEPTH), lambda: build_fused(SEQ, DEPTH))
    common = {}
    for i in range(DEPTH + 1):
        WA, GA = phaseA_weights(inp, i - 1 if i > 0 else None, i if i < DEPTH else None)
        common["WA%d" % i] = WA
        common["GA%d" % i] = GA
    wb = [phaseB_weights(inp, L, g) for L in range(DEPTH) for g in range(4)]
    for name in ("WB1", "WB1t", "PB", "CW", "WB2", "WB2t", "WB3"):
        common[name] = np.ascontiguousarray(np.stack([w[name] for w in wb], axis=0))
    common["CONST"] = tri_consts()
    in_maps = []
    for b in range(BATCH):
        m = dict(common)
        m["xT"] = np.ascontiguousarray(inp["x"][b].T)
        m["pT"] = np.ascontiguousarray(inp["p"][:, b].transpose(0, 2, 1))
        m["pos"] = np.ascontiguousarray(inp["positions"][b:b + 1]).astype(np.int32)
        in_maps.append(m)
    res = run_bass_kernel_spmd(nc, in_maps, core_ids=list(range(BATCH)))
    out = np.empty((BATCH, SEQ, D), np.float32)
    for b in range(BATCH):
        out[b] = np.asarray(res.results[b]["outT"]).T
    return out


def kernel_unfused(**inp):
    inp = {kk: np.asarray(v) for kk, v in inp.items()}
    TOK = SEQ // 4
    ncores = BATCH * 4
    cores = list(range(ncores))
    x = inp["x"]
    hT = [np.ascontiguousarray(x[c // 4, (c % 4) * TOK:(c % 4 + 1) * TOK].T) for c in cores]
    yT = None
    out = None
    for i in range(DEPTH + 1):
        has_tail = i > 0
        has_head = i < DEPTH
        nc = _prog(("A", has_tail, has_head, TOK), lambda: build_phaseA(has_tail, has_head, not has_head, TOK))
        WA, GA = phaseA_weights(inp, i - 1 if has_tail else None, i if has_head else None)
        in_maps = []
        for c in cores:
            b, q = c // 4, c % 4
            m = {"hT": hT[c], "WA": WA, "GA": GA}
            if has_tail:
                m["yT"] = yT[c]
                m["pT"] = np.ascontiguousarray(inp["p"][i - 1, b, q * TOK:(q + 1) * TOK].T)
            if has_head:
                m["pos"] = np.ascontiguousarray(inp["positions"][b:b + 1, q * TOK:(q + 1) * TOK]).astype(np.int32)
            in_maps.append(m)
        res = run_bass_kernel_spmd(nc, in_maps, core_ids=cores)
        hT = [np.asarray(res.results[c]["hTo"]) for c in cores]
        if not has_head:
            out = np.empty((BATCH, SEQ, D), np.float32)
            for c in cores:
                out[c // 4, (c % 4) * TOK:(c % 4 + 1) * TOK] = hT[c].T
            break
        ex = [np.asarray(res.results[c]["exT"]) for c in cores]
        exb = [np.ascontiguousarray(np.concatenate(ex[b * 4:(b + 1) * 4], axis=1)) for b in range(BATCH)]
        ncB = _prog(("B", SEQ), lambda: build_phaseB(SEQ))
        in_maps = []
        for c in cores:
            b, g = c // 4, c % 4
            m = phaseB_weights(inp, i, g)
            m["exT"] = exb[b]
            in_maps.append(m)
        res = run_bass_kernel_spmd(ncB, in_maps, core_ids=cores)
        yB = [np.asarray(res.results[c]["yT"]) for c in cores]
        yT = []
        for c in cores:
            b, q = c // 4, c % 4
            rows = []
            for br in range(3):
                for g in range(4):
                    rows.append(yB[b * 4 + g][br * 256:(br + 1) * 256, q * TOK:(q + 1) * TOK])
            yT.append(np.ascontiguousarray(np.concatenate(rows, axis=0)))
    return out
```

```python
import contextlib
import math
import numpy as np
import ml_dtypes
import concourse.bass as bass
import concourse.mybir as mybir
from concourse.bass_utils import run_bass_kernel_spmd

F32 = mybir.dt.float32
BF16 = mybir.dt.bfloat16
I32 = mybir.dt.int32
AF = mybir.ActivationFunctionType
ALU = mybir.AluOpType
AX = mybir.AxisListType

D = 2048
DFF = 5632
KC = D // 128
JC = DFF // 128
SEQ = 8192
BATCH = 2
DEPTH = 4
EPS = 1e-6
CH = 128
PLE_DIM = 256
OFF_Q, OFF_K, OFF_V, OFF_O, OFF_IG, OFF_FG = 0, 512, 1024, 2048, 3072, 3080
OFF_Z, OFF_XBC, OFF_DT = 3088, 4112, 5648
OFF_CQ, OFF_CKV, OFF_KR, OFF_GATE = 5680, 6192, 6704, 6768
EX_U, EX_Q, EX_KV, EX_KR, EX_COS, EX_SIN, EX_ROWS = 0, 2048, 2560, 3072, 3136, 3200, 3264
TWO_PI = 2.0 * math.pi
ENGS = ("pe", "act", "dve", "pool", "sp")


class Buf:
    __slots__ = ("name", "w", "r", "x")

    def __init__(self, name=None, x=False):
        self.name = name
        self.w = None
        self.r = {}
        self.x = x


class DSem:
    __slots__ = ("key", "count")

    def __init__(self, key):
        self.key = key
        self.count = 0


class Shared:
    def __init__(self, nc):
        self.nc = nc
        self.stack = contextlib.ExitStack()
        self.cnt = {e: 0 for e in ENGS}
        self.waited = {e: {} for e in ENGS}
        self.sems = {}
        for e in ENGS:
            self.sems[e] = self.stack.enter_context(nc.semaphore("s_" + e))
        self.dpool = []
        self.ntens = 0

    def close(self):
        self.stack.close()


class KB:
    def __init__(self, nc, shared=None):
        self.nc = nc
        self.stack = contextlib.ExitStack()
        self.streams = {e: [] for e in ENGS}
        self.own = shared is None
        if shared is None:
            shared = Shared(nc)
        self.shared = shared
        self.cnt = shared.cnt
        self.waited = shared.waited
        self.sems = shared.sems
        self.dsems = []
        self.sb_bytes = 0
        self.n_ops = 0

    def sbuf(self, name, shape, dtype):
        self.shared.ntens += 1
        t = self.stack.enter_context(self.nc.sbuf_tensor("%s_%d" % (name, self.shared.ntens), list(shape), dtype))
        sz = 4 if dtype in (F32, I32) else 2
        self.sb_bytes += int(np.prod(shape[1:])) * sz
        return t

    def psum(self, name, shape, dtype=F32):
        self.shared.ntens += 1
        return self.stack.enter_context(self.nc.psum_tensor("%s_%d" % (name, self.shared.ntens), list(shape), dtype))

    def buf(self, name=None):
        return Buf(name)

    def dsem(self):
        i = len(self.dsems)
        pool = self.shared.dpool
        if i >= len(pool):
            key = "d%d" % i
            self.sems[key] = self.shared.stack.enter_context(self.nc.semaphore("sd_%d" % i))
            pool.append(DSem(key))
        d = pool[i]
        self.dsems.append(d)
        return d

    def _deps(self, reads, writes, accum=None):
        deps = {}
        for b in reads:
            if b.w is not None:
                kk, v = b.w
                if deps.get(kk, 0) < v:
                    deps[kk] = v
        for b in writes:
            if b.w is not None and b.w[0] != accum:
                kk, v = b.w
                if deps.get(kk, 0) < v:
                    deps[kk] = v
            for kk, v in b.r.items():
                if deps.get(kk, 0) < v:
                    deps[kk] = v
        return deps

    def _emit_waits(self, eng, deps):
        wd = self.waited[eng]
        for kk, v in deps.items():
            if kk == eng and eng in ("pe", "sp"):
                continue
            if wd.get(kk, 0) >= v:
                continue
            wd[kk] = v
            self.streams[eng].append(("w", kk, v))

    def _mark(self, tok, reads, writes):
        kk, v = tok
        for b in reads:
            if b.r.get(kk, 0) < v:
                b.r[kk] = v
        for b in writes:
            b.w = tok
            b.r = {}

    def op(self, eng, fn, reads=(), writes=()):
        xr = [b for b in reads if b.x]
        if xr:
            writes = list(writes) + xr
        deps = self._deps(reads, writes)
        self._emit_waits(eng, deps)
        self.cnt[eng] += 1
        tok = (eng, self.cnt[eng])
        self.streams[eng].append(("o", fn, eng, 1))
        self._mark(tok, reads, writes)
        self.n_ops += 1
        return tok

    def dma(self, q, ds, out, in_, reads=(), writes=(), accum=False):
        deps = self._deps(reads, writes, accum=(ds.key if accum else None))
        self._emit_waits(q, deps)
        ds.count += 16
        tok = (ds.key, ds.count)

        def fn(e, out=out, in_=in_):
            return e.dma_start(out=out, in_=in_)

        self.streams[q].append(("o", fn, ds.key, 16))
        self._mark(tok, reads, writes)
        return tok

    def gbegin(self, q, ds):
        if ds.count:
            self._emit_waits(q, {ds.key: ds.count})
        self._g = (ds, [], [])

    def gdma(self, q, out, in_, reads=(), writes=()):
        ds, gw, gr = self._g
        self.dma(q, ds, out, in_, reads=reads, writes=writes, accum=True)
        gw.extend(writes)
        gr.extend(reads)

    def group(self, q, ds, items):
        self.gbegin(q, ds)
        for (out, in_, rd, wr) in items:
            self.gdma(q, out, in_, reads=rd, writes=wr)
        self.gend()

    def gend(self):
        ds, gw, gr = self._g
        for b in gw:
            b.w = (ds.key, ds.count)
        for b in gr:
            b.r[ds.key] = ds.count
        self._g = None

    def barrier(self):
        for eng in ENGS:
            deps = {}
            for d in self.dsems:
                if d.count:
                    deps[d.key] = d.count
            for e in ENGS:
                if e != eng and self.cnt[e]:
                    deps[e] = self.cnt[e]
            self._emit_waits(eng, deps)

    def wait_all(self, eng="sp"):
        deps = {}
        for d in self.dsems:
            if d.count:
                deps[d.key] = d.count
        for e in ENGS:
            if e != eng and self.cnt[e]:
                deps[e] = self.cnt[e]
        self._emit_waits(eng, deps)

    def emit(self):
        nc = self.nc
        sems = self.sems
        streams = self.streams
        with nc.Block() as block:
            def run(e, items):
                for it in items:
                    if it[0] == "w":
                        e.wait_ge(sems[it[1]], it[2])
                    else:
                        ins = it[1](e)
                        ins.then_inc(sems[it[2]], it[3])

            @block.tensor
            def _(e):
                run(e, streams["pe"])

            @block.scalar
            def _(e):
                run(e, streams["act"])

            @block.vector
            def _(e):
                run(e, streams["dve"])

            @block.gpsimd
            def _(e):
                run(e, streams["pool"])

            @block.sync
            def _(e):
                run(e, streams["sp"])
        self.stack.close()
        if self.own:
            self.shared.close()


class Rot:
    def __init__(self, items):
        self.items = items
        self.i = 0

    def next(self):
        it = self.items[self.i]
        self.i = (self.i + 1) % len(self.items)
        return it


def mk_rot(k, name, n, shape, dtype):
    return Rot([(k.sbuf(name, shape, dtype), k.buf()) for _ in range(n)])


def chunks_lhsT(W, cw=128):
    K, N = W.shape
    kci = K // 128
    nch = N // cw
    return W.reshape(kci, 128, nch, cw).transpose(2, 1, 0, 3).reshape(nch, 128, kci * cw)


def rhs_layout(W):
    K, N = W.shape
    return np.ascontiguousarray(W.reshape(K // 128, 128, N).transpose(1, 0, 2))


def gain_cols(g):
    return np.ascontiguousarray(g.reshape(-1, 128).T)


def tri_consts():
    s = np.arange(128)[:, None]
    t = np.arange(128)[None, :]
    c = np.zeros((8, 128, 128), np.float32)
    c[0] = (s <= t)
    c[1] = (s >= t)
    c[2] = (s > t)
    c[3] = (s < t)
    c[4] = np.eye(128)
    c[5] = 1.0
    return c


class Common:
    def __init__(self, k):
        self.k = k
        self.ps = [(k.psum("ps%d" % i, [128, 512]), Buf(x=True)) for i in range(7)]
        self.psrot = Rot(self.ps)
        pst = k.psum("pst", [128, 1024], BF16)
        pstb = Buf(x=True)
        self.pst = Rot([(pst[:, 0:512], pstb), (pst[:, 512:1024], pstb)])
        self.d_misc = k.dsem()

    def nps(self):
        return self.psrot.next()


def emit_sincos(k, cm, posi, posib, invf_ap, sgn_ap, n, cos_out, sin_out, outb, tmp, scale=1.0):
    (ang, angb), (y, yb), (yf, yfb), (t, tb) = tmp[:4]
    (yi, yib) = tmp[4]
    k.op("dve", lambda e: e.tensor_copy(out=ang[0:64, 0:n], in_=posi), reads=[posib], writes=[angb])
    k.op("dve", lambda e: e.tensor_scalar(out=ang[0:64, 0:n], in0=ang[0:64, 0:n], scalar1=invf_ap, scalar2=None, op0=ALU.mult),
         reads=[angb], writes=[angb])
    for which in (0, 1):
        off = 0.25 if which == 0 else 0.0
        k.op("dve", lambda e, off=off: e.tensor_scalar(out=y[0:64, 0:n], in0=ang[0:64, 0:n], scalar1=float(1.0 / TWO_PI), scalar2=off,
                                                        op0=ALU.mult, op1=ALU.add), reads=[angb], writes=[yb])
        k.op("dve", lambda e: e.tensor_copy(out=yi[0:64, 0:n], in_=y[0:64, 0:n]), reads=[yb], writes=[yib])
        k.op("dve", lambda e: e.tensor_copy(out=yf[0:64, 0:n], in_=yi[0:64, 0:n]), reads=[yib], writes=[yfb])
        k.op("dve", lambda e: e.tensor_tensor(out=y[0:64, 0:n], in0=y[0:64, 0:n], in1=yf[0:64, 0:n], op=ALU.subtract),
             reads=[yb, yfb], writes=[yb])
        k.op("dve", lambda e: e.tensor_scalar(out=yf[0:64, 0:n], in0=y[0:64, 0:n], scalar1=0.5, scalar2=None, op0=ALU.is_gt),
             reads=[yb], writes=[yfb])
        k.op("dve", lambda e: e.tensor_tensor(out=y[0:64, 0:n], in0=y[0:64, 0:n], in1=yf[0:64, 0:n], op=ALU.subtract),
             reads=[yb, yfb], writes=[yb])
        k.op("dve", lambda e: e.tensor_scalar(out=yf[0:64, 0:n], in0=y[0:64, 0:n], scalar1=-0.5, scalar2=None, op0=ALU.is_lt),
             reads=[yb], writes=[yfb])
        k.op("dve", lambda e: e.tensor_tensor(out=y[0:64, 0:n], in0=y[0:64, 0:n], in1=yf[0:64, 0:n], op=ALU.add),
             reads=[yb, yfb], writes=[yb])
        k.op("act", lambda e: e.activation(out=t[0:64, 0:n], in_=y[0:64, 0:n], func=AF.Sin, scale=float(TWO_PI * (1 - 1e-6))),
             reads=[yb], writes=[tb])
        if which == 0:
            k.op("dve", lambda e: e.tensor_scalar(out=cos_out, in0=t[0:64, 0:n], scalar1=float(scale), scalar2=None, op0=ALU.mult),
                 reads=[tb], writes=[outb])
        else:
            k.op("dve", lambda e: e.tensor_scalar(out=sin_out, in0=t[0:64, 0:n], scalar1=sgn_ap, scalar2=float(scale),
                                                   op0=ALU.mult, op1=ALU.mult), reads=[tb], writes=[outb])


def phaseA_layout(has_tail, has_head):
    idx = {}
    n = 0

    def add(name, cnt):
        nonlocal n
        idx[name] = n
        n += cnt
    if has_tail:
        add("wo", 8)
        add("wz", 8)
        add("wgate", 48)
        add("wbr", 48)
        add("wout", 16)
        add("f2_w13", 2 * JC)
        add("f2_w2", JC)
        add("pgate", 16)
        add("pproj", 16)
    if has_head:
        add("f1_w13", 2 * JC)
        add("f1_w2", JC)
        add("wcq", 4)
        add("wckv", 4)
        add("wkr", 2)
    g = {}
    m = 0
    for name, cnt in (("mixp", 16), ("ssmn", 8), ("ffn2", 16), ("ple", 16), ("final", 16), ("ffn1", 16), ("mix", 16),
                      ("qn", 4), ("kvn", 4), ("invf", 1), ("sgn", 1)):
        g[name] = m
        m += cnt
    return idx, n, g, m


def build_phaseA(has_tail, has_head, is_last, TOK, nc=None, ext=None, shared=None):
    PASS = min(1024, TOK)
    NPASS = TOK // PASS
    NT = PASS // 512
    idx, NCH, gi, NG = phaseA_layout(has_tail, has_head)
    fused = nc is not None
    if not fused:
        nc = bass.Bass("TRN2", target_bir_lowering=False)
        hT_d = nc.dram_tensor("hT", [D, TOK], F32, kind="ExternalInput").ap()
        WA = nc.dram_tensor("WA", [NCH, 128, 2048], F32, kind="ExternalInput").ap()
        GA = nc.dram_tensor("GA", [128, NG], F32, kind="ExternalInput").ap()
        if has_tail:
            yT_d = nc.dram_tensor("yT", [3072, TOK], BF16, kind="ExternalInput").ap()
            pT_d = nc.dram_tensor("pT", [PLE_DIM, TOK], F32, kind="ExternalInput").ap()
        if has_head:
            pos_d = nc.dram_tensor("pos", [1, TOK], I32, kind="ExternalInput").ap()
            exT_d = nc.dram_tensor("exT", [EX_ROWS, TOK], BF16, kind="ExternalOutput").ap()
        hO_d = nc.dram_tensor("hTo", [D, TOK], F32, kind="ExternalOutput").ap()
    else:
        hT_d, WA, GA, hO_d = ext["hT"], ext["WA"], ext["GA"], ext["hTo"]
        yT_d, pT_d, pos_d, exT_d = ext.get("yT"), ext.get("pT"), ext.get("pos"), ext.get("exT")

    k = KB(nc, shared)
    cm = Common(k)
    nps = cm.nps
    hT = k.sbuf("hT", [128, KC, PASS], F32)
    hb = [[k.buf() for _ in range(NT)] for _ in range(KC)]
    xn = k.sbuf("xn", [128, KC, PASS], BF16)
    xb = [k.buf() for _ in range(NT)]
    ga = k.sbuf("ga", [128, NG], F32)
    gab = k.buf()
    ones = k.sbuf("ones", [128, 128], BF16)
    onesb = k.buf()
    NSLOT = 5
    wsl = [(k.sbuf("wsl", [128, 2048], BF16), k.buf(), k.dsem()) for _ in range(NSLOT)]
    wst = [0]
    sqr = mk_rot(k, "sq", 3, [128, 512], BF16)
    rstd = k.sbuf("rstd", [128, 512], F32)
    rstdb = k.buf()
    f32r = mk_rot(k, "f32t", 3, [128, 512], F32)
    U = k.sbuf("U", [128, 24 * PASS], BF16)
    ybuf = U[:, 0:8 * PASS].rearrange("p (a b) -> p a b", b=PASS)
    merged = U[:, 8 * PASS:24 * PASS].rearrange("p (a b) -> p a b", b=PASS)
    gT = [U[:, i * 4 * PASS:(i + 1) * 4 * PASS].rearrange("p (a b) -> p a b", b=PASS) for i in range(2)]
    pTs = U[:, 8 * PASS:10 * PASS].rearrange("p (a b) -> p a b", b=PASS)
    lat = U[:, 0:8 * PASS].bitcast(F32).rearrange("p (a b) -> p a b", b=PASS)
    latn = U[:, 8 * PASS:12 * PASS].rearrange("p (a b) -> p a b", b=PASS)
    krt = U[:, 12 * PASS:13 * PASS]
    cst = U[:, 13 * PASS:15 * PASS]
    Ub = k.buf()
    ybb = [[k.buf() for _ in range(NT)] for _ in range(8)]
    mgb = [[k.buf() for _ in range(NT)] for _ in range(KC)]
    gTb = [[[k.buf() for _ in range(NT)] for _ in range(4)] for _ in range(2)]
    latb = [[k.buf() for _ in range(NT)] for _ in range(4)]
    latnb = [k.buf() for _ in range(NT)]
    d_in = k.dsem()
    d_out = k.dsem()
    d_y = k.dsem()
    d_p = k.dsem()
    if has_head:
        cosf = k.sbuf("cosf", [64, PASS], F32)
        sinf = k.sbuf("sinf", [64, PASS], F32)
        csb = k.buf()
        posi = k.sbuf("posi", [64, PASS], I32)
        posib = k.buf()
        sct = [(k.sbuf("sct", [64, 512], F32), k.buf()) for _ in range(4)] + [(k.sbuf("sci", [64, 512], I32), k.buf())]

    def sl(t):
        return slice(t * 512, (t + 1) * 512)

    def wload(ci, n):
        i = wst[0]
        wst[0] = (i + 1) % NSLOT
        t, b, ds = wsl[i]
        k.dma("pool", ds, t[:, 0:n], WA[ci, :, 0:n], writes=[b])
        return t, b

    k.group("sp", d_in, [(ga[:], GA, [], [gab])])
    k.op("dve", lambda e: e.memset(ones[:], 1.0), writes=[onesb])

    def rmsnorm(src, srcb, gcol, nkc, dmodel, out, outb_fn, writes_extra=()):
        for t in range(NT):
            pt, pb = nps()
            for kc in range(nkc):
                sq, sqb = sqr.next()
                k.op("act", lambda e, sq=sq, kc=kc, t=t: e.activation(out=sq[:], in_=src[:, kc, sl(t)], func=AF.Square),
                     reads=[srcb[kc][t]], writes=[sqb])
                k.op("pe", lambda e, sq=sq, kc=kc, pt=pt: e.matmul(pt[:], lhsT=ones[:], rhs=sq[:], start=(kc == 0), stop=(kc == nkc - 1)),
                     reads=[onesb, sqb], writes=[pb])
            k.op("act", lambda e, pt=pt: e.activation(out=rstd[:], in_=pt[:], func=AF.Sqrt, scale=1.0 / dmodel, bias=EPS),
                 reads=[pb], writes=[rstdb])
            k.op("dve", lambda e: e.reciprocal(out=rstd[:], in_=rstd[:]), reads=[rstdb], writes=[rstdb])
            for kc in range(nkc):
                k.op("dve", lambda e, kc=kc, t=t: e.scalar_tensor_tensor(
                    out=out[:, kc, sl(t)], in0=src[:, kc, sl(t)], scalar=ga[:, gcol + kc:gcol + kc + 1], in1=rstd[:],
                    op0=ALU.mult, op1=ALU.mult), reads=[srcb[kc][t], gab, rstdb], writes=[outb_fn(kc, t)])

    def lin(ci, ncols_per_kc, nkc, rhs, rhsb_fn, evac, mrows=128):
        w, wb = wload(ci, nkc * ncols_per_kc)
        for t in range(NT):
            pt, pb = nps()

            def mm(e, w=w, pt=pt, t=t):
                r = None
                for kc in range(nkc):
                    r = e.matmul(pt[0:mrows, :], lhsT=w[:, kc * ncols_per_kc:(kc + 1) * ncols_per_kc], rhs=rhs[:, kc, sl(t)],
                                 start=(kc == 0), stop=(kc == nkc - 1))
                return r
            k.op("pe", mm, reads=[wb] + rhsb_fn(t), writes=[pb])
            evac(t, pt, pb)

    def xn_bufs(t):
        return [xb[t]]

    def ffn(gcol, c13, c2):
        rmsnorm(hT, hb, gcol, KC, D, xn, lambda kc, t: xb[t])
        J = 4
        for g in range(JC // J):
            gi_ = g % 2
            for jl in range(J):
                j = g * J + jl
                hold = {}

                def ev_a(t, pt, pb, hold=hold):
                    sa, sab = f32r.next()
                    k.op("act", lambda e, sa=sa, pt=pt: e.activation(out=sa[:], in_=pt[:], func=AF.Silu), reads=[pb], writes=[sab])
                    hold[t] = (sa, sab)
                lin(c13 + 2 * j, 128, KC, xn, xn_bufs, ev_a)

                def ev_b(t, pt, pb, hold=hold, gi_=gi_, jl=jl):
                    sa, sab = hold[t]
                    k.op("dve", lambda e, sa=sa, pt=pt, t=t: e.tensor_tensor(out=gT[gi_][:, jl, sl(t)], in0=sa[:], in1=pt[:], op=ALU.mult),
                         reads=[sab, pb], writes=[gTb[gi_][jl][t]])
                lin(c13 + 2 * j + 1, 128, KC, xn, xn_bufs, ev_b)
            w2s = [wload(c2 + g * J + jl, D) for jl in range(J)]
            for m in range(KC):
                for t in range(NT):
                    po, pob = nps()

                    def mm3(e, po=po, m=m, t=t, gi_=gi_, w2s=w2s):
                        r = None
                        for jl in range(J):
                            r = e.matmul(po[:], lhsT=w2s[jl][0][:, m * 128:(m + 1) * 128], rhs=gT[gi_][:, jl, sl(t)],
                                         start=(jl == 0), stop=(jl == J - 1))
                        return r
                    k.op("pe", mm3, reads=[w[1] for w in w2s] + [gTb[gi_][jl][t] for jl in range(J)], writes=[pob])
                    k.op("dve", lambda e, po=po, m=m, t=t: e.scalar_tensor_tensor(
                        out=hT[:, m, sl(t)], in0=po[:], scalar=0.5, in1=hT[:, m, sl(t)], op0=ALU.mult, op1=ALU.add),
                        reads=[pob, hb[m][t]], writes=[hb[m][t]])

    def region_switch(new_bufs):
        k.barrier()

    for ps_i in range(NPASS):
        t0 = ps_i * PASS
        k.group("sp", d_in, [(hT[:, kc, sl(t)], hT_d[kc * 128:(kc + 1) * 128, t0 + t * 512:t0 + (t + 1) * 512], [], [hb[kc][t]])
                             for kc in range(KC) for t in range(NT)])
        if has_tail:
            region_switch(None)
            rmsnorm(hT, hb, gi["mixp"], KC, D, xn, lambda kc, t: xb[t])
            for br in range(3):
                k.group("sp", d_y, [(ybuf[:, c, sl(t)], yT_d[br * 1024 + c * 128:br * 1024 + (c + 1) * 128, t0 + t * 512:t0 + (t + 1) * 512], [], [ybb[c][t]])
                                    for c in range(8) for t in range(NT)])
                if br < 2:
                    for c in range(8):
                        def ev_g(t, pt, pb, c=c, br=br):
                            sa, sab = f32r.next()
                            k.op("act", lambda e, sa=sa, pt=pt: e.activation(out=sa[:], in_=pt[:], func=(AF.Sigmoid if br == 0 else AF.Silu)),
                                 reads=[pb], writes=[sab])
                            k.op("dve", lambda e, sa=sa, c=c, t=t: e.tensor_tensor(out=ybuf[:, c, sl(t)], in0=ybuf[:, c, sl(t)], in1=sa[:], op=ALU.mult),
                                 reads=[sab, ybb[c][t]], writes=[ybb[c][t]])
                        lin(idx["wo" if br == 0 else "wz"] + c, 128, KC, xn, xn_bufs, ev_g)
                if br == 1:
                    rmsnorm(ybuf, ybb, gi["ssmn"], 8, 1024, ybuf, lambda kc, t: ybb[kc][t])
                for m in range(KC):
                    hold = {}

                    def ev_gate(t, pt, pb, hold=hold):
                        sa, sab = f32r.next()
                        k.op("act", lambda e, sa=sa, pt=pt: e.activation(out=sa[:], in_=pt[:], func=AF.Sigmoid), reads=[pb], writes=[sab])
                        hold[t] = (sa, sab)
                    lin(idx["wgate"] + br * 16 + m, 128, KC, xn, xn_bufs, ev_gate)

                    def ev_br(t, pt, pb, hold=hold, m=m, br=br):
                        sa, sab = hold[t]
                        if br == 0:
                            k.op("dve", lambda e, sa=sa, pt=pt, t=t: e.tensor_tensor(out=merged[:, m, sl(t)], in0=sa[:], in1=pt[:], op=ALU.mult),
                                 reads=[sab, pb], writes=[mgb[m][t]])
                        else:
                            k.op("dve", lambda e, sa=sa, pt=pt: e.tensor_tensor(out=sa[:], in0=sa[:], in1=pt[:], op=ALU.mult),
                                 reads=[sab, pb], writes=[sab])
                            k.op("dve", lambda e, sa=sa, t=t: e.tensor_tensor(out=merged[:, m, sl(t)], in0=merged[:, m, sl(t)], in1=sa[:], op=ALU.add),
                                 reads=[sab, mgb[m][t]], writes=[mgb[m][t]])
                    lin(idx["wbr"] + br * 16 + m, 128, 8, ybuf, lambda t: [ybb[c][t] for c in range(8)], ev_br)
            for m in range(KC):
                def ev_out(t, pt, pb, m=m):
                    k.op("dve", lambda e, pt=pt, t=t: e.tensor_tensor(out=hT[:, m, sl(t)], in0=hT[:, m, sl(t)], in1=pt[:], op=ALU.add),
                         reads=[pb, hb[m][t]], writes=[hb[m][t]])
                lin(idx["wout"] + m, 128, KC, merged, lambda t: [mgb[c][t] for c in range(KC)], ev_out)
            region_switch(None)
            ffn(gi["ffn2"], idx["f2_w13"], idx["f2_w2"])
            region_switch(None)
            rmsnorm(hT, hb, gi["ple"], KC, D, xn, lambda kc, t: xb[t])
            pTb = [k.buf() for _ in range(NT)]
            k.group("pool", d_p, [(pTs[:, c, sl(t)], pT_d[c * 128:(c + 1) * 128, t0 + t * 512:t0 + (t + 1) * 512], [], [pTb[t]])
                                  for c in range(2) for t in range(NT)])
            for m in range(KC):
                hold = {}

                def ev_pg(t, pt, pb, hold=hold):
                    sa, sab = f32r.next()
                    k.op("act", lambda e, sa=sa, pt=pt: e.activation(out=sa[:], in_=pt[:], func=AF.Sigmoid), reads=[pb], writes=[sab])
                    hold[t] = (sa, sab)
                lin(idx["pgate"] + m, 128, KC, xn, xn_bufs, ev_pg)

                def ev_pp(t, pt, pb, hold=hold, m=m):
                    sa, sab = hold[t]
                    k.op("dve", lambda e, sa=sa, pt=pt: e.tensor_tensor(out=sa[:], in0=sa[:], in1=pt[:], op=ALU.mult), reads=[sab, pb], writes=[sab])
                    k.op("dve", lambda e, sa=sa, t=t: e.tensor_tensor(out=hT[:, m, sl(t)], in0=hT[:, m, sl(t)], in1=sa[:], op=ALU.add),
                         reads=[sab, hb[m][t]], writes=[hb[m][t]])
                lin(idx["pproj"] + m, 128, 2, pTs, lambda t: [pTb[t]], ev_pp)
        if is_last:
            rmsnorm(hT, hb, gi["final"], KC, D, hT, lambda kc, t: hb[kc][t])
        if has_head:
            region_switch(None)
            ffn(gi["ffn1"], idx["f1_w13"], idx["f1_w2"])
            region_switch(None)
            rmsnorm(hT, hb, gi["mix"], KC, D, xn, lambda kc, t: xb[t])
            k.group("sp", d_out, [(exT_d[EX_U + kc * 128:EX_U + (kc + 1) * 128, t0 + t * 512:t0 + (t + 1) * 512], xn[:, kc, sl(t)], [xb[t]], [])
                                  for kc in range(KC) for t in range(NT)])
            k.group("sp", d_in, [(posi[:], pos_d[:, t0:t0 + PASS].partition_broadcast(64), [], [posib])])
            for t in range(NT):
                emit_sincos(k, cm, posi[:, sl(t)], posib, ga[0:64, gi["invf"]:gi["invf"] + 1], ga[0:64, gi["sgn"]:gi["sgn"] + 1], 512,
                            cosf[:, sl(t)], sinf[:, sl(t)], csb, sct)
            k.op("act", lambda e: e.copy(out=cst[0:64, 0:PASS], in_=cosf[:]), reads=[csb], writes=[Ub])
            k.op("act", lambda e: e.copy(out=cst[0:64, PASS:2 * PASS], in_=sinf[:]), reads=[csb], writes=[Ub])
            k.group("sp", d_out, [(exT_d[EX_COS:EX_COS + 64, t0:t0 + PASS], cst[0:64, 0:PASS], [Ub], []),
                                  (exT_d[EX_SIN:EX_SIN + 64, t0:t0 + PASS], cst[0:64, PASS:2 * PASS], [Ub], [])])
            for which, wname, gname, exoff in ((0, "wcq", "qn", EX_Q), (1, "wckv", "kvn", EX_KV)):
                for c in range(4):
                    def ev_lat(t, pt, pb, c=c):
                        k.op("act", lambda e, pt=pt, t=t: e.copy(out=lat[:, c, sl(t)], in_=pt[:]), reads=[pb], writes=[latb[c][t]])
                    lin(idx[wname] + c, 128, KC, xn, xn_bufs, ev_lat)
                rmsnorm(lat, latb, gi[gname], 4, 512, latn, lambda kc, t: latnb[t])
                k.group("sp", d_out, [(exT_d[exoff + c * 128:exoff + (c + 1) * 128, t0 + t * 512:t0 + (t + 1) * 512], latn[:, c, sl(t)], [latnb[t]], [])
                                      for c in range(4) for t in range(NT)])
            hold = {}

            def ev_kr(t, pt, pb, hold=hold):
                sa, sab = f32r.next()
                k.op("dve", lambda e, sa=sa, pt=pt, t=t: e.tensor_tensor(out=sa[0:64, :], in0=pt[0:64, :], in1=cosf[:, sl(t)], op=ALU.mult),
                     reads=[pb, csb], writes=[sab])
                hold[t] = (sa, sab)
            lin(idx["wkr"], 64, KC, xn, xn_bufs, ev_kr, mrows=64)
            krb = [k.buf() for _ in range(NT)]

            def ev_krs(t, pt, pb, hold=hold):
                sa, sab = hold[t]
                sb_, sbb = f32r.next()
                k.op("dve", lambda e, sb_=sb_, pt=pt, t=t: e.tensor_tensor(out=sb_[0:64, :], in0=pt[0:64, :], in1=sinf[:, sl(t)], op=ALU.mult),
                     reads=[pb, csb], writes=[sbb])
                k.op("dve", lambda e, sa=sa, sb_=sb_, t=t: e.tensor_tensor(out=krt[0:64, sl(t)], in0=sa[0:64, :], in1=sb_[0:64, :], op=ALU.add),
                     reads=[sab, sbb], writes=[krb[t]])
                k.group("sp", d_out, [(exT_d[EX_KR:EX_KR + 64, t0 + t * 512:t0 + (t + 1) * 512], krt[0:64, sl(t)], [krb[t]], [])])
            lin(idx["wkr"] + 1, 64, KC, xn, xn_bufs, ev_krs, mrows=64)
        k.group("sp", d_out, [(hO_d[kc * 128:(kc + 1) * 128, t0 + t * 512:t0 + (t + 1) * 512], hT[:, kc, sl(t)], [hb[kc][t]], [])
                              for kc in range(KC) for t in range(NT)])
        k.barrier()
    if fused:
        k.barrier()
    else:
        k.wait_all("sp")
    k.emit()
    return nc


def phaseA_weights(inp, L_tail, L_head):
    has_tail = L_tail is not None
    has_head = L_head is not None
    idx, NCH, gi, NG = phaseA_layout(has_tail, has_head)
    WA = np.zeros((NCH, 128, 2048), np.float32)
    GA = np.zeros((128, NG), np.float32)

    def put(name, arr):
        n = arr.shape[0]
        WA[idx[name]:idx[name] + n, :, :arr.shape[2]] = arr

    def ffn_pack(prefix, w13, w2):
        a = chunks_lhsT(w13[:, :DFF])
        b = chunks_lhsT(w13[:, DFF:])
        ab = np.empty((2 * JC, 128, 2048), np.float32)
        ab[0::2] = a
        ab[1::2] = b
        put(prefix + "_w13", ab)
        put(prefix + "_w2", w2.reshape(JC, 128, D))
    if has_tail:
        L = L_tail
        w_in = inp["w_in"][L]
        put("wo", chunks_lhsT(w_in[:, OFF_O:OFF_O + 1024]))
        put("wz", chunks_lhsT(w_in[:, OFF_Z:OFF_Z + 1024]))
        put("wgate", chunks_lhsT(w_in[:, OFF_GATE:OFF_GATE + 3 * D]))
        wbr = np.concatenate([chunks_lhsT(inp["w_branch"][L, i]) for i in range(3)], axis=0)
        put("wbr", wbr)
        put("wout", chunks_lhsT(inp["w_out"][L]))
        ffn_pack("f2", inp["ffn2_w13"][L], inp["ffn2_w2"][L])
        put("pgate", chunks_lhsT(inp["w_ple_gate"][L]))
        put("pproj", chunks_lhsT(inp["w_ple_proj"][L]))
        GA[:, gi["mixp"]:gi["mixp"] + 16] = gain_cols(inp["mix_norm"][L])
        GA[:, gi["ssmn"]:gi["ssmn"] + 8] = gain_cols(inp["ssm_norm"][L])
        GA[:, gi["ffn2"]:gi["ffn2"] + 16] = gain_cols(inp["ffn2_norm"][L])
        GA[:, gi["ple"]:gi["ple"] + 16] = gain_cols(inp["ple_norm"][L])
    GA[:, gi["final"]:gi["final"] + 16] = gain_cols(inp["final_norm"])
    if has_head:
        L = L_head
        w_in = inp["w_in"][L]
        ffn_pack("f1", inp["ffn1_w13"][L], inp["ffn1_w2"][L])
        put("wcq", chunks_lhsT(w_in[:, OFF_CQ:OFF_CQ + 512]))
        put("wckv", chunks_lhsT(w_in[:, OFF_CKV:OFF_CKV + 512]))
        wkr = w_in[:, OFF_KR:OFF_KR + 64]
        wkr_sw = np.concatenate([wkr[:, 32:], wkr[:, :32]], axis=1)
        put("wkr", np.concatenate([chunks_lhsT(wkr, 64), chunks_lhsT(wkr_sw, 64)], axis=0))
        GA[:, gi["ffn1"]:gi["ffn1"] + 16] = gain_cols(inp["ffn1_norm"][L])
        GA[:, gi["mix"]:gi["mix"] + 16] = gain_cols(inp["mix_norm"][L])
        GA[:, gi["qn"]:gi["qn"] + 4] = gain_cols(inp["mla_q_norm"][L])
        GA[:, gi["kvn"]:gi["kvn"] + 4] = gain_cols(inp["mla_kv_norm"][L])
    invf = (10000.0 ** (-np.arange(32, dtype=np.float32) / 32)).astype(np.float32)
    GA[0:64, gi["invf"]] = np.concatenate([invf, invf])
    GA[0:64, gi["sgn"]] = np.concatenate([-np.ones(32, np.float32), np.ones(32, np.float32)])
    return WA, GA


PB_BIG, PB_BFG, PB_MN, PB_DTB, PB_ALOG, PB_DSK, NPB = 0, 2, 4, 260, 268, 276, 532
ARENA_BYTES = 188 * 1024
B_PARTS = (1, 2, 3)
DBG = 99


class Arena:
    def __init__(self, k, nbytes):
        self.t = k.sbuf("arena", [128, nbytes // 2], BF16)
        self.nbytes = nbytes
        self.off = 0
        self.limit = None
        self.resume = 0

    def reset(self):
        self.off = 0
        self.limit = None

    def reclaim(self, lo, hi):
        self.resume = self.off
        self.limit = hi
        self.off = lo

    def alloc(self, free, dtype):
        sz = 4 if dtype in (F32, I32) else 2
        n = int(np.prod(free))
        self.off = (self.off + 63) // 64 * 64
        nb = n * sz
        if self.limit is not None and self.off + nb > self.limit:
            self.off = (self.resume + 63) // 64 * 64
            self.limit = None
        start = self.off // 2
        assert self.off + nb <= self.nbytes, ("arena overflow", self.off, nb)
        v = self.t[:, start:start + nb // 2]
        if dtype == F32:
            v = v.bitcast(F32)
        elif dtype == I32:
            v = v.bitcast(I32)
        self.off += nb
        if len(free) == 2:
            v = v.rearrange("p (a b) -> p a b", b=free[1])
        elif len(free) == 3:
            v = v.rearrange("p (a b c) -> p a b c", b=free[1], c=free[2])
        return v


class ARot:
    def __init__(self, k, ar, n, free, dtype):
        self.items = [(ar.alloc(free, dtype), k.buf()) for _ in range(n)]
        self.i = 0

    def next(self):
        it = self.items[self.i]
        self.i = (self.i + 1) % len(self.items)
        return it


def build_phaseB(SEQ_, nc=None, ext=None, shared=None):
    NCk = SEQ_ // 128
    fused = nc is not None
    if not fused:
        nc = bass.Bass("TRN2", target_bir_lowering=False)
        exT_d = nc.dram_tensor("exT", [EX_ROWS, SEQ_], BF16, kind="ExternalInput").ap()
        WB1 = nc.dram_tensor("WB1", [2, 128, 2048], F32, kind="ExternalInput").ap()
        WB1t = nc.dram_tensor("WB1t", [128, 16, 388], F32, kind="ExternalInput").ap()
        PB = nc.dram_tensor("PB", [1, NPB], F32, kind="ExternalInput").ap()
        CW = nc.dram_tensor("CW", [128, 4, 6], F32, kind="ExternalInput").ap()
        WB2 = nc.dram_tensor("WB2", [4, 128, 2048], F32, kind="ExternalInput").ap()
        WB2t = nc.dram_tensor("WB2t", [128, 16, 8], F32, kind="ExternalInput").ap()
        WB3 = nc.dram_tensor("WB3", [2, 5, 128, 512], F32, kind="ExternalInput").ap()
        CONST = nc.dram_tensor("CONST", [8, 128, 128], F32, kind="ExternalInput").ap()
        yT_d = nc.dram_tensor("yT", [768, SEQ_], BF16, kind="ExternalOutput").ap()
        yparts = [yT_d[0:256], yT_d[256:512], yT_d[512:768]]
    else:
        exT_d, WB1, WB1t, PB, CW, WB2, WB2t, WB3, CONST = (ext[n] for n in ("exT", "WB1", "WB1t", "PB", "CW", "WB2", "WB2t", "WB3", "CONST"))
        yparts = ext["yparts"]

    k = KB(nc, shared)
    cm = Common(k)
    nps = cm.nps
    ar = Arena(k, ARENA_BYTES)
    cst = k.sbuf("cst", [128, 8, 128], F32)
    cstb = k.buf()
    TRI = [cst[:, 0, :], cst[:, 1, :]]
    STR = [cst[:, 2, :], cst[:, 3, :]]
    ID32 = cst[:, 4, :]
    ONE32 = cst[:, 5, :]
    MS = [cst[:, 6, :], cst[:, 7, :]]
    idh = k.sbuf("idh", [128, 128], BF16)
    oneh = k.sbuf("oneh", [128, 128], BF16)
    pbt = k.sbuf("pbt", [128, NPB], F32)
    pbb = k.buf()
    d_c = k.dsem()
    d_u = [k.dsem(), k.dsem()]
    d_w = k.dsem()
    d_o = k.dsem()
    k.group("sp", d_c, [(cst[:, 0:6, :], CONST[0:6].rearrange("c p n -> p c n"), [], [cstb]),
                        (pbt[:], PB.partition_broadcast(128), [], [pbb])])
    k.op("act", lambda e: e.copy(out=idh[:], in_=ID32), reads=[cstb], writes=[cstb])
    k.op("act", lambda e: e.copy(out=oneh[:], in_=ONE32), reads=[cstb], writes=[cstb])
    sc = 128.0 ** -0.5
    k.op("dve", lambda e: e.tensor_scalar(out=MS[0], in0=TRI[0], scalar1=sc, scalar2=None, op0=ALU.mult), reads=[cstb], writes=[cstb])
    k.op("dve", lambda e: e.tensor_scalar(out=MS[1], in0=TRI[1], scalar1=sc, scalar2=None, op0=ALU.mult), reads=[cstb], writes=[cstb])

    def pcol(c):
        return pbt[:, c:c + 1]

    def ew(eng, out, in0, in1, op, reads, writes):
        k.op(eng, lambda e: e.tensor_tensor(out=out, in0=in0, in1=in1, op=op), reads=reads, writes=writes)

    def emit_B1():
        ar.reset()
        TB = 256
        w1q = ar.alloc((2048,), BF16)
        w1k = ar.alloc((2048,), BF16)
        w1t = ar.alloc((16, 388), BF16)
        wb_ = k.buf()
        urot = ARot(k, ar, 2, (16, TB), BF16)
        proj_end = ar.off
        qT = ar.alloc((SEQ_,), BF16)
        kT = ar.alloc((SEQ_,), BF16)
        ktok = ar.alloc((NCk, 128), BF16)
        vext = ar.alloc((NCk, 260), BF16)
        gates = ar.alloc((NCk, 4), F32)
        hbk = ar.alloc((NCk, 256), BF16)
        qb = [k.buf() for _ in range(NCk)]
        kb_ = [k.buf() for _ in range(NCk)]
        ktb = [k.buf() for _ in range(NCk)]
        vb = [k.buf() for _ in range(NCk)]
        gtb = k.buf()
        hbb = [k.buf() for _ in range(NCk)]
        if DBG == 11:
            k.barrier()
            return
        k.group("pool", d_w, [(w1q, WB1[0], [], [wb_]), (w1k, WB1[1], [], [wb_]), (w1t, WB1t, [], [wb_])])
        if DBG == 12:
            k.barrier()
            return
        k.op("pool", lambda e: e.memset(vext, 1.0), writes=vb)
        if DBG == 13:
            k.barrier()
            return
        cpt = TB // 128
        for tt in range(SEQ_ // TB):
            if DBG == 14 and tt == 1:
                k.barrier()
                return
            u, ub = urot.next()
            k.group("sp", d_u[tt % 2], [(u, exT_d[EX_U:EX_U + 2048, tt * TB:(tt + 1) * TB].rearrange("(kc p) t -> p kc t", p=128), [], [ub])])
            cl = list(range(tt * cpt, (tt + 1) * cpt))
            for (w, dst, dbs, eng) in ((w1q, qT, qb, "act"), (w1k, kT, kb_, "dve")):
                pt, pb = nps()

                def mm(e, w=w, pt=pt, u=u):
                    r = None
                    for kc in range(16):
                        r = e.matmul(pt[:, 0:TB], lhsT=w[:, kc * 128:(kc + 1) * 128], rhs=u[:, kc, :], start=(kc == 0), stop=(kc == 15))
                    return r
                k.op("pe", mm, reads=[wb_, ub], writes=[pb])
                if eng == "act":
                    k.op("act", lambda e, pt=pt, dst=dst, tt=tt: e.copy(out=dst[:, tt * TB:(tt + 1) * TB], in_=pt[:, 0:TB]), reads=[pb], writes=[dbs[c] for c in cl])
                else:
                    k.op("dve", lambda e, pt=pt, dst=dst, tt=tt: e.tensor_copy(out=dst[:, tt * TB:(tt + 1) * TB], in_=pt[:, 0:TB]), reads=[pb], writes=[dbs[c] for c in cl])
            if DBG == 15:
                k.barrier()
                return
            for cc in range(cpt):
                c = tt * cpt + cc
                pt, pb = nps()

                def mm2(e, pt=pt, u=u, cc=cc):
                    r = None
                    for kc in range(16):
                        r = e.matmul(pt[:, 0:388], lhsT=u[:, kc, cc * 128:(cc + 1) * 128], rhs=w1t[:, kc, :], start=(kc == 0), stop=(kc == 15))
                    return r
                k.op("pe", mm2, reads=[wb_, ub], writes=[pb])
                if DBG == 16:
                    k.barrier()
                    return
                k.op("act", lambda e, pt=pt, c=c: e.copy(out=ktok[:, c, :], in_=pt[:, 0:128]), reads=[pb], writes=[ktb[c]])
                if DBG == 17:
                    k.barrier()
                    return
                k.op("dve", lambda e, pt=pt, c=c: e.tensor_copy(out=vext[:, c, 0:256], in_=pt[:, 128:384]), reads=[pb], writes=[vb[c]])
                if DBG == 18:
                    k.barrier()
                    return
                k.op("act", lambda e, pt=pt, c=c: e.copy(out=gates[:, c, :], in_=pt[:, 384:388]), reads=[pb], writes=[gtb])
                if DBG == 19:
                    k.barrier()
                    return

        if DBG == 1:
            k.barrier()
            return
        k.barrier()
        ar.reclaim(0, proj_end)

        def T():
            return ar.alloc((NCk,), F32)
        G = []

        def gmath(d):
            gd = {}
            tb_ = k.buf()
            tA, lf, li, bsb, gsum, a, cmx, mloc, Mc, einter, ea, es, aprev, aloc, floor_, tmp = [T() for _ in range(16)]
            marr = ar.alloc((NCk + 1,), F32)
            sm = ar.alloc((8,), F32)
            dg = ar.alloc((NCk,), F32)
            apad = ar.alloc((128,), F32)
            k.op("dve", lambda e: e.memset(apad, 0.0), writes=[tb_])
            k.op("dve", lambda e: e.memset(sm, 0.0), writes=[tb_])
            R_ = [tb_]
            k.op("dve", lambda e: e.tensor_scalar(out=sm[:, 0:1], in0=pcol(PB_BFG + d), scalar1=-1.0, scalar2=None, op0=ALU.mult), reads=[pbb], writes=R_)
            k.op("act", lambda e: e.activation(out=tA, in_=gates[:, :, 2 + d], func=AF.Exp, scale=-1.0, bias=sm[:, 0:1]), reads=[gtb] + R_, writes=R_)
            k.op("act", lambda e: e.activation(out=tA, in_=tA, func=AF.Ln, bias=1.0), reads=R_, writes=R_)
            k.op("dve", lambda e: e.tensor_scalar(out=lf, in0=tA, scalar1=-1.0, scalar2=None, op0=ALU.mult), reads=R_, writes=R_)
            k.op("dve", lambda e: e.tensor_scalar(out=li, in0=gates[:, :, d], scalar1=pcol(PB_BIG + d), scalar2=None, op0=ALU.add), reads=[gtb, pbb] + R_, writes=R_)
            p1, p1b = nps()
            p2, p2b = nps()
            k.op("pe", lambda e: e.matmul(p1[:, 0:NCk], lhsT=TRI[d], rhs=lf, start=True, stop=True), reads=[cstb] + R_, writes=[p1b])
            k.op("pe", lambda e: e.matmul(p2[:, 0:NCk], lhsT=ONE32, rhs=lf, start=True, stop=True), reads=[cstb] + R_, writes=[p2b])
            k.op("act", lambda e: e.copy(out=bsb, in_=p1[:, 0:NCk]), reads=[p1b], writes=R_)
            k.op("dve", lambda e: e.tensor_copy(out=gsum, in_=p2[:, 0:NCk]), reads=[p2b], writes=R_)
            ew("dve", a, li, bsb, ALU.subtract, R_, R_)
            k.op("dve", lambda e: e.tensor_copy(out=apad[:, 0:NCk], in_=a), reads=R_, writes=R_)
            p3, p3b = nps()
            k.op("pe", lambda e: e.matmul(p3[:, 0:128], lhsT=apad, rhs=ID32, start=True, stop=True), reads=[cstb] + R_, writes=[p3b])
            k.op("dve", lambda e: e.reduce_max(out=sm[0:NCk, 1:2], in_=p3[0:NCk, 0:128], axis=AX.X), reads=[p3b], writes=R_)
            k.op("dve", lambda e: e.tensor_scalar(out=dg, in0=ID32[:, 0:NCk], scalar1=sm[:, 1:2], scalar2=None, op0=ALU.mult), reads=[cstb] + R_, writes=R_)
            p4, p4b = nps()
            k.op("pe", lambda e: e.matmul(p4[:, 0:NCk], lhsT=ONE32, rhs=dg, start=True, stop=True), reads=[cstb] + R_, writes=[p4b])
            k.op("act", lambda e: e.copy(out=cmx, in_=p4[:, 0:NCk]), reads=[p4b], writes=R_)
            ew("dve", mloc, gsum, cmx, ALU.add, R_, R_)
            k.op("dve", lambda e: e.memset(marr, 0.0), writes=R_)
            order = list(range(NCk)) if d == 0 else list(reversed(range(NCk)))
            for c in order:
                src = c if d == 0 else c + 1
                dst = c + 1 if d == 0 else c
                k.op("dve", lambda e, c=c, src=src, dst=dst: e.scalar_tensor_tensor(
                    out=marr[:, dst:dst + 1], in0=marr[:, src:src + 1], scalar=gsum[:, c:c + 1], in1=mloc[:, c:c + 1], op0=ALU.add, op1=ALU.max),
                    reads=R_, writes=R_)
            m0 = marr[:, 0:NCk] if d == 0 else marr[:, 1:NCk + 1]
            mn = marr[:, 1:NCk + 1] if d == 0 else marr[:, 0:NCk]
            ew("dve", Mc, m0, cmx, ALU.max, R_, R_)
            ew("dve", tmp, m0, Mc, ALU.subtract, R_, R_)
            k.op("act", lambda e: e.activation(out=einter, in_=tmp, func=AF.Exp), reads=R_, writes=R_)
            ew("dve", tmp, a, Mc, ALU.subtract, R_, R_)
            k.op("act", lambda e: e.activation(out=ea, in_=tmp, func=AF.Exp), reads=R_, writes=R_)
            ew("dve", tmp, a, cmx, ALU.subtract, R_, R_)
            k.op("act", lambda e: e.activation(out=es, in_=tmp, func=AF.Exp), reads=R_, writes=R_)
            k.op("dve", lambda e: e.tensor_scalar(out=es, in0=es, scalar1=sc, scalar2=None, op0=ALU.mult), reads=R_, writes=R_)
            ew("dve", tmp, gsum, m0, ALU.add, R_, R_)
            ew("dve", tmp, tmp, mn, ALU.subtract, R_, R_)
            k.op("act", lambda e: e.activation(out=aprev, in_=tmp, func=AF.Exp), reads=R_, writes=R_)
            ew("dve", tmp, mloc, mn, ALU.subtract, R_, R_)
            k.op("act", lambda e: e.activation(out=aloc, in_=tmp, func=AF.Exp), reads=R_, writes=R_)
            ew("dve", tmp, bsb, Mc, ALU.add, R_, R_)
            k.op("act", lambda e: e.activation(out=floor_, in_=tmp, func=AF.Exp, scale=-1.0), reads=R_, writes=R_)
            gd.update(b=tb_, einter=einter, ea=ea, es=es, aprev=aprev, aloc=aloc, floor=floor_)
            G.append(gd)
        gmath(0)
        gmath(1)

        if DBG == 2:
            k.barrier()
            return
        PTr = ARot(k, ar, 4, (128,), BF16)
        vscr = ARot(k, ar, 4, (260,), BF16)
        kscr = ARot(k, ar, 4, (128,), BF16)
        t1r = ARot(k, ar, 3, (260,), F32)
        t2r = ARot(k, ar, 3, (260,), F32)
        t3r = ARot(k, ar, 2, (260,), F32)
        smr = ARot(k, ar, 4, (8,), F32)
        hsr = ARot(k, ar, 2, (256,), F32)
        jkr = ARot(k, ar, 1, (256,), F32)
        ybr = ARot(k, ar, 2, (256,), BF16)
        visited = set()

        def mk_scan1(d):
            gd = G[d]
            gb = [gd["b"]]
            C32 = ar.alloc((260,), F32)
            C32b = k.buf()
            Cbfr = ARot(k, ar, 2, (260,), BF16)
            ystr = ARot(k, ar, 2, (2, 512), BF16)
            st = {}
            k.op("pool", lambda e: e.memset(C32, 0.0), writes=[C32b])
            cbf0, cbfb0 = Cbfr.next()
            k.op("pool", lambda e: e.memset(cbf0, 0.0), writes=[cbfb0])
            st["cbf"] = (cbf0, cbfb0)
            st["yst"] = None

            def step(c):
                cbf, cbfb = st["cbf"]
                yst = st["yst"]
                first = c not in visited
                visited.add(c)
                cs = slice(c * 128, (c + 1) * 128)
                pS, pSb = nps()
                k.op("pe", lambda e, pS=pS, cs=cs: e.matmul(pS[:, 0:128], lhsT=kT[:, cs], rhs=qT[:, cs], start=True, stop=True),
                     reads=[kb_[c], qb[c]], writes=[pSb])
                PT, PTb = PTr.next()
                k.op("dve", lambda e, PT=PT, pS=pS: e.tensor_tensor(out=PT, in0=pS[:, 0:128], in1=MS[d], op=ALU.mult), reads=[pSb, cstb], writes=[PTb])
                vsc, vscb = vscr.next()
                k.op("pool", lambda e, vsc=vsc, c=c: e.tensor_scalar(out=vsc[:, 0:257], in0=vext[:, c, 0:257], scalar1=gd["ea"][:, c:c + 1], scalar2=None, op0=ALU.mult),
                     reads=[vb[c]] + gb, writes=[vscb])
                pI, pIb = nps()
                k.op("pe", lambda e, pI=pI, PT=PT, vsc=vsc: e.matmul(pI[:, 0:257], lhsT=PT, rhs=vsc[:, 0:257], start=True, stop=True),
                     reads=[PTb, vscb], writes=[pIb])
                pX, pXb = nps()
                k.op("pe", lambda e, pX=pX, cs=cs, cbf=cbf: e.matmul(pX[:, 0:257], lhsT=qT[:, cs], rhs=cbf[:, 0:257], start=True, stop=True),
                     reads=[qb[c], cbfb], writes=[pXb])
                t1, t1b = t1r.next()
                k.op("act", lambda e, t1=t1, pX=pX, c=c: e.activation(out=t1[:, 0:257], in_=pX[:, 0:257], func=AF.Identity, scale=gd["einter"][:, c:c + 1]),
                     reads=[pXb] + gb, writes=[t1b])
                t2, t2b = t2r.next()
                k.op("dve", lambda e, t2=t2, t1=t1, pI=pI: e.tensor_tensor(out=t2[:, 0:257], in0=t1[:, 0:257], in1=pI[:, 0:257], op=ALU.add),
                     reads=[t1b, pIb], writes=[t2b])
                sm, smb = smr.next()
                k.op("dve", lambda e, sm=sm, t2=t2: e.scalar_tensor_tensor(out=sm[:, 0:1], in0=t2[:, 256:257], scalar=-1.0, in1=t2[:, 256:257], op0=ALU.mult, op1=ALU.max),
                     reads=[t2b], writes=[smb])
                k.op("dve", lambda e, sm=sm, c=c: e.tensor_tensor(out=sm[:, 1:2], in0=sm[:, 0:1], in1=gd["floor"][:, c:c + 1], op=ALU.max), reads=[smb] + gb, writes=[smb])
                k.op("dve", lambda e, sm=sm: e.reciprocal(out=sm[:, 2:3], in_=sm[:, 1:2]), reads=[smb], writes=[smb])
                if first:
                    k.op("dve", lambda e, sm=sm, t2=t2, c=c: e.tensor_scalar(out=hbk[:, c, :], in0=t2[:, 0:256], scalar1=sm[:, 2:3], scalar2=None, op0=ALU.mult),
                         reads=[smb, t2b], writes=[hbb[c]])
                else:
                    hs, hsb = hsr.next()
                    k.op("dve", lambda e, hs=hs, sm=sm, t2=t2, c=c: e.scalar_tensor_tensor(out=hs, in0=t2[:, 0:256], scalar=sm[:, 2:3], in1=hbk[:, c, :], op0=ALU.mult, op1=ALU.add),
                         reads=[smb, t2b, hbb[c]], writes=[hsb])
                    jk, jkb = jkr.next()
                    k.op("act", lambda e, jk=jk, hs=hs, sm=sm: e.activation(out=jk, in_=hs, func=AF.Square, accum_out=sm[:, 3:4]), reads=[hsb], writes=[jkb, smb])
                    k.op("dve", lambda e, sm=sm: e.tensor_scalar(out=sm[:, 4:5], in0=sm[:, 3:4], scalar1=1.0 / 256, scalar2=EPS, op0=ALU.mult, op1=ALU.add), reads=[smb], writes=[smb])
                    k.op("act", lambda e, sm=sm: e.activation(out=sm[:, 4:5], in_=sm[:, 4:5], func=AF.Sqrt), reads=[smb], writes=[smb])
                    k.op("dve", lambda e, sm=sm: e.reciprocal(out=sm[:, 5:6], in_=sm[:, 4:5]), reads=[smb], writes=[smb])
                    yb_, ybb_ = ybr.next()
                    k.op("dve", lambda e, yb_=yb_, hs=hs, sm=sm: e.scalar_tensor_tensor(out=yb_, in0=hs, scalar=sm[:, 5:6], in1=pbt[:, PB_MN:PB_MN + 256], op0=ALU.mult, op1=ALU.mult),
                         reads=[hsb, smb, pbb], writes=[ybb_])
                    pT_, pTb_ = cm.pst.next()
                    pbf = pT_[:, 0:256]

                    def tr(e, pbf=pbf, yb_=yb_):
                        e.transpose(pbf[:, 0:128], yb_[:, 0:128], idh[:])
                        return e.transpose(pbf[:, 128:256], yb_[:, 128:256], idh[:])
                    k.op("pe", tr, reads=[ybb_, cstb], writes=[pTb_])
                    if c % 4 == (0 if d == 0 else 3):
                        yst = ystr.next()
                        st["yst"] = yst
                    ys, ysb = yst
                    cc = c % 4
                    k.op("act", lambda e, ys=ys, pbf=pbf, cc=cc: e.copy(out=ys[:, :, cc * 128:(cc + 1) * 128], in_=pbf.rearrange("p (a b) -> p a b", b=128)),
                         reads=[pTb_], writes=[ysb])
                    if c % 4 == (3 if d == 0 else 0):
                        c0 = c - c % 4
                        k.group("sp", d_o, [(yparts[0][:, c0 * 128:(c0 + 4) * 128].rearrange("(a p) t -> p a t", p=128), ys, [ysb], [])])
                ksc, kscb = kscr.next()
                k.op("pool", lambda e, ksc=ksc, c=c: e.tensor_scalar(out=ksc, in0=ktok[:, c, :], scalar1=gd["es"][:, c:c + 1], scalar2=None, op0=ALU.mult),
                     reads=[ktb[c]] + gb, writes=[kscb])
                pC, pCb = nps()
                k.op("pe", lambda e, pC=pC, ksc=ksc, c=c: e.matmul(pC[:, 0:257], lhsT=ksc, rhs=vext[:, c, 0:257], start=True, stop=True),
                     reads=[kscb, vb[c]], writes=[pCb])
                t3, t3b = t3r.next()
                k.op("act", lambda e, t3=t3, pC=pC, c=c: e.activation(out=t3[:, 0:257], in_=pC[:, 0:257], func=AF.Identity, scale=gd["aloc"][:, c:c + 1]),
                     reads=[pCb] + gb, writes=[t3b])
                k.op("dve", lambda e, t3=t3, c=c: e.scalar_tensor_tensor(out=C32[:, 0:257], in0=C32[:, 0:257], scalar=gd["aprev"][:, c:c + 1], in1=t3[:, 0:257], op0=ALU.mult, op1=ALU.add),
                     reads=[t3b, C32b] + gb, writes=[C32b])
                cbf, cbfb = Cbfr.next()
                k.op("act", lambda e, cbf=cbf: e.copy(out=cbf[:, 0:257], in_=C32[:, 0:257]), reads=[C32b], writes=[cbfb])
                st["cbf"] = (cbf, cbfb)
            return step
        sb1 = mk_scan1(1)
        sf1 = mk_scan1(0)
        for i_ in range(NCk):
            sb1(NCk - 1 - i_)
            sf1(i_)
        k.barrier()

    def emit_B2():
        ar.reset()
        TB = 512
        NTB = SEQ_ // TB
        CT = ar.alloc((SEQ_,), BF16)
        BT = ar.alloc((SEQ_,), BF16)
        Btok = ar.alloc((NCk, 128), BF16)
        xtok = ar.alloc((NCk, 256), BF16)
        dtr = ar.alloc((2, NCk, 4), F32)
        dA = ar.alloc((2, NCk, 4), F32)
        eacs = ar.alloc((2, NCk, 4), F32)
        dte = ar.alloc((2, NCk, 4), F32)
        cdk = ar.alloc((2, NCk, 4), F32)
        cwt = ar.alloc((4, 6), F32)
        ctb = [k.buf() for _ in range(NCk)]
        btb = [k.buf() for _ in range(NCk)]
        bkb = [k.buf() for _ in range(NCk)]
        xkb = [k.buf() for _ in range(NCk)]
        dtb_ = k.buf()
        gmb = k.buf()
        cwb = k.buf()
        mark = ar.off
        w2 = [ar.alloc((2048,), BF16) for _ in range(4)]
        w2t = ar.alloc((16, 8), BF16)
        wb_ = k.buf()
        urot = ARot(k, ar, 2, (16, TB), BF16)
        ring = [ar.alloc((4, TB + 4), F32) for _ in range(3)]
        ringb = [[k.buf() for _ in range(4)] for _ in range(3)]
        accr = ARot(k, ar, 2, (TB,), F32)
        cvr = ARot(k, ar, 3, (TB,), BF16)
        k.group("pool", d_w, [(w2[i], WB2[i], [], [wb_]) for i in range(4)] + [(w2t, WB2t, [], [wb_])])
        k.group("sp", d_c, [(cwt, CW, [], [cwb])])
        for i in range(3):
            k.op("pool", lambda e, i=i: e.memset(ring[i], 0.0), writes=ringb[i])

        def conv_tile(ti):
            r = ring[ti % 3]
            rb = ringb[ti % 3]
            for ch in range(4):
                acc, accb = accr.next()
                k.op("dve", lambda e, acc=acc, r=r, ch=ch: e.tensor_scalar(out=acc, in0=r[:, ch, 0:TB], scalar1=cwt[:, ch, 0:1], scalar2=None, op0=ALU.mult),
                     reads=[rb[ch], cwb], writes=[accb])
                for kk in range(1, 5):
                    eng = "dve"
                    k.op(eng, lambda e, acc=acc, r=r, ch=ch, kk=kk: e.scalar_tensor_tensor(out=acc, in0=r[:, ch, kk:kk + TB], scalar=cwt[:, ch, kk:kk + 1], in1=acc, op0=ALU.mult, op1=ALU.add),
                         reads=[rb[ch], cwb, accb], writes=[accb])
                cv, cvb = cvr.next()
                cl = list(range(ti * 4, ti * 4 + 4))
                if ch == 3:
                    k.op("act", lambda e, acc=acc, ti=ti: e.activation(out=CT[:, ti * TB:(ti + 1) * TB], in_=acc, func=AF.Silu, bias=cwt[:, 3, 5:6]),
                         reads=[accb, cwb], writes=[ctb[c] for c in cl])
                    continue
                if ch == 2:
                    k.op("act", lambda e, acc=acc, ti=ti: e.activation(out=BT[:, ti * TB:(ti + 1) * TB], in_=acc, func=AF.Silu, bias=cwt[:, 2, 5:6]),
                         reads=[accb, cwb], writes=[btb[c] for c in cl])
                    src = BT[:, ti * TB:(ti + 1) * TB]
                    srcb = [btb[c] for c in cl]
                else:
                    k.op("act", lambda e, acc=acc, cv=cv, ch=ch: e.activation(out=cv, in_=acc, func=AF.Silu, bias=cwt[:, ch, 5:6]),
                         reads=[accb, cwb], writes=[cvb])
                    src = cv
                    srcb = [cvb]
                pT_, pTb_ = cm.pst.next()
                pbf = pT_[:, 0:512]

                def tr(e, pbf=pbf, src=src):
                    r_ = None
                    for cc in range(4):
                        r_ = e.transpose(pbf[:, cc * 128:(cc + 1) * 128], src[:, cc * 128:(cc + 1) * 128], idh[:])
                    return r_
                k.op("pe", tr, reads=srcb + [cstb], writes=[pTb_])
                if ch == 2:
                    k.op("act", lambda e, pbf=pbf, ti=ti: e.copy(out=Btok[:, ti * 4:ti * 4 + 4, :], in_=pbf.rearrange("p (a b) -> p a b", b=128)),
                         reads=[pTb_], writes=[bkb[c] for c in cl])
                else:
                    k.op("act", lambda e, pbf=pbf, ti=ti, ch=ch: e.copy(out=xtok[:, ti * 4:ti * 4 + 4, ch * 128:(ch + 1) * 128], in_=pbf.rearrange("p (a b) -> p a b", b=128)),
                         reads=[pTb_], writes=[xkb[c] for c in cl])

        for tt in range(NTB):
            u, ub = urot.next()
            k.group("sp", d_u[tt % 2], [(u, exT_d[EX_U:EX_U + 2048, tt * TB:(tt + 1) * TB].rearrange("(kc p) t -> p kc t", p=128), [], [ub])])
            r = ring[tt % 3]
            rb = ringb[tt % 3]
            if tt + 1 < NTB or True:
                pass
            for ch in range(4):
                pt, pb = nps()

                def mm(e, pt=pt, u=u, ch=ch):
                    r_ = None
                    for kc in range(16):
                        r_ = e.matmul(pt[:, 0:TB], lhsT=w2[ch][:, kc * 128:(kc + 1) * 128], rhs=u[:, kc, :], start=(kc == 0), stop=(kc == 15))
                    return r_
                k.op("pe", mm, reads=[wb_, ub], writes=[pb])
                k.op("act", lambda e, pt=pt, r=r, ch=ch: e.copy(out=r[:, ch, 2:TB + 2], in_=pt[:, 0:TB]), reads=[pb], writes=[rb[ch]])
                if tt > 0:
                    rp = ring[(tt - 1) % 3]
                    k.op("dve", lambda e, pt=pt, rp=rp, ch=ch: e.tensor_copy(out=rp[:, ch, TB + 2:TB + 4], in_=pt[:, 0:2]), reads=[pb], writes=[ringb[(tt - 1) % 3][ch]])
                if tt + 1 < NTB:
                    rn = ring[(tt + 1) % 3]
                    k.op("dve", lambda e, pt=pt, rn=rn, ch=ch: e.tensor_copy(out=rn[:, ch, 0:2], in_=pt[:, TB - 2:TB]), reads=[pb], writes=[ringb[(tt + 1) % 3][ch]])
                else:
                    k.op("dve", lambda e, r=r, ch=ch: e.memset(r[:, ch, TB + 2:TB + 4], 0.0), writes=[rb[ch]])
            if tt == 0:
                for ch in range(4):
                    k.op("dve", lambda e, r=r, ch=ch: e.memset(r[:, ch, 0:2], 0.0), writes=[rb[ch]])
            for cc in range(4):
                c = tt * 4 + cc
                pt, pb = nps()

                def mm2(e, pt=pt, u=u, cc=cc):
                    r_ = None
                    for kc in range(16):
                        r_ = e.matmul(pt[:, 0:8], lhsT=u[:, kc, cc * 128:(cc + 1) * 128], rhs=w2t[:, kc, :], start=(kc == 0), stop=(kc == 15))
                    return r_
                k.op("pe", mm2, reads=[wb_, ub], writes=[pb])
                k.op("dve", lambda e, pt=pt, c=c: e.tensor_copy(out=dtr[:, :, c, :], in_=pt[:, 0:8].rearrange("p (a b) -> p a b", b=4)), reads=[pb], writes=[dtb_])
            if tt > 0:
                conv_tile(tt - 1)
        conv_tile(NTB - 1)

        t_a = ar.alloc((2, NCk, 4), F32)
        t_b = ar.alloc((2, NCk, 4), F32)
        acs = ar.alloc((2, NCk, 4), F32)
        tot = ar.alloc((2, NCk, 4), F32)
        Abc = ar.alloc((8,), F32)
        R_ = [gmb]
        n2 = NCk * 4
        dtbias = pbt[:, PB_DTB:PB_DTB + 8].rearrange("p (a b) -> p a b", b=4).unsqueeze(2).broadcast_to([128, 2, NCk, 4])
        k.op("dve", lambda e: e.tensor_tensor(out=dtr, in0=dtr, in1=dtbias, op=ALU.add), reads=[dtb_, pbb], writes=[dtb_])
        k.op("dve", lambda e: e.scalar_tensor_tensor(out=t_a, in0=dtr, scalar=-1.0, in1=dtr, op0=ALU.mult, op1=ALU.max), reads=[dtb_], writes=R_)
        k.op("act", lambda e: e.activation(out=t_a, in_=t_a, func=AF.Exp, scale=-1.0), reads=R_, writes=R_)
        k.op("act", lambda e: e.activation(out=t_a, in_=t_a, func=AF.Ln, bias=1.0), reads=R_, writes=R_)
        k.op("dve", lambda e: e.scalar_tensor_tensor(out=dtr, in0=dtr, scalar=0.0, in1=t_a, op0=ALU.max, op1=ALU.add), reads=R_ + [dtb_], writes=[dtb_])
        k.op("act", lambda e: e.activation(out=Abc, in_=pbt[:, PB_ALOG:PB_ALOG + 8], func=AF.Exp), reads=[pbb], writes=R_)
        k.op("dve", lambda e: e.tensor_scalar(out=Abc, in0=Abc, scalar1=-1.0, scalar2=None, op0=ALU.mult), reads=R_, writes=R_)
        Abb = Abc.rearrange("p (a b) -> p a b", b=4).unsqueeze(2).broadcast_to([128, 2, NCk, 4])
        k.op("dve", lambda e: e.tensor_tensor(out=dA, in0=dtr, in1=Abb, op=ALU.mult), reads=R_ + [dtb_], writes=R_)
        for d in range(2):
            p1, p1b = nps()
            p2, p2b = nps()
            k.op("pe", lambda e, p1=p1, d=d: e.matmul(p1[:, 0:n2], lhsT=TRI[d], rhs=dA[:, d].rearrange("p a b -> p (a b)"), start=True, stop=True), reads=[cstb] + R_, writes=[p1b])
            k.op("pe", lambda e, p2=p2, d=d: e.matmul(p2[:, 0:n2], lhsT=ONE32, rhs=dA[:, d].rearrange("p a b -> p (a b)"), start=True, stop=True), reads=[cstb] + R_, writes=[p2b])
            k.op("act", lambda e, p1=p1, d=d: e.copy(out=acs[:, d].rearrange("p a b -> p (a b)"), in_=p1[:, 0:n2]), reads=[p1b], writes=R_)
            k.op("dve", lambda e, p2=p2, d=d: e.tensor_copy(out=tot[:, d].rearrange("p a b -> p (a b)"), in_=p2[:, 0:n2]), reads=[p2b], writes=R_)
        k.op("act", lambda e: e.activation(out=eacs, in_=acs, func=AF.Exp), reads=R_, writes=R_)
        k.op("act", lambda e: e.activation(out=cdk, in_=tot, func=AF.Exp), reads=R_, writes=R_)
        ew("dve", t_b, tot, acs, ALU.subtract, R_, R_)
        k.op("act", lambda e: e.activation(out=t_b, in_=t_b, func=AF.Exp), reads=R_, writes=R_)
        ew("dve", dte, t_b, dtr, ALU.mult, R_ + [dtb_], R_)
        k.barrier()

        ar.off = mark
        yacc = ar.alloc((NCk, 256), F32)
        yab = [k.buf() for _ in range(NCk)]
        Gr = ARot(k, ar, 4, (128,), F32)
        Lr = ARot(k, ar, 4, (128,), F32)
        decr = ARot(k, ar, 4, (128,), F32)
        Wr = ARot(k, ar, 6, (128,), BF16)
        xdr = ARot(k, ar, 4, (256,), BF16)
        xddr = ARot(k, ar, 4, (256,), BF16)
        tmpr = ARot(k, ar, 4, (256,), F32)
        ytr = ARot(k, ar, 2, (256,), F32)
        ybr = ARot(k, ar, 2, (256,), BF16)
        visited2 = set()

        def mk_scan2(d):
            S32 = ar.alloc((256,), F32)
            S32b = k.buf()
            Sbfr = ARot(k, ar, 2, (256,), BF16)
            ystr = ARot(k, ar, 2, (2, 512), BF16)
            st = {}
            k.op("pool", lambda e: e.memset(S32, 0.0), writes=[S32b])
            sbf0, sbfb0 = Sbfr.next()
            k.op("pool", lambda e: e.memset(sbf0, 0.0), writes=[sbfb0])
            st["sbf"] = (sbf0, sbfb0)
            st["yst"] = None

            def step(c):
                sbf, sbfb = st["sbf"]
                yst = st["yst"]
                first = c not in visited2
                visited2.add(c)
                cs = slice(c * 128, (c + 1) * 128)
                pCB, pCBb = nps()
                k.op("pe", lambda e, pCB=pCB, cs=cs: e.matmul(pCB[:, 0:128], lhsT=BT[:, cs], rhs=CT[:, cs], start=True, stop=True), reads=[btb[c], ctb[c]], writes=[pCBb])
                Gm, Gb = Gr.next()
                k.op("dve", lambda e, Gm=Gm, pCB=pCB: e.tensor_tensor(out=Gm, in0=pCB[:, 0:128], in1=TRI[d], op=ALU.mult), reads=[pCBb, cstb], writes=[Gb])
                xd, xdb = xdr.next()
                k.op("pool", lambda e, xd=xd, c=c: e.tensor_tensor(out=xd.rearrange("p (a b) -> p a b", b=64), in0=xtok[:, c, :].rearrange("p (a b) -> p a b", b=64),
                                                                  in1=dtr[:, d, c, :].unsqueeze(2).broadcast_to([128, 4, 64]), op=ALU.mult),
                     reads=[xkb[c], dtb_], writes=[xdb])
                pY, pYb = nps()
                for hl in range(4):
                    Lm, Lb = Lr.next()
                    k.op("pool", lambda e, Lm=Lm, c=c, hl=hl: e.tensor_scalar(out=Lm, in0=STR[d], scalar1=dA[:, d, c, hl:hl + 1], scalar2=None, op0=ALU.mult),
                         reads=[cstb, gmb], writes=[Lb])
                    pSg, pSgb = nps()
                    k.op("pe", lambda e, pSg=pSg, Lm=Lm: e.matmul(pSg[:, 0:128], lhsT=Lm, rhs=TRI[d], start=True, stop=True), reads=[Lb, cstb], writes=[pSgb])
                    dec, decb = decr.next()
                    k.op("act", lambda e, dec=dec, pSg=pSg: e.activation(out=dec, in_=pSg[:, 0:128], func=AF.Exp), reads=[pSgb], writes=[decb])
                    Wm, Wb = Wr.next()
                    k.op("dve", lambda e, Wm=Wm, Gm=Gm, dec=dec: e.tensor_tensor(out=Wm, in0=Gm, in1=dec, op=ALU.mult), reads=[Gb, decb], writes=[Wb])
                    k.op("pe", lambda e, pY=pY, Wm=Wm, xd=xd, hl=hl: e.matmul(pY[:, hl * 64:(hl + 1) * 64], lhsT=Wm, rhs=xd[:, hl * 64:(hl + 1) * 64], start=True, stop=True),
                         reads=[Wb, xdb], writes=[pYb])
                pO, pOb = nps()
                k.op("pe", lambda e, pO=pO, cs=cs, sbf=sbf: e.matmul(pO[:, 0:256], lhsT=CT[:, cs], rhs=sbf, start=True, stop=True), reads=[ctb[c], sbfb], writes=[pOb])
                tmp, tmpb = tmpr.next()
                k.op("dve", lambda e, tmp=tmp, pO=pO, c=c: e.tensor_tensor(out=tmp.rearrange("p (a b) -> p a b", b=64), in0=pO[:, 0:256].rearrange("p (a b) -> p a b", b=64),
                                                                        in1=eacs[:, d, c, :].unsqueeze(2).broadcast_to([128, 4, 64]), op=ALU.mult),
                     reads=[pOb, gmb], writes=[tmpb])
                if first:
                    k.op("dve", lambda e, tmp=tmp, pY=pY, c=c: e.tensor_tensor(out=yacc[:, c, :], in0=tmp, in1=pY[:, 0:256], op=ALU.add), reads=[tmpb, pYb], writes=[yab[c]])
                else:
                    yt, ytb = ytr.next()
                    k.op("dve", lambda e, yt=yt, tmp=tmp, pY=pY: e.tensor_tensor(out=yt, in0=tmp, in1=pY[:, 0:256], op=ALU.add), reads=[tmpb, pYb], writes=[ytb])
                    k.op("pool", lambda e, yt=yt, c=c: e.tensor_tensor(out=yt, in0=yt, in1=yacc[:, c, :], op=ALU.add), reads=[ytb, yab[c]], writes=[ytb])
                    k.op("pool", lambda e, tmp=tmp, c=c: e.tensor_tensor(out=tmp, in0=xtok[:, c, :], in1=pbt[:, PB_DSK:PB_DSK + 256], op=ALU.mult), reads=[xkb[c], pbb, tmpb], writes=[tmpb])
                    yb_, ybb_ = ybr.next()
                    k.op("dve", lambda e, yb_=yb_, yt=yt, tmp=tmp: e.tensor_tensor(out=yb_, in0=yt, in1=tmp, op=ALU.add), reads=[ytb, tmpb], writes=[ybb_])
                    pT_, pTb_ = cm.pst.next()
                    pbf = pT_[:, 0:256]

                    def tr2(e, pbf=pbf, yb_=yb_):
                        e.transpose(pbf[:, 0:128], yb_[:, 0:128], idh[:])
                        return e.transpose(pbf[:, 128:256], yb_[:, 128:256], idh[:])
                    k.op("pe", tr2, reads=[ybb_, cstb], writes=[pTb_])
                    if c % 4 == (0 if d == 0 else 3):
                        yst = ystr.next()
                        st["yst"] = yst
                    ys, ysb = yst
                    cc = c % 4
                    k.op("act", lambda e, ys=ys, pbf=pbf, cc=cc: e.copy(out=ys[:, :, cc * 128:(cc + 1) * 128], in_=pbf.rearrange("p (a b) -> p a b", b=128)),
                         reads=[pTb_], writes=[ysb])
                    if c % 4 == (3 if d == 0 else 0):
                        c0 = c - c % 4
                        k.group("sp", d_o, [(yparts[1][:, c0 * 128:(c0 + 4) * 128].rearrange("(a p) t -> p a t", p=128), ys, [ysb], [])])
                xdd, xddb = xddr.next()
                k.op("pool", lambda e, xdd=xdd, c=c: e.tensor_tensor(out=xdd.rearrange("p (a b) -> p a b", b=64), in0=xtok[:, c, :].rearrange("p (a b) -> p a b", b=64),
                                                                   in1=dte[:, d, c, :].unsqueeze(2).broadcast_to([128, 4, 64]), op=ALU.mult),
                     reads=[xkb[c], gmb], writes=[xddb])
                pSt, pStb = nps()
                k.op("pe", lambda e, pSt=pSt, c=c, xdd=xdd: e.matmul(pSt[:, 0:256], lhsT=Btok[:, c, :], rhs=xdd, start=True, stop=True), reads=[bkb[c], xddb], writes=[pStb])
                k.op("dve", lambda e, c=c: e.tensor_tensor(out=S32.rearrange("p (a b) -> p a b", b=64), in0=S32.rearrange("p (a b) -> p a b", b=64),
                                                           in1=cdk[:, d, c, :].unsqueeze(2).broadcast_to([128, 4, 64]), op=ALU.mult), reads=[S32b, gmb], writes=[S32b])
                k.op("dve", lambda e, pSt=pSt: e.tensor_tensor(out=S32, in0=S32, in1=pSt[:, 0:256], op=ALU.add), reads=[S32b, pStb], writes=[S32b])
                sbf, sbfb = Sbfr.next()
                k.op("act", lambda e, sbf=sbf: e.copy(out=sbf, in_=S32), reads=[S32b], writes=[sbfb])
                st["sbf"] = (sbf, sbfb)
            return step
        sb2 = mk_scan2(1)
        sf2 = mk_scan2(0)
        for i_ in range(NCk):
            sb2(NCk - 1 - i_)
            sf2(i_)
        k.barrier()

    def emit_B3():
        TB = 512
        NTB = SEQ_ // TB
        qscale = 192.0 ** -0.5

        def head(hh):
            ar.reset()
            wq = [ar.alloc((512,), BF16) for _ in range(5)]
            wb_ = k.buf()
            qN = ar.alloc((SEQ_,), BF16)
            qR = ar.alloc((SEQ_,), BF16)
            kN = ar.alloc((SEQ_,), BF16)
            kR = ar.alloc((SEQ_,), BF16)
            V = ar.alloc((NCk, 128), BF16)
            qNb = [k.buf() for _ in range(NTB)]
            qRb = [k.buf() for _ in range(NTB)]
            kNb = [k.buf() for _ in range(NCk)]
            kRb = k.buf()
            Vb = [k.buf() for _ in range(NCk)]
            latr = ARot(k, ar, 2, (4, TB), BF16)
            csr = ARot(k, ar, 2, (2, TB), BF16)
            sqr = ARot(k, ar, 3, (TB,), BF16)
            f32r = ARot(k, ar, 4, (TB,), F32)
            PTr = ARot(k, ar, 3, (TB,), BF16)
            yor = ARot(k, ar, 2, (TB,), BF16)
            kmax = ar.alloc((4,), F32)
            kmb = k.buf()
            k.group("pool", d_w, [(wq[i], WB3[hh, i], [], [wb_]) for i in range(5)])
            k.op("pool", lambda e: e.memset(kR, 1.0), writes=[kRb])
            k.group("sp", d_c, [(kR[0:64, :], exT_d[EX_KR:EX_KR + 64, :], [], [kRb])])
            k.op("dve", lambda e: e.memset(kmax, 0.0), writes=[kmb])
            for tt in range(NTB):
                ts = slice(tt * TB, (tt + 1) * TB)
                lt, ltb = latr.next()
                k.group("sp", d_u[tt % 2], [(lt, exT_d[EX_KV:EX_KV + 512, ts].rearrange("(kc p) t -> p kc t", p=128), [], [ltb])])
                cl = list(range(tt * 4, tt * 4 + 4))
                pt, pb = nps()

                def mm(e, pt=pt, lt=lt):
                    r_ = None
                    for kc in range(4):
                        r_ = e.matmul(pt[:, 0:TB], lhsT=wq[3][:, kc * 128:(kc + 1) * 128], rhs=lt[:, kc, :], start=(kc == 0), stop=(kc == 3))
                    return r_
                k.op("pe", mm, reads=[wb_, ltb], writes=[pb])
                k.op("act", lambda e, pt=pt, ts=ts: e.copy(out=kN[:, ts], in_=pt[:, 0:TB]), reads=[pb], writes=[kNb[c] for c in cl])
                for cc in range(4):
                    c = tt * 4 + cc
                    pv, pvb = nps()

                    def mmv(e, pv=pv, lt=lt, cc=cc):
                        r_ = None
                        for kc in range(4):
                            r_ = e.matmul(pv[:, 0:128], lhsT=lt[:, kc, cc * 128:(cc + 1) * 128], rhs=wq[4][:, kc * 128:(kc + 1) * 128], start=(kc == 0), stop=(kc == 3))
                        return r_
                    k.op("pe", mmv, reads=[wb_, ltb], writes=[pvb])
                    k.op("dve", lambda e, pv=pv, c=c: e.tensor_copy(out=V[:, c, :], in_=pv[:, 0:128]), reads=[pvb], writes=[Vb[c]])
                s1, s1b = sqr.next()
                s2, s2b = sqr.next()
                k.op("act", lambda e, s1=s1, ts=ts: e.activation(out=s1, in_=kN[:, ts], func=AF.Square), reads=[kNb[c] for c in cl], writes=[s1b])
                k.op("act", lambda e, s2=s2, ts=ts: e.activation(out=s2[0:64, :], in_=kR[0:64, ts], func=AF.Square), reads=[kRb], writes=[s2b])
                pn, pnb = nps()

                def mmn(e, pn=pn, s1=s1, s2=s2):
                    e.matmul(pn[:, 0:TB], lhsT=oneh[:], rhs=s1, start=True, stop=False)
                    return e.matmul(pn[:, 0:TB], lhsT=oneh[0:64, :], rhs=s2[0:64, :], start=False, stop=True)
                k.op("pe", mmn, reads=[s1b, s2b, cstb], writes=[pnb])
                k.op("dve", lambda e, pn=pn: e.reduce_max(out=kmax[:, 1:2], in_=pn[:, 0:TB], axis=AX.X), reads=[pnb, kmb], writes=[kmb])
                k.op("dve", lambda e: e.tensor_tensor(out=kmax[:, 0:1], in0=kmax[:, 0:1], in1=kmax[:, 1:2], op=ALU.max), reads=[kmb], writes=[kmb])
            k.op("act", lambda e: e.activation(out=kmax[:, 2:3], in_=kmax[:, 0:1], func=AF.Sqrt), reads=[kmb], writes=[kmb])
            for tt in range(NTB):
                ts = slice(tt * TB, (tt + 1) * TB)
                lt, ltb = latr.next()
                cs_, csb = csr.next()
                k.group("sp", d_u[tt % 2], [(lt, exT_d[EX_Q:EX_Q + 512, ts].rearrange("(kc p) t -> p kc t", p=128), [], [ltb]),
                                             (cs_[0:64, 0, :], exT_d[EX_COS:EX_COS + 64, ts], [], [csb]),
                                             (cs_[0:64, 1, :], exT_d[EX_SIN:EX_SIN + 64, ts], [], [csb])])
                pt, pb = nps()

                def mmq(e, pt=pt, lt=lt):
                    r_ = None
                    for kc in range(4):
                        r_ = e.matmul(pt[:, 0:TB], lhsT=wq[0][:, kc * 128:(kc + 1) * 128], rhs=lt[:, kc, :], start=(kc == 0), stop=(kc == 3))
                    return r_
                k.op("pe", mmq, reads=[wb_, ltb], writes=[pb])
                k.op("act", lambda e, pt=pt, ts=ts: e.activation(out=qN[:, ts], in_=pt[:, 0:TB], func=AF.Copy, scale=qscale), reads=[pb], writes=[qNb[tt]])
                pr, prb = nps()
                pw, pwb = nps()
                for (wi, pp, ppb) in ((1, pr, prb), (2, pw, pwb)):
                    def mmr(e, pp=pp, lt=lt, wi=wi):
                        r_ = None
                        for kc in range(4):
                            r_ = e.matmul(pp[0:64, 0:TB], lhsT=wq[wi][:, kc * 64:(kc + 1) * 64], rhs=lt[:, kc, :], start=(kc == 0), stop=(kc == 3))
                        return r_
                    k.op("pe", mmr, reads=[wb_, ltb], writes=[ppb])
                a1, a1b = f32r.next()
                a2, a2b = f32r.next()
                k.op("dve", lambda e, a1=a1, pr=pr, cs_=cs_: e.tensor_tensor(out=a1[0:64, :], in0=pr[0:64, 0:TB], in1=cs_[0:64, 0, :], op=ALU.mult), reads=[prb, csb], writes=[a1b])
                k.op("dve", lambda e, a2=a2, pw=pw, cs_=cs_: e.tensor_tensor(out=a2[0:64, :], in0=pw[0:64, 0:TB], in1=cs_[0:64, 1, :], op=ALU.mult), reads=[pwb, csb], writes=[a2b])
                k.op("pool", lambda e, a1=a1, a2=a2: e.tensor_tensor(out=a1[0:64, :], in0=a1[0:64, :], in1=a2[0:64, :], op=ALU.add), reads=[a1b, a2b], writes=[a1b])
                k.op("act", lambda e, a1=a1, ts=ts: e.activation(out=qR[0:64, ts], in_=a1[0:64, :], func=AF.Copy, scale=qscale), reads=[a1b], writes=[qRb[tt]])
                s1, s1b = sqr.next()
                s2, s2b = sqr.next()
                k.op("act", lambda e, s1=s1, ts=ts: e.activation(out=s1, in_=qN[:, ts], func=AF.Square), reads=[qNb[tt]], writes=[s1b])
                k.op("act", lambda e, s2=s2, ts=ts: e.activation(out=s2[0:64, :], in_=qR[0:64, ts], func=AF.Square), reads=[qRb[tt]], writes=[s2b])
                pn, pnb = nps()

                def mmn2(e, pn=pn, s1=s1, s2=s2):
                    e.matmul(pn[:, 0:TB], lhsT=oneh[:], rhs=s1, start=True, stop=False)
                    return e.matmul(pn[:, 0:TB], lhsT=oneh[0:64, :], rhs=s2[0:64, :], start=False, stop=True)
                k.op("pe", mmn2, reads=[s1b, s2b, cstb], writes=[pnb])
                a3, a3b = f32r.next()
                k.op("act", lambda e, a3=a3, pn=pn: e.activation(out=a3[64:65, :], in_=pn[64:65, 0:TB], func=AF.Sqrt), reads=[pnb], writes=[a3b])
                k.op("dve", lambda e, a3=a3, ts=ts: e.tensor_scalar(out=qR[64:65, ts], in0=a3[64:65, :], scalar1=kmax[64:65, 2:3], scalar2=-1.0, op0=ALU.mult, op1=ALU.mult),
                     reads=[a3b, kmb, qRb[tt]], writes=[qRb[tt]])
            accs = [[cm.ps[3], cm.ps[4]], [cm.ps[5], cm.ps[6]]]
            srot = Rot([cm.ps[i] for i in range(3)])
            for qg in range(NTB):
                ts = slice(qg * TB, (qg + 1) * TB)
                (pO, pOb), (pL, pLb) = accs[qg % 2]
                def issue_s(kb, ts=ts, qg=qg):
                    ks = slice(kb * 128, (kb + 1) * 128)
                    pS, pSb = srot.next()

                    def mms(e, pS=pS, ks=ks, ts=ts):
                        e.matmul(pS[:, 0:TB], lhsT=kN[:, ks], rhs=qN[:, ts], start=True, stop=False)
                        return e.matmul(pS[:, 0:TB], lhsT=kR[0:65, ks], rhs=qR[0:65, ts], start=False, stop=True)
                    k.op("pe", mms, reads=[kNb[kb], kRb, qNb[qg], qRb[qg]], writes=[pSb])
                    return pS, pSb
                nxt = issue_s(0)
                for kb in range(NCk):
                    pS, pSb = nxt
                    if kb + 1 < NCk:
                        nxt = issue_s(kb + 1)
                    PT, PTb = PTr.next()
                    k.op("act", lambda e, PT=PT, pS=pS: e.activation(out=PT, in_=pS[:, 0:TB], func=AF.Exp), reads=[pSb], writes=[PTb])

                    def mmo(e, pO=pO, pL=pL, PT=PT, kb=kb):
                        e.matmul(pO[:, 0:TB], lhsT=V[:, kb, :], rhs=PT, start=(kb == 0), stop=(kb == NCk - 1))
                        return e.matmul(pL[:, 0:TB], lhsT=oneh[:], rhs=PT, start=(kb == 0), stop=(kb == NCk - 1))
                    k.op("pe", mmo, reads=[Vb[kb], PTb, cstb], writes=[pOb, pLb])
                rl, rlb = f32r.next()
                k.op("dve", lambda e, rl=rl, pL=pL: e.reciprocal(out=rl, in_=pL[:, 0:TB]), reads=[pLb], writes=[rlb])
                yo, yob = yor.next()
                k.op("dve", lambda e, yo=yo, pO=pO, rl=rl: e.tensor_tensor(out=yo, in0=pO[:, 0:TB], in1=rl, op=ALU.mult), reads=[pOb, rlb], writes=[yob])
                k.group("sp", d_o, [(yparts[2][hh * 128:(hh + 1) * 128, ts], yo, [yob], [])])
            k.barrier()
        head(0)
        head(1)

    if 1 in B_PARTS:
        emit_B1()
    if 2 in B_PARTS:
        emit_B2()
    if 3 in B_PARTS:
        emit_B3()
    if fused:
        k.barrier()
    else:
        k.wait_all("sp")
    k.emit()
    return nc


def phaseB_weights(inp, L, g):
    w_in = inp["w_in"][L]
    out = {}
    wq = w_in[:, OFF_Q + g * 128:OFF_Q + (g + 1) * 128]
    wk = w_in[:, OFF_K + g * 128:OFF_K + (g + 1) * 128]
    wv = w_in[:, OFF_V + g * 256:OFF_V + (g + 1) * 256]
    gc = [OFF_IG + g, OFF_IG + 4 + g, OFF_FG + g, OFF_FG + 4 + g]
    out["WB1"] = np.ascontiguousarray(np.concatenate([chunks_lhsT(wq), chunks_lhsT(wk)], axis=0))
    out["WB1t"] = rhs_layout(np.concatenate([wk, wv, w_in[:, gc]], axis=1))
    pb = np.zeros((1, NPB), np.float32)
    pb[0, PB_BIG:PB_BIG + 2] = inp["mlstm_b_igate"][L][:, g]
    pb[0, PB_BFG:PB_BFG + 2] = inp["mlstm_b_fgate"][L][:, g]
    pb[0, PB_MN:PB_MN + 256] = inp["mlstm_norm"][L][g * 256:(g + 1) * 256]
    pb[0, PB_DTB:PB_DTB + 8] = inp["ssm_dt_bias"][L][:, 4 * g:4 * g + 4].reshape(8)
    pb[0, PB_ALOG:PB_ALOG + 8] = inp["ssm_a_log"][L][:, 4 * g:4 * g + 4].reshape(8)
    pb[0, PB_DSK:PB_DSK + 256] = np.repeat(inp["ssm_d"][L][4 * g:4 * g + 4], 64)
    out["PB"] = pb
    grp = g // 2
    chans = np.concatenate([np.arange(256 * g, 256 * g + 256), 1024 + grp * 128 + np.arange(128), 1280 + grp * 128 + np.arange(128)])
    cw = np.zeros((128, 4, 6), np.float32)
    cwl = inp["conv_w"][L][:, chans]
    cw[:, :, 0:5] = cwl.reshape(5, 4, 128).transpose(2, 1, 0)
    cw[:, :, 5] = inp["conv_b"][L][chans].reshape(4, 128).T
    out["CW"] = cw
    out["WB2"] = np.ascontiguousarray(chunks_lhsT(w_in[:, OFF_XBC + chans]))
    dtc = [OFF_DT + d * 16 + 4 * g + hl for d in range(2) for hl in range(4)]
    out["WB2t"] = rhs_layout(w_in[:, dtc])
    w3 = np.zeros((2, 5, 128, 512), np.float32)
    for hh in range(2):
        h = 2 * g + hh
        uq = inp["mla_w_uq"][L][:, h * 192:(h + 1) * 192]
        ukv = inp["mla_w_ukv"][L][:, h * 256:(h + 1) * 256]
        w3[hh, 0] = chunks_lhsT(uq[:, 0:128])[0]
        rot = uq[:, 128:192]
        w3[hh, 1, :, 0:256] = chunks_lhsT(rot, 64)[0]
        w3[hh, 2, :, 0:256] = chunks_lhsT(np.concatenate([rot[:, 32:], rot[:, :32]], axis=1), 64)[0]
        w3[hh, 3] = chunks_lhsT(ukv[:, 0:128])[0]
        w3[hh, 4] = rhs_layout(ukv[:, 128:256]).reshape(128, 512)
    out["WB3"] = w3
    out["CONST"] = tri_consts()
    return out


_PROG = {}


def _prog(key, fn):
    if key not in _PROG:
        _PROG[key] = fn()
    return _PROG[key]


def build_fused(SEQ_, DEPTH_):
    TOK = SEQ_ // 4
    nc = bass.Bass("TRN2", target_bir_lowering=False)

    def din(name, shape, dt):
        return nc.dram_tensor(name, list(shape), dt, kind="ExternalInput").ap()
    xT = din("xT", [D, SEQ_], F32)
    pT = din("pT", [DEPTH_, PLE_DIM, SEQ_], F32)
    pos = din("pos", [1, SEQ_], I32)
    WAs, GAs = [], []
    for i in range(DEPTH_ + 1):
        _, nch, _, ng = phaseA_layout(i > 0, i < DEPTH_)
        WAs.append(din("WA%d" % i, [nch, 128, 2048], F32))
        GAs.append(din("GA%d" % i, [128, ng], F32))
    nB = DEPTH_ * 4
    WB1 = din("WB1", [nB, 2, 128, 2048], F32)
    WB1t = din("WB1t", [nB, 128, 16, 388], F32)
    PB = din("PB", [nB, 1, NPB], F32)
    CW = din("CW", [nB, 128, 4, 6], F32)
    WB2 = din("WB2", [nB, 4, 128, 2048], F32)
    WB2t = din("WB2t", [nB, 128, 16, 8], F32)
    WB3 = din("WB3", [nB, 2, 5, 128, 512], F32)
    CONST = din("CONST", [8, 128, 128], F32)
    outT = nc.dram_tensor("outT", [D, SEQ_], F32, kind="ExternalOutput").ap()
    hT = nc.dram_tensor("hT_int", [D, SEQ_], F32, kind="Internal").ap()
    exT = nc.dram_tensor("exT_int", [EX_ROWS, SEQ_], BF16, kind="Internal").ap()
    yT = nc.dram_tensor("yT_int", [3072, SEQ_], BF16, kind="Internal").ap()
    shared = Shared(nc)
    for i in range(DEPTH_ + 1):
        has_tail = i > 0
        has_head = i < DEPTH_
        for q in range(4):
            cs = slice(q * TOK, (q + 1) * TOK)
            ext = dict(hT=(xT if i == 0 else hT)[:, cs], WA=WAs[i], GA=GAs[i], hTo=(hT if has_head else outT)[:, cs],
                       yT=yT[:, cs], pT=(pT[i - 1][:, cs] if has_tail else None), pos=pos[:, cs], exT=exT[:, cs])
            build_phaseA(has_tail, has_head, not has_head, TOK, nc=nc, ext=ext, shared=shared)
        if has_head:
            for g in range(4):
                j = i * 4 + g
                ext = dict(exT=exT, WB1=WB1[j], WB1t=WB1t[j], PB=PB[j], CW=CW[j], WB2=WB2[j], WB2t=WB2t[j], WB3=WB3[j], CONST=CONST,
                           yparts=[yT[br * 1024 + g * 256:br * 1024 + (g + 1) * 256] for br in range(3)])
                build_phaseB(SEQ_, nc=nc, ext=ext, shared=shared)
    shared.close()
    return nc


FUSED = False


def kernel(**inp):
    if FUSED:
        return kernel_fused(**inp)
    return kernel_unfused(**inp)


def kernel_fused(**inp):
    inp = {kk: np.asarray(v) for kk, v in inp.items()}
    nc = _prog(("F", SEQ, DEPTH), lambda: build_fused(SEQ, DEPTH))
    common = {}
    for i in range(DEPTH + 1):
        WA, GA = phaseA_weights(inp, i - 1 if i > 0 else None, i if i < DEPTH else None)
        common["WA%d" % i] = WA
        common["GA%d" % i] = GA
    wb = [phaseB_weights(inp, L, g) for L in range(DEPTH) for g in range(4)]
    for name in ("WB1", "WB1t", "PB", "CW", "WB2", "WB2t", "WB3"):
        common[name] = np.ascontiguousarray(np.stack([w[name] for w in wb], axis=0))
    common["CONST"] = tri_consts()
    in_maps = []
    for b in range(BATCH):
        m = dict(common)
        m["xT"] = np.ascontiguousarray(inp["x"][b].T)
        m["pT"] = np.ascontiguousarray(inp["p"][:, b].transpose(0, 2, 1))
        m["pos"] = np.ascontiguousarray(inp["positions"][b:b + 1]).astype(np.int32)
        in_maps.append(m)
    res = run_bass_kernel_spmd(nc, in_maps, core_ids=list(range(BATCH)))
    out = np.empty((BATCH, SEQ, D), np.float32)
    for b in range(BATCH):
        out[b] = np.asarray(res.results[b]["outT"]).T
    return out


def kernel_unfused(**inp):
    inp = {kk: np.asarray(v) for kk, v in inp.items()}
    TOK = SEQ // 4
    ncores = BATCH * 4
    cores = list(range(ncores))
    x = inp["x"]
    hT = [np.ascontiguousarray(x[c // 4, (c % 4) * TOK:(c % 4 + 1) * TOK].T) for c in cores]
    yT = None
    out = None
    for i in range(DEPTH + 1):
        has_tail = i > 0
        has_head = i < DEPTH
        nc = _prog(("A", has_tail, has_head, TOK), lambda: build_phaseA(has_tail, has_head, not has_head, TOK))
        WA, GA = phaseA_weights(inp, i - 1 if has_tail else None, i if has_head else None)
        in_maps = []
        for c in cores:
            b, q = c // 4, c % 4
            m = {"hT": hT[c], "WA": WA, "GA": GA}
            if has_tail:
                m["yT"] = yT[c]
                m["pT"] = np.ascontiguousarray(inp["p"][i - 1, b, q * TOK:(q + 1) * TOK].T)
            if has_head:
                m["pos"] = np.ascontiguousarray(inp["positions"][b:b + 1, q * TOK:(q + 1) * TOK]).astype(np.int32)
            in_maps.append(m)
        res = run_bass_kernel_spmd(nc, in_maps, core_ids=cores)
        hT = [np.asarray(res.results[c]["hTo"]) for c in cores]
        if not has_head:
            out = np.empty((BATCH, SEQ, D), np.float32)
            for c in cores:
                out[c // 4, (c % 4) * TOK:(c % 4 + 1) * TOK] = hT[c].T
            break
        ex = [np.asarray(res.results[c]["exT"]) for c in cores]
        exb = [np.ascontiguousarray(np.concatenate(ex[b * 4:(b + 1) * 4], axis=1)) for b in range(BATCH)]
        ncB = _prog(("B", SEQ), lambda: build_phaseB(SEQ))
        in_maps = []
        for c in cores:
            b, g = c // 4, c % 4
            m = phaseB_weights(inp, i, g)
            m["exT"] = exb[b]
            in_maps.append(m)
        res = run_bass_kernel_spmd(ncB, in_maps, core_ids=cores)
        yB = [np.asarray(res.results[c]["yT"]) for c in cores]
        yT = []
        for c in cores:
            b, q = c // 4, c % 4
            rows = []
            for br in range(3):
                for g in range(4):
                    rows.append(yB[b * 4 + g][br * 256:(br + 1) * 256, q * TOK:(q + 1) * TOK])
            yT.append(np.ascontiguousarray(np.concatenate(rows, axis=0)))
    return out
```
